# Optimizing a Trainium2 kernel written in Bass

```python
import math
import jax, jax.numpy as jnp
from jax import lax
import numpy as np

D_MODEL = 1024
BATCH = 16
SEQ = 2048
DEPTH = 2
DEC_BATCH = 32
DEC_SEQ = 64
PAST_LEN = 1024

CHUNK = 64
Q_BLOCK = 128
N_EVEN = (DEPTH + 1) // 2
N_ODD = DEPTH // 2
MLA_HEADS = D_MODEL // 128
QK_NOPE = 64
QK_ROPE = 32
V_HEAD = 64
Q_LORA = D_MODEL // 4
KV_LORA = D_MODEL // 4
ROPE_BASE = 10000.0
MLA_OUT = MLA_HEADS * V_HEAD
CONV_CH = D_MODEL // 2
CONV_K = 3
EVEN_IN = Q_LORA + KV_LORA + QK_ROPE + 3 * CONV_CH
EVEN_SPLITS = (Q_LORA, Q_LORA + KV_LORA, Q_LORA + KV_LORA + QK_ROPE,
               Q_LORA + KV_LORA + QK_ROPE + CONV_CH, Q_LORA + KV_LORA + QK_ROPE + 2 * CONV_CH)
SSM_WIDTH = D_MODEL
SSM_GROUP = 16
SSM_GROUPS = SSM_WIDTH // SSM_GROUP
SSM_STATE = 64
MEM_LEN = 256
X_HEADS = 4
X_HEAD_DIM = D_MODEL // X_HEADS
D_FF = 256 * ((8 * D_MODEL // 3 + 255) // 256)
EPS = 1e-6
NEG_INF = -1e30

kernel_name = 'hybrid_mla_conv_s5_streaming_step'


def rms_norm(x, g):
    x32 = x.astype(jnp.float32)
    y = x32 * lax.rsqrt(jnp.mean(x32 * x32, axis=-1, keepdims=True) + EPS)
    return (y * g.astype(jnp.float32)).astype(x.dtype)


def swiglu(x, w_gu, w_down):
    gate, up = jnp.split(x @ w_gu, 2, axis=-1)
    return (jax.nn.silu(gate) * up) @ w_down


def rope_cos_sin(pos):
    inv = ROPE_BASE ** (-jnp.arange(0, QK_ROPE, 2, dtype=jnp.float32) / QK_ROPE)
    ang = pos.astype(jnp.float32)[:, None] * inv[None, :]
    return jnp.cos(ang), jnp.sin(ang)


def apply_rope(x, cos, sin):
    x32 = x.astype(jnp.float32)
    x1, x2 = jnp.split(x32, 2, axis=-1)
    return jnp.concatenate([x1 * cos - x2 * sin, x1 * sin + x2 * cos], axis=-1).astype(x.dtype)


def chunk_causal_attention(q_nope, q_rope, q_pos, k_nope, k_rope, v, k_pos):
    B, Sq, H, _ = q_nope.shape
    scale = (QK_NOPE + QK_ROPE) ** -0.5
    k_chunk = k_pos // CHUNK

    def attend(args):
        qn, qr, qp = args
        s = jnp.einsum('bqhd,bkhd->bhqk', qn, k_nope) + jnp.einsum('bqhd,bkd->bhqk', qr, k_rope)
        s = s.astype(jnp.float32) * scale
        visible = k_chunk[None, :] <= (qp // CHUNK)[:, None]
        s = jnp.where(visible[None, None], s, NEG_INF)
        p = jax.nn.softmax(s, axis=-1).astype(v.dtype)
        return jnp.einsum('bhqk,bkhd->bqhd', p, v)

    if Sq <= Q_BLOCK:
        return attend((q_nope, q_rope, q_pos))
    nb = Sq // Q_BLOCK

    def to_blocks(t):
        return t.reshape((B, nb, Q_BLOCK) + t.shape[2:]).swapaxes(0, 1)

    out = lax.map(attend, (to_blocks(q_nope), to_blocks(q_rope), q_pos.reshape(nb, Q_BLOCK)))
    return out.swapaxes(0, 1).reshape(B, Sq, H, V_HEAD)


def even_mixer(h, pos, w_in, q_norm, kv_norm, w_uq, w_ukv, conv_w, w_out,
               past_latent, past_krope, past_conv):
    B, S, _ = h.shape
    c_q, c_kv, k_r, gate_b, gate_c, v_in = jnp.split(h @ w_in, EVEN_SPLITS, axis=-1)
    cos, sin = rope_cos_sin(pos)
    q = (rms_norm(c_q, q_norm) @ w_uq).reshape(B, S, MLA_HEADS, QK_NOPE + QK_ROPE)
    q_nope = q[..., :QK_NOPE]
    q_rope = apply_rope(q[..., QK_NOPE:], cos[None, :, None], sin[None, :, None])
    latent = rms_norm(c_kv, kv_norm)
    k_rope = apply_rope(k_r, cos[None], sin[None])
    if past_latent is None:
        lat_all, kr_all, k_pos = latent, k_rope, pos
    else:
        lat_all = jnp.concatenate([past_latent, latent], axis=1)
        kr_all = jnp.concatenate([past_krope, k_rope], axis=1)
        k_pos = jnp.arange(past_latent.shape[1] + S)
    Sk = lat_all.shape[1]
    kv = (lat_all @ w_ukv).reshape(B, Sk, MLA_HEADS, QK_NOPE + V_HEAD)
    k_nope, v = kv[..., :QK_NOPE], kv[..., QK_NOPE:]
    attn = chunk_causal_attention(q_nope, q_rope, pos, k_nope, kr_all, v, k_pos).reshape(B, S, MLA_OUT)
    u = gate_c * v_in
    if past_conv is None:
        past_conv = jnp.zeros((B, CONV_K - 1, CONV_CH), u.dtype)
    u_pad = jnp.concatenate([past_conv, u], axis=1)
    conv = sum(conv_w[k] * u_pad[:, k:k + S] for k in range(CONV_K))
    z = gate_b * conv
    out = jnp.concatenate([attn, z], axis=-1) @ w_out
    return out, latent, k_rope, u_pad[:, -(CONV_K - 1):]


def ssm_mixer(h, w_in, a_re, a_im, b_re, b_im, c_re, c_im, log_dt, d_skip, w_glu, h0_re, h0_im):
    B, S, _ = h.shape
    f32 = jnp.float32
    u = h @ w_in
    u32 = u.astype(f32)
    ug = u32.reshape(B, S, SSM_GROUPS, SSM_GROUP)
    lam = lax.complex(a_re.astype(f32), a_im.astype(f32))
    dt = jnp.exp(log_dt.astype(f32))[:, None]
    a_bar = jnp.exp(lam * dt)
    b_bar = ((a_bar - 1.0) / lam)[..., None] * lax.complex(b_re.astype(f32), b_im.astype(f32))
    c_mat = lax.complex(c_re.astype(f32), c_im.astype(f32))
    bu = lax.complex(jnp.einsum('bsgc,gpc->bsgp', ug, b_bar.real),
                     jnp.einsum('bsgc,gpc->bsgp', ug, b_bar.imag))
    if h0_re is None:
        h0 = jnp.zeros((B, SSM_GROUPS, SSM_STATE), jnp.complex64)
    else:
        h0 = lax.complex(h0_re.astype(f32), h0_im.astype(f32))
    L = CHUNK if S % CHUNK == 0 else S
    nb = S // L
    bu_blocks = bu.reshape(B, nb, L, SSM_GROUPS, SSM_STATE).swapaxes(0, 1)

    def combine(e1, e2):
        a1, b1 = e1
        a2, b2 = e2
        return a1 * a2, a2 * b1 + b2

    def block_step(h_prev, bu_blk):
        bu_blk = bu_blk.at[:, 0].add(a_bar * h_prev)
        a_seq = jnp.broadcast_to(a_bar, bu_blk.shape)
        _, hs = lax.associative_scan(combine, (a_seq, bu_blk), axis=1)
        y = jnp.einsum('blgp,gcp->blgc', hs, c_mat).real
        return hs[:, -1], y

    h_last, ys = lax.scan(block_step, h0, bu_blocks)
    y = ys.swapaxes(0, 1).reshape(B, S, SSM_WIDTH) + d_skip.astype(f32) * u32
    val, gate = jnp.split(jax.nn.gelu(y).astype(h.dtype) @ w_glu, 2, axis=-1)
    out = val * jax.nn.sigmoid(gate)
    return out, h_last.real, h_last.imag


def memory_kv(mem, g, w_k, w_v):
    B = mem.shape[0]
    m = rms_norm(mem, g)
    return ((m @ w_k).reshape(B, MEM_LEN, X_HEADS, X_HEAD_DIM),
            (m @ w_v).reshape(B, MEM_LEN, X_HEADS, X_HEAD_DIM))


def cross_attend(h, mem_k, mem_v, w_q, w_o):
    B, S, _ = h.shape
    q = (h @ w_q).reshape(B, S, X_HEADS, X_HEAD_DIM)
    s = jnp.einsum('bshd,bmhd->bhsm', q, mem_k).astype(jnp.float32) * (X_HEAD_DIM ** -0.5)
    p = jax.nn.softmax(s, axis=-1).astype(mem_v.dtype)
    o = jnp.einsum('bhsm,bmhd->bshd', p, mem_v).reshape(B, S, X_HEADS * X_HEAD_DIM)
    return o @ w_o


def setup_inputs(seed: int = 0) -> dict:
    key = jax.random.key(seed)
    ks = iter(jax.random.split(key, 64))

    def nrm(shape, scale):
        return scale * jax.random.normal(next(ks), shape, jnp.float32)

    def gain(shape):
        return 1.0 + nrm(shape, 0.01)

    D = D_MODEL
    n_idx = jnp.arange(SSM_STATE, dtype=jnp.float32)
    ssm_shape = (N_ODD, SSM_GROUPS, SSM_STATE)
    return {
        'x_prompt': nrm((BATCH, SEQ, D), 1.0),
        'x_sample': nrm((DEC_BATCH, DEC_SEQ, D), 1.0),
        'cache_mla_latent': nrm((N_EVEN, DEC_BATCH, PAST_LEN, KV_LORA), 1.0),
        'cache_mla_krope': nrm((N_EVEN, DEC_BATCH, PAST_LEN, QK_ROPE), 1.0),
        'state_conv': nrm((N_EVEN, DEC_BATCH, CONV_K - 1, CONV_CH), 1.0),
        'state_ssm_re': nrm((N_ODD, DEC_BATCH, SSM_GROUPS, SSM_STATE), 0.3),
        'state_ssm_im': nrm((N_ODD, DEC_BATCH, SSM_GROUPS, SSM_STATE), 0.3),
        'cache_mem_k': nrm((DEPTH, DEC_BATCH, MEM_LEN, X_HEADS, X_HEAD_DIM), 1.0),
        'cache_mem_v': nrm((DEPTH, DEC_BATCH, MEM_LEN, X_HEADS, X_HEAD_DIM), 1.0),
        'mem_prompt': nrm((BATCH, MEM_LEN, D), 1.0),
        'ln_ffn1': gain((DEPTH, D)),
        'w_ffn1_gu': nrm((DEPTH, D, 2 * D_FF), D ** -0.5),
        'w_ffn1_down': nrm((DEPTH, D_FF, D), D_FF ** -0.5),
        'ln_mix': gain((DEPTH, D)),
        'w_in_even': nrm((N_EVEN, D, EVEN_IN), D ** -0.5),
        'q_norm': gain((N_EVEN, Q_LORA)),
        'kv_norm': gain((N_EVEN, KV_LORA)),
        'w_uq': nrm((N_EVEN, Q_LORA, MLA_HEADS * (QK_NOPE + QK_ROPE)), Q_LORA ** -0.5),
        'w_ukv': nrm((N_EVEN, KV_LORA, MLA_HEADS * (QK_NOPE + V_HEAD)), KV_LORA ** -0.5),
        'conv_w': nrm((N_EVEN, CONV_K, CONV_CH), CONV_K ** -0.5),
        'w_out_even': nrm((N_EVEN, MLA_OUT + CONV_CH, D), (MLA_OUT + CONV_CH) ** -0.5),
        'w_in_odd': nrm((N_ODD, D, SSM_WIDTH), D ** -0.5),
        'ssm_a_re': -0.5 + nrm(ssm_shape, 0.01),
        'ssm_a_im': math.pi * n_idx + nrm(ssm_shape, 0.01),
        'ssm_b_re': nrm((N_ODD, SSM_GROUPS, SSM_STATE, SSM_GROUP), (2 * SSM_GROUP) ** -0.5),
        'ssm_b_im': nrm((N_ODD, SSM_GROUPS, SSM_STATE, SSM_GROUP), (2 * SSM_GROUP) ** -0.5),
        'ssm_c_re': nrm((N_ODD, SSM_GROUPS, SSM_GROUP, SSM_STATE), (2 * SSM_STATE) ** -0.5),
        'ssm_c_im': nrm((N_ODD, SSM_GROUPS, SSM_GROUP, SSM_STATE), (2 * SSM_STATE) ** -0.5),
        'ssm_log_dt': jax.random.uniform(next(ks), (N_ODD, SSM_GROUPS), jnp.float32,
                                         math.log(1e-3), math.log(1e-1)),
        'ssm_d': nrm((N_ODD, SSM_WIDTH), 1.0),
        'w_glu': nrm((N_ODD, SSM_WIDTH, 2 * D), SSM_WIDTH ** -0.5),
        'ln_cross': gain((DEPTH, D)),
        'ln_mem': gain((DEPTH, D)),
        'w_xq': nrm((DEPTH, D, X_HEADS * X_HEAD_DIM), D ** -0.5),
        'w_xk': nrm((DEPTH, D, X_HEADS * X_HEAD_DIM), D ** -0.5),
        'w_xv': nrm((DEPTH, D, X_HEADS * X_HEAD_DIM), D ** -0.5),
        'w_xo': nrm((DEPTH, X_HEADS * X_HEAD_DIM, D), (X_HEADS * X_HEAD_DIM) ** -0.5),
        'ln_ffn2': gain((DEPTH, D)),
        'w_ffn2_gu': nrm((DEPTH, D, 2 * D_FF), D ** -0.5),
        'w_ffn2_down': nrm((DEPTH, D_FF, D), D_FF ** -0.5),
        'ln_final': gain((D,)),
    }


def reference(x_prompt, x_sample, cache_mla_latent, cache_mla_krope, state_conv, state_ssm_re, state_ssm_im,
              cache_mem_k, cache_mem_v, mem_prompt,
              ln_ffn1, w_ffn1_gu, w_ffn1_down, ln_mix, w_in_even, q_norm, kv_norm, w_uq, w_ukv, conv_w,
              w_out_even, w_in_odd, ssm_a_re, ssm_a_im, ssm_b_re, ssm_b_im, ssm_c_re, ssm_c_im, ssm_log_dt,
              ssm_d, w_glu, ln_cross, ln_mem, w_xq, w_xk, w_xv, w_xo, ln_ffn2, w_ffn2_gu, w_ffn2_down,
              ln_final):

    def run(x, past, mem_kv):
        B, S, _ = x.shape
        past_len = 0 if past is None else past[0].shape[2]
        pos = past_len + jnp.arange(S)
        lat_new, kr_new, conv_new, sre_new, sim_new = [], [], [], [], []
        for l in range(DEPTH):
            i = l // 2
            x = x + 0.5 * swiglu(rms_norm(x, ln_ffn1[l]), w_ffn1_gu[l], w_ffn1_down[l])
            h = rms_norm(x, ln_mix[l])
            if l % 2 == 0:
                p_lat = p_kr = p_conv = None
                if past is not None:
                    p_lat, p_kr, p_conv = past[0][i], past[1][i], past[2][i]
                mix, lat, kr, cst = even_mixer(h, pos, w_in_even[i], q_norm[i], kv_norm[i], w_uq[i], w_ukv[i],
                                               conv_w[i], w_out_even[i], p_lat, p_kr, p_conv)
                lat_new.append(lat)
                kr_new.append(kr)
                conv_new.append(cst)
            else:
                h0r = h0i = None
                if past is not None:
                    h0r, h0i = past[3][i], past[4][i]
                mix, sre, sim = ssm_mixer(h, w_in_odd[i], ssm_a_re[i], ssm_a_im[i], ssm_b_re[i], ssm_b_im[i],
                                          ssm_c_re[i], ssm_c_im[i], ssm_log_dt[i], ssm_d[i], w_glu[i], h0r, h0i)
                sre_new.append(sre)
                sim_new.append(sim)
            x = x + mix
            mk, mv = mem_kv[l]
            x = x + cross_attend(rms_norm(x, ln_cross[l]), mk, mv, w_xq[l], w_xo[l])
            x = x + 0.5 * swiglu(rms_norm(x, ln_ffn2[l]), w_ffn2_gu[l], w_ffn2_down[l])
        y = rms_norm(x, ln_final)
        return (y, jnp.stack(lat_new), jnp.stack(kr_new), jnp.stack(conv_new),
                jnp.stack(sre_new), jnp.stack(sim_new))

    mem_p = [memory_kv(mem_prompt, ln_mem[l], w_xk[l], w_xv[l]) for l in range(DEPTH)]
    mem_k_p = jnp.stack([kv[0] for kv in mem_p])
    mem_v_p = jnp.stack([kv[1] for kv in mem_p])
    y_prompt, lat_p, kr_p, conv_p, sre_p, sim_p = run(x_prompt, None, mem_p)

    mem_s = [(cache_mem_k[l], cache_mem_v[l]) for l in range(DEPTH)]
    past = (cache_mla_latent, cache_mla_krope, state_conv, state_ssm_re, state_ssm_im)
    y_sample, lat_s, kr_s, conv_s, sre_s, sim_s = run(x_sample, past, mem_s)

    return (y_prompt, y_sample, lat_p, kr_p, conv_p, sre_p, sim_p, mem_k_p, mem_v_p,
            lat_s, kr_s, conv_s, sre_s, sim_s)
```

```python
from contextlib import ExitStack
import os
import numpy as np
import concourse.bass as bass
import concourse.mybir as mybir
from concourse.bass_utils import run_bass_kernel_spmd

F32 = mybir.dt.float32
BF16 = mybir.dt.bfloat16
AF = mybir.ActivationFunctionType
ALU = mybir.AluOpType

D = 1024
DFF = 2816
EPS = 1e-6
NCORES = 8


class Buf:
    __slots__ = ("name", "w", "r", "wj")

    def __init__(self, name=""):
        self.name = name
        self.w = None
        self.r = []
        self.wj = []


class Eng:
    def __init__(self, name, handle, sem):
        self.name = name
        self.h = handle
        self.sem = sem
        self.count = 0
        self.seen = {}


class Sched:
    def __init__(self, nc, ctx):
        self.nc = nc
        self.sems = {}
        self.E = {}
        for name, h in (("pe", nc.tensor), ("act", nc.scalar), ("dve", nc.vector),
                        ("pool", nc.gpsimd), ("sp", nc.sync)):
            s = ctx.enter_context(nc.semaphore("s_" + name))
            self.sems[name] = s
            self.E[name] = Eng(name, h, s)
        self.dpool = {"sp": [], "pool": [], "act": []}
        self.drr = {"sp": 0, "pool": 0, "act": 0}
        for q, n in (("sp", 16), ("pool", 10), ("act", 2)):
            for i in range(n):
                key = "d%s%d" % (q, i)
                self.sems[key] = ctx.enter_context(nc.semaphore(key))
                self.dpool[q].append([key, 0, None])
        self.n_inst = 0
        self.n_pe = 0
        self.marks = []

    def mark(self, name):
        self.marks.append((name, self.n_pe))

    def _wait(self, eng, tok):
        if tok is None:
            return
        key, val = tok
        if eng.seen.get(key, 0) >= val:
            return
        if key == eng.name:
            assert val <= eng.count, "wait on own future"
        eng.h.wait_ge(self.sems[key], val)
        eng.seen[key] = val

    def _deps(self, reads, writes, join=False):
        toks = {}

        def add(t):
            if t is not None and toks.get(t[0], 0) < t[1]:
                toks[t[0]] = t[1]
        for b in reads:
            add(b.w)
            for t in b.wj:
                add(t)
        for b in writes:
            add(b.w)
            if not join:
                for t in b.wj:
                    add(t)
            for t in b.r:
                add(t)
        return toks

    def _record(self, tok, reads, writes, join=False):
        for b in reads:
            b.r.append(tok)
            if len(b.r) > 64:
                mx = {}
                for k, v in b.r:
                    if mx.get(k, 0) < v:
                        mx[k] = v
                b.r = list(mx.items())
        for b in writes:
            if join:
                b.wj.append(tok)
            else:
                b.w = tok
                b.wj = []
            b.r = []

    def group(self, en, fns, reads=(), writes=()):
        eng = self.E[en]
        for k, v in self._deps(reads, writes).items():
            if en == "pe" and k == "pe":
                continue
            self._wait(eng, (k, v))
        inst = None
        for fn in fns:
            inst = fn()
            self.n_inst += 1
            if en == "pe":
                self.n_pe += 1
        inst.then_inc(eng.sem, 1)
        eng.count += 1
        tok = (en, eng.count)
        self._record(tok, reads, writes)
        return tok

    def op(self, en, fn, reads=(), writes=()):
        return self.group(en, [fn], reads, writes)

    def dma(self, en, out_ap, in_ap, reads=(), writes=(), join=False, **kw):
        eng = self.E[en]
        pool = self.dpool[en]
        slot = pool[self.drr[en]]
        self.drr[en] = (self.drr[en] + 1) % len(pool)
        self._wait(eng, slot[2])
        for k, v in self._deps(reads, writes, join).items():
            self._wait(eng, (k, v))
        inst = eng.h.dma_start(out=out_ap, in_=in_ap, **kw)
        self.n_inst += 1
        slot[1] += 16
        inst.then_inc(self.sems[slot[0]], 16)
        tok = (slot[0], slot[1])
        slot[2] = tok
        self._record(tok, reads, writes, join)
        return tok

    def barrier(self):
        toks = [(e.name, e.count) for e in self.E.values() if e.count > 0]
        for q in self.dpool.values():
            for s in q:
                if s[2] is not None:
                    toks.append(s[2])
        for e in self.E.values():
            for t in toks:
                if t[0] == e.name:
                    continue
                self._wait(e, t)

    def finish(self):
        eng = self.E["sp"]
        for q in self.dpool.values():
            for s in q:
                self._wait(eng, s[2])
        for e in self.E.values():
            if e.name != "sp" and e.count > 0:
                self._wait(eng, (e.name, e.count))


WEIGHT_NAMES = ['ln_ffn1', 'w_ffn1_gu', 'w_ffn1_down', 'ln_mix', 'w_in_even', 'q_norm', 'kv_norm', 'w_uq', 'w_ukv',
                'conv_w', 'w_out_even', 'w_in_odd', 'ssm_a_re', 'ssm_a_im', 'ssm_b_re', 'ssm_b_im', 'ssm_c_re',
                'ssm_c_im', 'ssm_log_dt', 'ssm_d', 'w_glu', 'ln_cross', 'ln_mem', 'w_xq', 'w_xk', 'w_xv', 'w_xo',
                'ln_ffn2', 'w_ffn2_gu', 'w_ffn2_down', 'ln_final']
WEIGHT_SHAPES = {
    'ln_ffn1': [2, 1024], 'w_ffn1_gu': [2, 1024, 5632], 'w_ffn1_down': [2, 2816, 1024], 'ln_mix': [2, 1024],
    'w_in_even': [1, 1024, 2080], 'q_norm': [1, 256], 'kv_norm': [1, 256], 'w_uq': [1, 256, 768],
    'w_ukv': [1, 256, 1024], 'conv_w': [1, 3, 512], 'w_out_even': [1, 1024, 1024], 'w_in_odd': [1, 1024, 1024],
    'ssm_a_re': [1, 64, 64], 'ssm_a_im': [1, 64, 64], 'ssm_b_re': [1, 64, 64, 16], 'ssm_b_im': [1, 64, 64, 16],
    'ssm_c_re': [1, 64, 16, 64], 'ssm_c_im': [1, 64, 16, 64], 'ssm_log_dt': [1, 64], 'ssm_d': [1, 1024],
    'w_glu': [1, 1024, 2048], 'ln_cross': [2, 1024], 'ln_mem': [2, 1024], 'w_xq': [2, 1024, 1024],
    'w_xk': [2, 1024, 1024], 'w_xv': [2, 1024, 1024], 'w_xo': [2, 1024, 1024], 'ln_ffn2': [2, 1024],
    'w_ffn2_gu': [2, 1024, 5632], 'w_ffn2_down': [2, 2816, 1024], 'ln_final': [1024]}


def build(parts="all"):
    nc = bass.Bass("TRN2", target_bir_lowering=False)

    def din(name, shape):
        return nc.dram_tensor(name, shape, F32, kind="ExternalInput").ap()

    def dout(name, shape):
        return nc.dram_tensor(name, shape, F32, kind="ExternalOutput").ap()

    xp = din("xp", [2, 2048, D])
    xs = din("xs", [4, 64, D])
    W = {n: din(n, WEIGHT_SHAPES[n]) for n in WEIGHT_NAMES}
    memp = din("memp", [2, 256, D])
    cmk = din("cmk", [2, 4, 256, D])
    cmv = din("cmv", [2, 4, 256, D])
    y_p = dout("y_p", [2, 2048, D])
    y_s = dout("y_s", [4, 64, D])
    rope = din("rope", [2112, 32])
    ropeT = din("ropeT", [64, 2112])
    latc = din("latc", [4, 1024, 256])
    krc = din("krc", [4, 1024, 32])
    convc = din("convc", [4, 2, 512])
    ssre = din("ssre", [4, 64, 64])
    ssim = din("ssim", [4, 64, 64])
    o_sre_p = dout("o_sre_p", [2, 64, 64])
    o_sim_p = dout("o_sim_p", [2, 64, 64])
    o_sre_s = dout("o_sre_s", [4, 64, 64])
    o_sim_s = dout("o_sim_s", [4, 64, 64])
    o_lat_p = dout("o_lat_p", [2, 2048, 256])
    o_kr_p = dout("o_kr_p", [2, 2048, 32])
    o_conv_p = dout("o_conv_p", [2, 2, 512])
    o_lat_s = dout("o_lat_s", [4, 64, 256])
    o_kr_s = dout("o_kr_s", [4, 64, 32])
    o_conv_s = dout("o_conv_s", [4, 2, 512])
    o_mk = dout("o_mk", [2, 2, 256, D])
    o_mv = dout("o_mv", [2, 2, 256, D])

    with ExitStack() as ctx:
        S = Sched(nc, ctx)

        uid = [0]

        def sb(c, name, shape, dt):
            uid[0] += 1
            return c.enter_context(nc.sbuf_tensor("%s_%d" % (name, uid[0]), shape, dt))

        xT = sb(ctx, "xT", [128, 8, 1024], F32)
        hT = sb(ctx, "hT", [128, 8, 1024], BF16)
        bx = [Buf("x0"), Buf("x1")]
        bh = [Buf("h0"), Buf("h1")]
        ident_f = sb(ctx, "ident_f", [128, 128], F32)
        ones_b = sb(ctx, "ones_b", [128, 128], BF16)
        b_const = Buf("const")
        gains = sb(ctx, "gains", [128, 11, 8], F32)
        b_gains = Buf("gains")
        stage = [sb(ctx, "stage%d" % i, [128, 1024], F32) for i in range(2)]
        b_stage = [Buf("stage0"), Buf("stage1")]
        sq = sb(ctx, "sq", [128, 8, 512], BF16)
        b_sq = Buf("sq")
        rs = sb(ctx, "rs", [128, 512], F32)
        b_rs = Buf("rs")
        pb = [ctx.enter_context(nc.psum_tensor("pb%d" % i, [128, 512], F32)) for i in range(7)]
        pT = ctx.enter_context(nc.psum_tensor("pT", [128, 1024], BF16))
        bp = [Buf("pb%d" % i) for i in range(7)]
        bpT = Buf("pT")
        ident_b = sb(ctx, "ident_b", [128, 128], BF16)
        latT = sb(ctx, "latT", [128, 2, 2048], BF16); blatT = Buf("latT")
        krT = sb(ctx, "krT", [128, 2048], BF16); bkrT = Buf("krT")
        chist = sb(ctx, "chist", [128, 4, 2], F32); bchist = Buf("chist")
        onesP = sb(ctx, "onesP", [128, 2, 128], BF16)
        zerosb = sb(ctx, "zerosb", [128, 128], BF16)
        hstate = sb(ctx, "hstate", [128, 2, 32, 4], F32); bhst = Buf("hstate")

        S.op("pool", lambda: nc.gpsimd.memset(ident_f[:], 0.0), writes=[b_const])
        S.op("pool", lambda: nc.gpsimd.affine_select(ident_f[:], ident_f[:], [[-1, 128]], ALU.not_equal, 1.0,
                                                     base=0, channel_multiplier=1), reads=[b_const], writes=[b_const])
        S.op("pool", lambda: nc.gpsimd.memset(ones_b[:], 1.0), writes=[b_const])
        S.op("pool", lambda: nc.gpsimd.tensor_copy(ident_b[:], ident_f[:]), reads=[b_const], writes=[b_const])
        S.op("pool", lambda: nc.gpsimd.memset(onesP[:], 0.0), writes=[b_const])
        S.op("pool", lambda: nc.gpsimd.memset(zerosb[:], 0.0), writes=[b_const])
        S.op("pool", lambda: nc.gpsimd.memset(onesP[:, 0, 0:64], 1.0), reads=[b_const], writes=[b_const])
        S.op("pool", lambda: nc.gpsimd.memset(onesP[:, 1, 64:128], 1.0), reads=[b_const], writes=[b_const])
        S.op("pool", lambda: nc.gpsimd.memset(krT[:], 0.0), writes=[bkrT])
        gsrc = [W['ln_ffn1'][0], W['ln_ffn1'][1], W['ln_mix'][0], W['ln_mix'][1], W['ln_cross'][0], W['ln_cross'][1],
                W['ln_ffn2'][0], W['ln_ffn2'][1], W['ln_final'], W['ln_mem'][0], W['ln_mem'][1]]
        for i, g in enumerate(gsrc):
            S.dma("sp", gains[:, i, :], g.rearrange("(k p) -> p k", p=128), writes=[b_gains],
                  allow_slow_non_contiguous=True)
        G_FFN1, G_MIX, G_CROSS, G_FFN2, G_FINAL, G_MEM = 0, 2, 4, 6, 8, 9

        def load_x(src_rows, ntok):
            for i in range(ntok // 128):
                st, bs = stage[i % 2], b_stage[i % 2]
                S.dma("sp", st[:], src_rows[i * 128:(i + 1) * 128, :], writes=[bs])
                for half in range(2):
                    pbank, bbank = pb[half], bp[half]
                    S.group("pe", [
                        (lambda kk=kk: nc.tensor.transpose(pbank[:, kk * 128:(kk + 1) * 128],
                                                           st[:, (half * 4 + kk) * 128:(half * 4 + kk + 1) * 128],
                                                           ident_f[:]))
                        for kk in range(4)], reads=[bs, b_const], writes=[bbank])
                    eng = "act" if half == 0 else "dve"
                    dst = xT[:, half * 4:(half + 1) * 4, i * 128:(i + 1) * 128]
                    srcp = pbank[:, :].rearrange("p (k t) -> p k t", k=4)
                    if eng == "act":
                        S.op("act", lambda: nc.scalar.copy(dst, srcp), reads=[bbank], writes=[bx[i // 4]])
                    else:
                        S.op("dve", lambda: nc.vector.tensor_copy(dst, srcp), reads=[bbank], writes=[bx[i // 4]])

        def norm_stats(t, T):
            sl = slice(t * 512, t * 512 + T)
            S.op("act", lambda: nc.scalar.activation(sq[:, :, 0:T], xT[:, :, sl], AF.Square),
                 reads=[bx[t]], writes=[b_sq])
            S.group("pe", [(lambda k=k: nc.tensor.matmul(pb[6][:, 0:T], ones_b[:], sq[:, k, 0:T],
                                                         start=(k == 0), stop=(k == 7))) for k in range(8)],
                    reads=[b_sq, b_const], writes=[bp[6]])
            S.op("act", lambda: nc.scalar.activation(rs[:, 0:T], pb[6][:, 0:T], AF.Ln, bias=EPS, scale=1.0 / D),
                 reads=[bp[6]], writes=[b_rs])
            S.op("act", lambda: nc.scalar.activation(rs[:, 0:T], rs[:, 0:T], AF.Exp, scale=-0.5),
                 reads=[b_rs], writes=[b_rs])

        def norm_to_h(gi, tiles):
            for t, T in tiles:
                sl = slice(t * 512, t * 512 + T)
                norm_stats(t, T)
                for k in range(8):
                    S.op("dve", lambda k=k: nc.vector.scalar_tensor_tensor(
                        hT[:, k, sl], xT[:, k, sl], gains[:, gi, k:k + 1], rs[:, 0:T], ALU.mult, ALU.mult),
                        reads=[bx[t], b_rs, b_gains], writes=[bh[t]])

        def ffn(l, which, tiles, c):
            wgu_d = W['w_ffn%d_gu' % which][l].rearrange("(k p) n -> p k n", p=128)
            wdn_d = W['w_ffn%d_down' % which][l].rearrange("(j p) n -> p j n", p=128)
            fb = getattr(c, "ffn_bufs", None)
            if fb is None:
                NS = 4
                fb = dict(NS=NS, gcount=[0],
                          wgu=[sb(c, "wgu%d" % i, [128, 8, 2, 512], BF16) for i in range(NS)],
                          wdn=[sb(c, "wdn%d" % i, [128, 4, 1024], BF16) for i in range(NS)],
                          bw=[Buf("w%d" % i) for i in range(NS)],
                          sg=[sb(c, "sg%d" % i, [128, 512], BF16) for i in range(2)], bsg=[Buf(), Buf()],
                          hid=[sb(c, "hid%d" % i, [128, 4, 512], BF16) for i in range(2)], bhid=[Buf(), Buf()])
                c.ffn_bufs = fb
            wgu, wdn, bw, sg, bsg, hid, bhid = fb["wgu"], fb["wdn"], fb["bw"], fb["sg"], fb["bsg"], fb["hid"], fb["bhid"]
            norm_to_h((G_FFN1 if which == 1 else G_FFN2) + l, tiles)
            groups = [(0, 4), (4, 4), (8, 4), (12, 4), (16, 4), (20, 2)]
            st = {"cnt": 0, "hc": 0, "oc": 0}

            def gu(s, ng, t, T, hb):
                sl = slice(t * 512, t * 512 + T)
                for j in range(ng):
                    cnt = st["cnt"]
                    st["cnt"] += 1
                    pg, pu = pb[2 * (cnt % 2)], pb[2 * (cnt % 2) + 1]
                    bg, bu = bp[2 * (cnt % 2)], bp[2 * (cnt % 2) + 1]
                    S.group("pe", [(lambda k=k: nc.tensor.matmul(pg[:, 0:T], wgu[s][:, k, 0, j * 128:(j + 1) * 128],
                                                                 hT[:, k, sl], start=(k == 0), stop=(k == 7)))
                                   for k in range(8)], reads=[bw[s], bh[t]], writes=[bg])
                    S.group("pe", [(lambda k=k: nc.tensor.matmul(pu[:, 0:T], wgu[s][:, k, 1, j * 128:(j + 1) * 128],
                                                                 hT[:, k, sl], start=(k == 0), stop=(k == 7)))
                                   for k in range(8)], reads=[bw[s], bh[t]], writes=[bu])
                    sgi = cnt % 2
                    S.op("act", lambda: nc.scalar.activation(sg[sgi][:, 0:T], pg[:, 0:T], AF.Silu),
                         reads=[bg], writes=[bsg[sgi]])
                    S.op("dve", lambda: nc.vector.tensor_tensor(hid[hb][:, j, 0:T], pu[:, 0:T], sg[sgi][:, 0:T], ALU.mult),
                         reads=[bu, bsg[sgi]], writes=[bhid[hb]])

            def down(s, ng, t, T, hb):
                sl = slice(t * 512, t * 512 + T)
                for o in range(8):
                    oc = st["oc"]
                    st["oc"] += 1
                    pa, ba = pb[4 + oc % 2], bp[4 + oc % 2]
                    S.group("pe", [(lambda j=j: nc.tensor.matmul(pa[:, 0:T], wdn[s][:, j, o * 128:(o + 1) * 128],
                                                                 hid[hb][:, j, 0:T], start=(j == 0), stop=(j == ng - 1)))
                                   for j in range(ng)], reads=[bw[s], bhid[hb]], writes=[ba])
                    S.op("dve", lambda: nc.vector.scalar_tensor_tensor(xT[:, o, sl], pa[:, 0:T], 0.5, xT[:, o, sl],
                                                                       ALU.mult, ALU.add),
                         reads=[ba, bx[t]], writes=[bx[t]])
            prev = None
            for gi, (j0, ng) in enumerate(groups):
                s = fb["gcount"][0] % fb["NS"]
                fb["gcount"][0] += 1
                S.dma("pool", wgu[s][:, :, 0, 0:ng * 128], wgu_d[:, :, j0 * 128:(j0 + ng) * 128], writes=[bw[s]])
                S.dma("pool", wgu[s][:, :, 1, 0:ng * 128], wgu_d[:, :, DFF + j0 * 128:DFF + (j0 + ng) * 128], writes=[bw[s]], join=True)
                S.dma("pool", wdn[s][:, 0:ng, :], wdn_d[:, j0:j0 + ng, :], writes=[bw[s]], join=True)
                for t, T in tiles:
                    hb = st["hc"] % 2
                    st["hc"] += 1
                    gu(s, ng, t, T, hb)
                    if prev is not None:
                        down(*prev)
                    prev = (s, ng, t, T, hb)
            down(*prev)

        def even_mixer(kind, s, hf, tiles, c):
            norm_to_h(G_MIX + 0, tiles)
            win = W['w_in_even'][0].rearrange("(k p) n -> p k n", p=128)
            SC = (64 + 32) ** -0.5
            ntok = sum(T for _, T in tiles)
            wkv = sb(c, "wkv", [128, 8, 288], BF16); bwkv = Buf()
            S.dma("pool", wkv[:], win[:, :, 256:544], writes=[bwkv])
            wqin = sb(c, "wqin", [128, 8, 256], BF16); bwqin = Buf()
            S.dma("pool", wqin[:], win[:, :, 0:256], writes=[bwqin])
            wuq = sb(c, "wuq", [128, 2, 768], BF16); bwuq = Buf()
            S.dma("pool", wuq[:], W['w_uq'][0].rearrange("(k p) n -> p k n", p=128), writes=[bwuq])
            wsw = sb(c, "wsw", [128, 2, 768], BF16); bwsw = Buf()
            wukv = sb(c, "wukv", [128, 2, 1024], BF16); bwukv = Buf()
            S.dma("pool", wukv[:], W['w_ukv'][0].rearrange("(k p) n -> p k n", p=128), writes=[bwukv])
            wout = sb(c, "wout", [128, 8, 1024], BF16); bwout = Buf()
            S.dma("pool", wout[:], W['w_out_even'][0].rearrange("(k p) n -> p k n", p=128), writes=[bwout])
            wuq4 = wuq[:, :, :].rearrange("p k (h d) -> p (k h) d", d=96)
            wsw4 = wsw[:, :, :].rearrange("p k (h d) -> p (k h) d", d=96)
            S.op("pool", lambda: nc.gpsimd.memset(wsw[:], 0.0), writes=[bwsw])
            S.op("dve", lambda: nc.vector.tensor_scalar(wsw4[:, :, 64:80], wuq4[:, :, 80:96], -1.0, None, ALU.mult),
                 reads=[bwuq, bwsw], writes=[bwsw])
            S.op("dve", lambda: nc.vector.tensor_copy(wsw4[:, :, 80:96], wuq4[:, :, 64:80]), reads=[bwuq, bwsw], writes=[bwsw])
            kvn = sb(c, "kvn", [128, 256], F32); bkvn = Buf()
            S.dma("sp", kvn[:], W['kv_norm'][0].partition_broadcast(128), writes=[bkvn])
            qn = sb(c, "qn", [128, 2], F32); bqn = Buf()
            S.dma("sp", qn[:], W['q_norm'][0].rearrange("(m p) -> p m", p=128), writes=[bqn], allow_slow_non_contiguous=True)
            cw = sb(c, "cw", [128, 4, 3], F32); bcw = Buf()
            for k3 in range(3):
                S.dma("sp", cw[:, :, k3], W['conv_w'][0][k3].rearrange("(c p) -> p c", p=128), writes=[bcw], allow_slow_non_contiguous=True)
            cosT = sb(c, "cosT", [128, 1024], F32); sinT = sb(c, "sinT", [128, 1024], F32); btab = Buf()
            p0 = hf * 1024 if kind == "p" else 2048
            npos = 1024 if kind == "p" else 64
            S.dma("sp", cosT[64:96, 0:npos], ropeT[0:32, p0:p0 + npos], writes=[btab])
            S.dma("sp", sinT[64:96, 0:npos], ropeT[32:64, p0:p0 + npos], writes=[btab])
            rt = [sb(c, "rt%d" % i, [128, 32], F32) for i in range(2)]; brt = [Buf(), Buf()]
            junk = sb(c, "junk", [128, 256], F32); bjunk = Buf()
            ss = [sb(c, "ss%d" % i, [128, 1], F32) for i in range(2)]; bss = [Buf(), Buf()]
            latf = [sb(c, "latf%d" % i, [128, 256], F32) for i in range(2)]; blat = [Buf(), Buf()]
            krf = [sb(c, "krf%d" % i, [128, 32], F32) for i in range(2)]; bkr = [Buf(), Buf()]
            tmp = sb(c, "tmpk", [128, 32], F32); btmp = Buf()
            latb = [sb(c, "latb%d" % i, [128, 256], BF16) for i in range(2)]; blatb = [Buf(), Buf()]
            krb = [sb(c, "krb%d" % i, [128, 128], BF16) for i in range(2)]; bkrb = [Buf(), Buf()]
            for i in range(2):
                S.op("pool", lambda i=i: nc.gpsimd.memset(krb[i][:], 0.0), writes=[bkrb[i]])
            block = [Buf(), Buf()]
            cqnT = sb(c, "cqnT", [128, 2, 1024], BF16); bcqn = Buf()
            KT = sb(c, "KT", [128, 2, 2048], BF16); bKT = Buf()
            Vp = sb(c, "Vp", [128, 2, 17, 128], BF16); bVp = Buf()
            S.op("pool", lambda: nc.gpsimd.memset(Vp[:], 0.0), writes=[bVp])
            QTs = [sb(c, "QT%d" % i, [128, 2, 512], BF16) for i in range(2)]; bQTs = [Buf(), Buf()]
            attT = sb(c, "attT", [128, 4, 1024], BF16); battT = [Buf() for _ in range(4)]
            PT = [sb(c, "PT%d" % i, [128, 512], BF16) for i in range(3)]; bPT = [Buf(), Buf(), Buf()]
            tq1 = sb(c, "tq1", [128, 512], F32); tq2 = sb(c, "tq2", [128, 512], F32); btq1 = Buf(); btq2 = Buf()
            rec = sb(c, "recm", [128, 512], F32); brec = Buf()
            bqblock = Buf()
            bvblock = Buf()
            cnt = {"u": 0, "pt": 0, "s": 0, "o": 0}

            def kv_unit(tok0, n, rrow, dlat, dkr, key0):
                a = cnt["u"] % 2
                cnt["u"] += 1
                ps, bps = pb[2 + a], bp[2 + a]
                tb = bh[tok0 // 512]
                S.dma("sp", rt[a][0:n, :], rope[rrow:rrow + n, :], writes=[brt[a]])
                S.group("pe", [(lambda k=k: nc.tensor.matmul(ps[0:n, 0:288], hT[:, k, tok0:tok0 + n], wkv[:, k, :],
                                                             start=(k == 0), stop=(k == 7))) for k in range(8)],
                        reads=[tb, bwkv], writes=[bps])
                S.op("act", lambda: nc.scalar.activation(junk[0:n, :], ps[0:n, 0:256], AF.Square, accum_out=ss[a][0:n, 0:1]),
                     reads=[bps], writes=[bjunk, bss[a], block[a]])
                S.op("act", lambda: nc.scalar.activation(ss[a][0:n, :], ss[a][0:n, :], AF.Ln, bias=EPS, scale=1.0 / 256.0),
                     reads=[bss[a]], writes=[bss[a]])
                S.op("act", lambda: nc.scalar.activation(ss[a][0:n, :], ss[a][0:n, :], AF.Exp, scale=-0.5),
                     reads=[bss[a]], writes=[bss[a]])
                S.op("dve", lambda: nc.vector.scalar_tensor_tensor(latf[a][0:n, :], ps[0:n, 0:256], ss[a][0:n, 0:1], kvn[0:n, :],
                                                                   ALU.mult, ALU.mult), reads=[bps, bss[a], bkvn], writes=[blat[a], block[a]])
                S.dma("sp", dlat, latf[a][0:n, :], reads=[blat[a]])
                x1, x2 = ps[0:n, 256:272], ps[0:n, 272:288]
                cs, sn = rt[a][0:n, 0:16], rt[a][0:n, 16:32]
                S.op("dve", lambda: nc.vector.tensor_tensor(krf[a][0:n, 0:16], x1, cs, ALU.mult), reads=[bps, brt[a]], writes=[bkr[a], block[a]])
                S.op("dve", lambda: nc.vector.tensor_tensor(tmp[0:n, 0:16], x2, sn, ALU.mult), reads=[bps, brt[a]], writes=[btmp, block[a]])
                S.op("dve", lambda: nc.vector.tensor_tensor(krf[a][0:n, 0:16], krf[a][0:n, 0:16], tmp[0:n, 0:16], ALU.subtract),
                     reads=[bkr[a], btmp], writes=[bkr[a]])
                S.op("dve", lambda: nc.vector.tensor_tensor(krf[a][0:n, 16:32], x1, sn, ALU.mult), reads=[bps, brt[a]], writes=[bkr[a], block[a]])
                S.op("dve", lambda: nc.vector.tensor_tensor(tmp[0:n, 16:32], x2, cs, ALU.mult), reads=[bps, brt[a], bkr[a]], writes=[btmp, block[a]])
                S.op("dve", lambda: nc.vector.tensor_tensor(krf[a][0:n, 16:32], krf[a][0:n, 16:32], tmp[0:n, 16:32], ALU.add),
                     reads=[bkr[a], btmp], writes=[bkr[a]])
                S.dma("sp", dkr, krf[a][0:n, :], reads=[bkr[a]])
                S.op("pool", lambda: nc.gpsimd.tensor_copy(latb[a][0:n, :], latf[a][0:n, :]), reads=[blat[a]], writes=[blatb[a]])
                S.op("pool", lambda: nc.gpsimd.tensor_copy(krb[a][0:n, 64:96], krf[a][0:n, :]), reads=[bkr[a]], writes=[bkrb[a]])
                return lambda: keys_from_tok(latb[a], blatb[a], krb[a], bkrb[a], n, key0)

            def keys_from_tok(lb, blb, kb, bkb, n, key0):
                S.group("pe", [(lambda: nc.tensor.transpose(pT[:, 0:n], lb[0:n, 0:128], ident_b[0:n, 0:n])),
                               (lambda: nc.tensor.transpose(pT[:, 128:128 + n], lb[0:n, 128:256], ident_b[0:n, 0:n])),
                               (lambda: nc.tensor.transpose(pT[:, 256:256 + n], kb[0:n, :], ident_b[0:n, 0:n]))],
                        reads=[blb, bkb, b_const], writes=[bpT])
                S.op("act", lambda: nc.scalar.copy(latT[:, :, key0:key0 + n], pT[:, 0:256].rearrange("p (k t) -> p k t", k=2)[:, :, 0:n]),
                     reads=[bpT], writes=[blatT])
                S.op("act", lambda: nc.scalar.copy(krT[64:96, key0:key0 + n], pT[64:96, 256:256 + n]), reads=[bpT], writes=[bkrT])

            def make_cqn():
                for t, T in tiles:
                    sl = slice(t * 512, t * 512 + T)
                    for m in range(2):
                        S.group("pe", [(lambda k=k: nc.tensor.matmul(pb[m][:, 0:T], wqin[:, k, m * 128:(m + 1) * 128], hT[:, k, sl],
                                                                     start=(k == 0), stop=(k == 7))) for k in range(8)],
                                reads=[bwqin, bh[t]], writes=[bp[m]])
                        S.op("act", lambda m=m: nc.scalar.activation(sq[:, m, 0:T], pb[m][:, 0:T], AF.Square), reads=[bp[m]], writes=[b_sq])
                    S.group("pe", [(lambda m=m: nc.tensor.matmul(pb[6][:, 0:T], ones_b[:], sq[:, m, 0:T], start=(m == 0), stop=(m == 1)))
                                   for m in range(2)], reads=[b_sq, b_const], writes=[bp[6]])
                    S.op("act", lambda: nc.scalar.activation(rs[:, 0:T], pb[6][:, 0:T], AF.Ln, bias=EPS, scale=1.0 / 256.0), reads=[bp[6]], writes=[b_rs])
                    S.op("act", lambda: nc.scalar.activation(rs[:, 0:T], rs[:, 0:T], AF.Exp, scale=-0.5), reads=[b_rs], writes=[b_rs])
                    for m in range(2):
                        S.op("dve", lambda m=m: nc.vector.scalar_tensor_tensor(cqnT[:, m, sl], pb[m][:, 0:T], qn[:, m:m + 1], rs[:, 0:T],
                                                                               ALU.mult, ALU.mult), reads=[bp[m], bqn, b_rs], writes=[bcqn])

            def add_x(pa, ba, o, sl, T):
                tb = bx[sl.start // 512]
                S.op("dve", lambda: nc.vector.scalar_tensor_tensor(xT[:, o, sl], pa[:, 0:T], 1.0, xT[:, o, sl], ALU.mult, ALU.add),
                     reads=[ba, tb], writes=[tb])

            def conv_path():
                with ExitStack() as cl:
                    conv_path_(cl)
                    S.barrier()

            def conv_path_(cl):
                upad = sb(cl, "upad", [128, 1056], F32); bup = Buf()
                Vs = [sb(cl, "Vs%d" % i, [128, 512], F32) for i in range(2)]; bVs = [Buf(), Buf()]
                Bs = [sb(cl, "Bs%d" % i, [128, 512], F32) for i in range(2)]; bBs = [Buf(), Buf()]
                cacc = [sb(cl, "cacc%d" % i, [128, 512], F32) for i in range(2)]; bca = [Buf(), Buf()]
                zb = [sb(cl, "zb%d" % i, [128, 512], BF16) for i in range(2)]; bzb = [Buf(), Buf()]
                wcv = [sb(cl, "wcv%d" % i, [128, 8, 3, 128], BF16) for i in range(2)]; bwcv = [Buf(), Buf()]
                up3 = upad[:, 0:264].rearrange("p (q t) -> p q t", q=4)
                ucnt = [0]

                def part1(cc, t, T, wv_, bwv_):
                    u = ucnt[0] % 2
                    ucnt[0] += 1
                    sl = slice(t * 512, t * 512 + T)
                    for j in range(3):
                        S.group("pe", [(lambda k=k: nc.tensor.matmul(pb[j][:, 0:T], wv_[:, k, j, :], hT[:, k, sl], start=(k == 0), stop=(k == 7)))
                                       for k in range(8)], reads=[bwv_, bh[t]], writes=[bp[j]])
                    S.op("act", lambda: nc.scalar.copy(Vs[u][:, 0:T], pb[2][:, 0:T]), reads=[bp[2]], writes=[bVs[u]])
                    S.op("act", lambda: nc.scalar.copy(Bs[u][:, 0:T], pb[0][:, 0:T]), reads=[bp[0]], writes=[bBs[u]])
                    if kind == "p":
                        S.op("dve", lambda: nc.vector.tensor_tensor(upad[:, 2 + t * 512:2 + t * 512 + T], pb[1][:, 0:T], Vs[u][:, 0:T], ALU.mult),
                             reads=[bp[1], bVs[u]], writes=[bup])
                        srcs = [upad[:, t * 512 + k:t * 512 + k + T] for k in range(3)]
                        dsts = cacc[u][:, 0:T]
                    else:
                        S.op("dve", lambda: nc.vector.tensor_tensor(up3[:, :, 2:66], pb[1][:, 0:256].rearrange("p (q t) -> p q t", q=4),
                                                                    Vs[u][:, 0:256].rearrange("p (q t) -> p q t", q=4), ALU.mult),
                             reads=[bp[1], bVs[u]], writes=[bup])
                        srcs = [up3[:, :, k:k + 64] for k in range(3)]
                        dsts = cacc[u][:, 0:256].rearrange("p (q t) -> p q t", q=4)
                    S.op("dve", lambda: nc.vector.tensor_scalar(dsts, srcs[0], cw[:, cc, 0:1], None, ALU.mult), reads=[bup, bcw], writes=[bca[u]])
                    S.op("dve", lambda: nc.vector.scalar_tensor_tensor(dsts, srcs[1], cw[:, cc, 1:2], dsts, ALU.mult, ALU.add),
                         reads=[bup, bcw, bca[u]], writes=[bca[u]])
                    S.op("dve", lambda: nc.vector.scalar_tensor_tensor(dsts, srcs[2], cw[:, cc, 2:3], dsts, ALU.mult, ALU.add),
                         reads=[bup, bcw, bca[u]], writes=[bca[u]])
                    S.op("dve", lambda: nc.vector.tensor_tensor(zb[u][:, 0:T], cacc[u][:, 0:T], Bs[u][:, 0:T], ALU.mult), reads=[bca[u], bBs[u]], writes=[bzb[u]])

                    def part2():
                        for o in range(8):
                            pa, ba = pb[4 + o % 2], bp[4 + o % 2]
                            S.group("pe", [lambda: nc.tensor.matmul(pa[:, 0:T], wout[:, 4 + cc, o * 128:(o + 1) * 128], zb[u][:, 0:T], start=True, stop=True)],
                                    reads=[bwout, bzb[u]], writes=[ba])
                            add_x(pa, ba, o, sl, T)
                    return part2
                pend = None
                for cc in range(4):
                    wv_, bwv_ = wcv[cc % 2], bwcv[cc % 2]
                    for j, c0 in enumerate((544, 1056, 1568)):
                        S.dma("pool", wv_[:, :, j, :], win[:, :, c0 + cc * 128:c0 + (cc + 1) * 128], writes=[bwv_], join=(j > 0))
                    if kind == "p":
                        if hf == 0:
                            S.op("pool", lambda: nc.gpsimd.memset(upad[:, 0:2], 0.0), reads=[bup], writes=[bup])
                        else:
                            S.op("pool", lambda: nc.gpsimd.tensor_copy(upad[:, 0:2], chist[:, cc, :]), reads=[bchist, bup], writes=[bup])
                    else:
                        for r2 in range(2):
                            S.dma("sp", up3[:, :, r2], convc[:, r2, cc * 128:(cc + 1) * 128].rearrange("q p -> p q"), reads=[bup], writes=[bup],
                                  allow_slow_non_contiguous=True)
                    for t, T in tiles:
                        nxt = part1(cc, t, T, wv_, bwv_)
                        if pend is not None:
                            pend()
                        pend = nxt
                    if kind == "p" and hf == 0:
                        S.op("pool", lambda: nc.gpsimd.tensor_copy(chist[:, cc, :], upad[:, 1024:1026]), reads=[bup], writes=[bchist])
                pend()

            def build_kv(hp, nk):
                nkt = (nk + 127) // 128
                for e in range(2):
                    h = 2 * hp + e
                    for k0 in range(0, nk, 512):
                        n = min(512, nk - k0)
                        pk, bk = pb[cnt["s"] % 2], bp[cnt["s"] % 2]
                        cnt["s"] += 1
                        S.group("pe", [(lambda kc=kc: nc.tensor.matmul(pk[0:64, 0:n], wukv[:, kc, h * 128:h * 128 + 64], latT[:, kc, k0:k0 + n],
                                                                       start=(kc == 0), stop=(kc == 1))) for kc in range(2)],
                                reads=[bwukv, blatT], writes=[bk])
                        if e == 0:
                            S.op("act", lambda: nc.scalar.copy(KT[0:64, e, k0:k0 + n], pk[0:64, 0:n]), reads=[bk], writes=[bKT])
                        else:
                            S.op("dve", lambda: nc.vector.tensor_copy(KT[0:64, e, k0:k0 + n], pk[0:64, 0:n]), reads=[bk], writes=[bKT])
                    S.op("pool", lambda: nc.gpsimd.tensor_copy(KT[64:96, e, 0:nk], krT[64:96, 0:nk]), reads=[bkrT], writes=[bKT])
                wv3 = wukv[:, :, :].rearrange("p k (h d) -> p k h d", d=128)
                for g0 in range(0, nkt, 4):
                    g1 = min(nkt, g0 + 4)
                    pk, bk = pb[cnt["s"] % 2], bp[cnt["s"] % 2]
                    cnt["s"] += 1
                    fns = []
                    for kt in range(g0, g1):
                        nkk = min(128, nk - kt * 128)
                        for kc in range(2):
                            fns.append(lambda kt=kt, kc=kc, nkk=nkk: nc.tensor.matmul(
                                pk[0:nkk, (kt - g0) * 128:(kt - g0 + 1) * 128], latT[:, kc, kt * 128:kt * 128 + nkk],
                                wv3[:, kc, 2 * hp:2 * hp + 2, 64:128], start=(kc == 0), stop=(kc == 1)))
                    S.group("pe", fns, reads=[bwukv, blatT], writes=[bk])
                    pk3 = pk[:, :].rearrange("p (t c) -> p t c", c=128)
                    ng = g1 - g0
                    S.op("act", lambda: nc.scalar.copy(Vp[:, 0, g0:g1, 0:64], pk3[:, 0:ng, 0:64]), reads=[bk], writes=[bVp, bvblock])
                    S.op("dve", lambda: nc.vector.tensor_copy(Vp[:, 1, g0:g1, 64:128], pk3[:, 0:ng, 64:128]), reads=[bk], writes=[bVp, bvblock])

            def build_qt(hp, qs, T, pos0, qb):
                QT, bQT = QTs[qb], bQTs[qb]
                for e in range(2):
                    h = 2 * hp + e
                    S.group("pe", [(lambda kc=kc: nc.tensor.matmul(pb[4][0:96, 0:T], wuq[:, kc, h * 96:(h + 1) * 96], cqnT[:, kc, qs],
                                                                   start=(kc == 0), stop=(kc == 1))) for kc in range(2)],
                            reads=[bwuq, bcqn], writes=[bp[4]])
                    S.group("pe", [(lambda kc=kc: nc.tensor.matmul(pb[5][0:96, 0:T], wsw[:, kc, h * 96:(h + 1) * 96], cqnT[:, kc, qs],
                                                                   start=(kc == 0), stop=(kc == 1))) for kc in range(2)],
                            reads=[bwsw, bcqn], writes=[bp[5]])
                    S.op("act", lambda: nc.scalar.copy(QT[0:64, e, 0:T], pb[4][0:64, 0:T]), reads=[bp[4]], writes=[bQT, bqblock])
                    S.op("dve", lambda: nc.vector.tensor_tensor(tq1[64:96, 0:T], pb[4][64:96, 0:T], cosT[64:96, pos0:pos0 + T], ALU.mult),
                         reads=[bp[4], btab], writes=[btq1, bqblock])
                    S.op("dve", lambda: nc.vector.tensor_tensor(tq2[64:96, 0:T], pb[5][64:96, 0:T], sinT[64:96, pos0:pos0 + T], ALU.mult),
                         reads=[bp[5], btab], writes=[btq2])
                    S.op("pool", lambda: nc.gpsimd.tensor_tensor(QT[64:96, e, 0:T], tq1[64:96, 0:T], tq2[64:96, 0:T], ALU.add),
                         reads=[btq1, btq2], writes=[bQT])

            def attend(hp, qs, T, ktiles, qb, prefetch=None):
                QT, bQT = QTs[qb], bQTs[qb]
                steps = [(e, kt, nkk, c0, diag) for e in range(2) for (kt, nkk, c0, diag) in ktiles]
                slots = []

                def s_mm(i):
                    e, kt, nkk, c0, diag = steps[i]
                    pS, bS = pb[cnt["s"] % 2], bp[cnt["s"] % 2]
                    cnt["s"] += 1
                    r = cnt["pt"] % 3
                    cnt["pt"] += 1
                    S.group("pe", [lambda: nc.tensor.matmul(pS[0:nkk, c0:T], KT[0:96, e, kt * 128:kt * 128 + nkk], QT[0:96, e, c0:T],
                                                            start=True, stop=True)], reads=[bKT, bQT], writes=[bS])
                    slots.append((pS, bS, r))
                s_mm(0)
                for i in range(len(steps)):
                    if i + 1 < len(steps):
                        s_mm(i + 1)
                    if i == 1 and prefetch is not None:
                        prefetch()
                        prefetch = None
                    e, kt, nkk, c0, diag = steps[i]
                    pS, bS, r = slots[i]
                    S.op("act", lambda: nc.scalar.activation(PT[r][0:nkk, c0:T], pS[0:nkk, c0:T], AF.Exp, scale=SC), reads=[bS], writes=[bPT[r]])
                    if diag:
                        S.op("pool", lambda: nc.gpsimd.memset(PT[r][64:128, c0:c0 + 64], 0.0), reads=[bPT[r]], writes=[bPT[r]])
                    first = (i == 0)
                    S.group("pe", [lambda: nc.tensor.matmul(pb[2][:, c0:T], Vp[0:nkk, e, kt, :], PT[r][0:nkk, c0:T], start=first, stop=False),
                                   lambda: nc.tensor.matmul(pb[3][:, c0:T], onesP[0:nkk, e, :], PT[r][0:nkk, c0:T], start=first, stop=False)],
                            reads=[bVp, bPT[r], b_const], writes=[bp[2], bp[3]])
                S.group("pe", [lambda: nc.tensor.matmul(pb[2][:, 0:T], zerosb[0:nkk, :], PT[r][0:nkk, 0:T], start=False, stop=True),
                               lambda: nc.tensor.matmul(pb[3][:, 0:T], zerosb[0:nkk, :], PT[r][0:nkk, 0:T], start=False, stop=True)],
                        reads=[bPT[r], b_const], writes=[bp[2], bp[3]])
                S.op("act", lambda: nc.scalar.activation(rec[:, 0:T], pb[3][:, 0:T], AF.Ln), reads=[bp[3]], writes=[brec])
                S.op("act", lambda: nc.scalar.activation(rec[:, 0:T], rec[:, 0:T], AF.Exp, scale=-1.0), reads=[brec], writes=[brec])
                if prefetch is not None:
                    prefetch()
                S.op("dve", lambda: nc.vector.tensor_tensor(attT[:, hp, qs], pb[2][:, 0:T], rec[:, 0:T], ALU.mult), reads=[bp[2], brec], writes=[battT[hp]])

            def wout_attn(qs, T):
                for o in range(8):
                    pa, ba = pb[4 + o % 2], bp[4 + o % 2]
                    S.group("pe", [(lambda hp=hp: nc.tensor.matmul(pa[:, 0:T], wout[:, hp, o * 128:(o + 1) * 128], attT[:, hp, qs],
                                                                   start=(hp == 0), stop=(hp == 3))) for hp in range(4)],
                            reads=[bwout] + battT, writes=[ba])
                    add_x(pa, ba, o, qs, T)

            make_cqn()
            if kind == "p":
                pend = None
                for i in range(8):
                    g0 = hf * 1024 + i * 128
                    nxt = kv_unit(i * 128, 128, g0, o_lat_p[s, g0:g0 + 128, :], o_kr_p[s, g0:g0 + 128, :], g0)
                    if pend is not None:
                        pend()
                    pend = nxt
                pend()
                nk = (hf + 1) * 1024
                blocks = [(hp, qi) for hp in range(4) for qi in range(2)]
                build_qt(0, slice(0, 512), 512, 0, 0)
                for bi, (hp, qi) in enumerate(blocks):
                    if qi == 0:
                        build_kv(hp, nk)
                    I = 2 * hf + qi
                    kts = [(kt, 128, 0, False) for kt in range(4 * I)] + [(4 * I + j, 128, j * 128, True) for j in range(4)]
                    pf = None
                    if bi + 1 < len(blocks):
                        nhp, nqi = blocks[bi + 1]
                        pf = (lambda nhp=nhp, nqi=nqi, nb_=(bi + 1) % 2: build_qt(nhp, slice(nqi * 512, (nqi + 1) * 512), 512, nqi * 512, nb_))
                    attend(hp, slice(qi * 512, (qi + 1) * 512), 512, kts, bi % 2, pf)
                for qi in range(2):
                    wout_attn(slice(qi * 512, (qi + 1) * 512), 512)
            else:
                clat = [sb(c, "clat%d" % i, [128, 256], BF16) for i in range(2)]; bclat = [Buf(), Buf()]
                for q in range(4):
                    for i in range(8):
                        a = i % 2
                        S.dma("pool", clat[a][:], latc[q, i * 128:(i + 1) * 128, :], writes=[bclat[a]])
                        S.dma("pool", krb[a][:, 64:96], krc[q, i * 128:(i + 1) * 128, :], writes=[bkrb[a]])
                        keys_from_tok(clat[a], bclat[a], krb[a], bkrb[a], 128, i * 128)
                    kv_unit(q * 64, 64, 2048, o_lat_s[q], o_kr_s[q], 1024)()
                    kts = [(kt, 128, 0, False) for kt in range(8)] + [(8, 64, 0, False)]
                    for hp in range(4):
                        build_kv(hp, 1088)
                        build_qt(hp, slice(q * 64, q * 64 + 64), 64, 0, hp % 2)
                        attend(hp, slice(q * 64, q * 64 + 64), 64, kts, hp % 2)
                    wout_attn(slice(q * 64, q * 64 + 64), 64)
            conv_path()
            if kind == "p" and hf == 0:
                return
            wC = sb(c, "wC", [128, 8, 512], BF16); bwC = Buf()
            wH = sb(c, "wH", [128, 8, 512], BF16); bwH = Buf()
            S.dma("pool", wC[:], win[:, :, 1056:1568], writes=[bwC])
            S.dma("pool", wH[:], win[:, :, 1568:2080], writes=[bwH])
            cu = sb(c, "cu", [2, 512], F32); bcu = Buf()
            cv = sb(c, "cv", [2, 512], F32); bcv = Buf()
            lasts = [(1022, o_conv_p[s])] if kind == "p" else [(q * 64 + 62, o_conv_s[q]) for q in range(4)]
            for tok0, dconv in lasts:
                tb = bh[tok0 // 512]
                S.group("pe", [(lambda k=k: nc.tensor.matmul(pb[0][0:2, :], hT[:, k, tok0:tok0 + 2], wC[:, k, :],
                                                             start=(k == 0), stop=(k == 7))) for k in range(8)],
                        reads=[tb, bwC], writes=[bp[0]])
                S.group("pe", [(lambda k=k: nc.tensor.matmul(pb[1][0:2, :], hT[:, k, tok0:tok0 + 2], wH[:, k, :],
                                                             start=(k == 0), stop=(k == 7))) for k in range(8)],
                        reads=[tb, bwH], writes=[bp[1]])
                S.op("act", lambda: nc.scalar.copy(cv[:, :], pb[1][0:2, :]), reads=[bp[1]], writes=[bcv])
                S.op("dve", lambda: nc.vector.tensor_tensor(cu[:, :], pb[0][0:2, :], cv[:, :], ALU.mult), reads=[bp[0], bcv], writes=[bcu])
                S.dma("sp", dconv, cu[:, :], reads=[bcu])

        s5w = nc.dram_tensor("s5w", [8, 128, 5248], BF16, kind="Internal").ap()
        bs5w = Buf("s5w")
        s5c = sb(ctx, "s5c", [128, 7, 2, 32], F32); bs5c = Buf("s5c")
        pw8 = sb(ctx, "pw8", [128, 2, 32, 8], F32)

        def s5_setup(c):
            PI = 3.141592653589793
            bS = Buf("s5pre")
            nat = lambda name: sb(c, name, [128, 32], F32)

            def dv(fn, extra_r=(), extra_w=()):
                S.op("dve", fn, reads=[bS] + list(extra_r), writes=[bS] + list(extra_w))

            def ac(fn):
                S.op("act", fn, reads=[bS], writes=[bS])
            are, aim, ldt = nat("are"), nat("aim"), nat("ldt")
            for g2 in range(2):
                rows = slice(g2 * 64, (g2 + 1) * 64)
                S.dma("sp", are[rows, :], W['ssm_a_re'][0].rearrange("(j g) p -> g p j", g=2)[g2], writes=[bS], allow_slow_non_contiguous=True)
                S.dma("sp", aim[rows, :], W['ssm_a_im'][0].rearrange("(j g) p -> g p j", g=2)[g2], writes=[bS], allow_slow_non_contiguous=True)
                S.dma("sp", ldt[rows, :], W['ssm_log_dt'][0].rearrange("(j g) -> g j", g=2)[g2].partition_broadcast(64), writes=[bS],
                      allow_slow_non_contiguous=True)
            t1, t2, t3, t4 = nat("t1"), nat("t2"), nat("t3"), nat("t4")
            zr, zi, mag = nat("zr"), nat("zi"), nat("mag")
            ac(lambda: nc.scalar.activation(ldt[:], ldt[:], AF.Exp))
            dv(lambda: nc.vector.tensor_tensor(zr[:], are[:], ldt[:], ALU.mult))
            dv(lambda: nc.vector.tensor_tensor(zi[:], aim[:], ldt[:], ALU.mult))
            pw = sb(c, "pw", [128, 9, 2, 32], F32)
            dp = s5c
            cr_, ci_ = nat("cr"), nat("ci")
            halfpi = sb(c, "halfpi", [128, 1], F32)
            S.op("pool", lambda: nc.gpsimd.memset(halfpi[:], PI / 2), writes=[bS])
            ac(lambda: nc.scalar.activation(mag[:], zr[:], AF.Exp, scale=1.0 / 16.0))
            ac(lambda: nc.scalar.activation(cr_[:], zi[:], AF.Sin, bias=halfpi[:, 0:1], scale=-1.0 / 16.0))
            ac(lambda: nc.scalar.activation(ci_[:], zi[:], AF.Sin, scale=1.0 / 16.0))
            dv(lambda: nc.vector.tensor_tensor(cr_[:], cr_[:], mag[:], ALU.mult))
            dv(lambda: nc.vector.tensor_tensor(ci_[:], ci_[:], mag[:], ALU.mult))

            def cmul(or_, oi_, ar, ai, br, bi):
                dv(lambda: nc.vector.tensor_tensor(t1[:], ar, br, ALU.mult))
                dv(lambda: nc.vector.tensor_tensor(t2[:], ai, bi, ALU.mult))
                dv(lambda: nc.vector.tensor_tensor(t3[:], ar, bi, ALU.mult))
                dv(lambda: nc.vector.tensor_tensor(t4[:], ai, br, ALU.mult))
                dv(lambda: nc.vector.tensor_tensor(or_, t1[:], t2[:], ALU.subtract))
                dv(lambda: nc.vector.tensor_tensor(oi_, t3[:], t4[:], ALU.add))
            for _ in range(3):
                cmul(cr_[:], ci_[:], cr_[:], ci_[:], cr_[:], ci_[:])
            cmul(pw[:, 1, 0, :], pw[:, 1, 1, :], cr_[:], ci_[:], cr_[:], ci_[:])
            S.op("pool", lambda: nc.gpsimd.memset(pw[:, 0, 0, :], 1.0), reads=[bS], writes=[bS])
            S.op("pool", lambda: nc.gpsimd.memset(pw[:, 0, 1, :], 0.0), reads=[bS], writes=[bS])
            for k in range(2, 9):
                cmul(pw[:, k, 0, :], pw[:, k, 1, :], pw[:, k - 1, 0, :], pw[:, k - 1, 1, :], pw[:, 1, 0, :], pw[:, 1, 1, :])
            dv(lambda: nc.vector.tensor_copy(dp[:, 0, :, :], pw[:, 8, :, :]), extra_w=[bs5c])
            dv(lambda: nc.vector.tensor_copy(dp[:, 6, :, :], pw[:, 8, :, :]), extra_w=[bs5c])
            for l in range(1, 6):
                cmul(dp[:, l, 0, :], dp[:, l, 1, :], dp[:, l - 1, 0, :], dp[:, l - 1, 1, :], dp[:, l - 1, 0, :], dp[:, l - 1, 1, :])
            dv(lambda: nc.vector.tensor_copy(pw8[:, :, :, 0], pw[:, 8, :, :]), extra_w=[bs5c])
            for b_ in range(1, 8):
                cmul(pw8[:, 0, :, b_], pw8[:, 1, :, b_], pw8[:, 0, :, b_ - 1], pw8[:, 1, :, b_ - 1], pw[:, 8, 0, :], pw[:, 8, 1, :])
            dv(lambda: nc.vector.tensor_copy(t1[:], t1[:]), extra_w=[bs5c])
            nr, m2 = nat("nr"), nat("m2")
            dv(lambda: nc.vector.tensor_scalar(nr[:], pw[:, 1, 0, :], -1.0, None, ALU.add))
            dv(lambda: nc.vector.tensor_tensor(t1[:], are[:], are[:], ALU.mult))
            dv(lambda: nc.vector.tensor_tensor(t2[:], aim[:], aim[:], ALU.mult))
            dv(lambda: nc.vector.tensor_tensor(m2[:], t1[:], t2[:], ALU.add))
            dv(lambda: nc.vector.reciprocal(m2[:], m2[:]))
            dv(lambda: nc.vector.tensor_tensor(t1[:], nr[:], are[:], ALU.mult))
            dv(lambda: nc.vector.tensor_tensor(t2[:], pw[:, 1, 1, :], aim[:], ALU.mult))
            dv(lambda: nc.vector.tensor_tensor(t1[:], t1[:], t2[:], ALU.add))
            dv(lambda: nc.vector.tensor_tensor(cr_[:], t1[:], m2[:], ALU.mult))
            dv(lambda: nc.vector.tensor_tensor(t1[:], pw[:, 1, 1, :], are[:], ALU.mult))
            dv(lambda: nc.vector.tensor_tensor(t2[:], nr[:], aim[:], ALU.mult))
            dv(lambda: nc.vector.tensor_tensor(t1[:], t1[:], t2[:], ALU.subtract))
            dv(lambda: nc.vector.tensor_tensor(ci_[:], t1[:], m2[:], ALU.mult))
            bbd = sb(c, "bbd", [128, 2, 32, 32], F32)
            CCn = sb(c, "CCn", [128, 2, 32, 32], F32)
            bbz = sb(c, "bbz", [128, 32, 2, 128], BF16)
            with ExitStack() as c2:
                Bn = sb(c2, "Bn", [128, 2, 32, 16], F32)
                for g2 in range(2):
                    rows = slice(g2 * 64, (g2 + 1) * 64)
                    S.dma("sp", Bn[rows, 0, :, :], W['ssm_b_re'][0].rearrange("(j g) p c -> g p j c", g=2)[g2], writes=[bS])
                    S.dma("sp", Bn[rows, 1, :, :], W['ssm_b_im'][0].rearrange("(j g) p c -> g p j c", g=2)[g2], writes=[bS])
                tb1 = sb(c2, "tb1", [128, 32, 16], F32); tb2 = sb(c2, "tb2", [128, 32, 16], F32)
                S.op("pool", lambda: nc.gpsimd.memset(bbd[:], 0.0), reads=[bS], writes=[bS])
                crb = cr_[:, :].unsqueeze(2).broadcast_to([128, 32, 16])
                cib = ci_[:, :].unsqueeze(2).broadcast_to([128, 32, 16])
                for ri in range(2):
                    dv(lambda: nc.vector.tensor_tensor(tb1[:], Bn[:, ri, :, :], crb, ALU.mult))
                    dv(lambda: nc.vector.tensor_tensor(tb2[:], Bn[:, 1 - ri, :, :], cib, ALU.mult))
                    dv(lambda: nc.vector.tensor_tensor(tb1[:], tb1[:], tb2[:], ALU.subtract if ri == 0 else ALU.add))
                    dv(lambda: nc.vector.tensor_copy(bbd[0:64, ri, :, 0:16], tb1[0:64, :, :]))
                    dv(lambda: nc.vector.tensor_copy(bbd[64:128, ri, :, 16:32], tb1[64:128, :, :]))
                Sc = sb(c2, "Sc", [32, 2, 32, 128], F32)
                S.op("pool", lambda: nc.gpsimd.memset(Sc[:], 0.0), reads=[bS], writes=[bS])
                for g2 in range(2):
                    S.dma("sp", Sc[g2 * 16:(g2 + 1) * 16, 0, :, g2 * 64:(g2 + 1) * 64], W['ssm_c_re'][0].rearrange("(j g) c p -> g c j p", g=2)[g2], writes=[bS])
                    S.dma("sp", Sc[g2 * 16:(g2 + 1) * 16, 1, :, g2 * 64:(g2 + 1) * 64], W['ssm_c_im'][0].rearrange("(j g) c p -> g c j p", g=2)[g2], writes=[bS])
                for ri in range(2):
                    for jb in range(2):
                        S.group("pe", [(lambda jj=jj: nc.tensor.transpose(pb[jb][:, jj * 32:(jj + 1) * 32], Sc[:, ri, jb * 16 + jj, :], ident_f[0:32, 0:32]))
                                       for jj in range(16)], reads=[bS, b_const], writes=[bp[jb]])
                        S.op("act", lambda: nc.scalar.copy(CCn[:, ri, jb * 16:(jb + 1) * 16, :], pb[jb][:, :].rearrange("p (j m) -> p j m", m=32)),
                             reads=[bp[jb]], writes=[bS])
                S.barrier()
            S.op("pool", lambda: nc.gpsimd.memset(bbz[:], 0.0), reads=[bS], writes=[bS])
            bbz5 = bbz[:, :, :, :].rearrange("p j r (q m) -> p j r q m", q=4)
            for ri in range(2):
                for slot in range(4):
                    dv(lambda: nc.vector.tensor_copy(bbz5[:, slot:32:4, ri, slot, :], bbd[:, ri, slot:32:4, :]))
            Wr = sb(c, "Wr", [128, 32, 32], F32); Wi = sb(c, "Wi", [128, 32, 32], F32); Wt = sb(c, "Wt", [128, 32, 32], F32)
            Wrb = sb(c, "Wrb", [128, 32, 32], BF16); Wib = sb(c, "Wib", [128, 32, 32], BF16)
            stg = [sb(c, "stg%d" % i, [128, 8, 256], BF16) for i in range(2)]; bstg = [Buf(), Buf()]
            s5v = s5w.rearrange("c p x -> p c x")
            nst = [0]

            def bc32(ap2):
                return ap2.unsqueeze(2).broadcast_to([128, 32, 32])

            def cplx(xr, xi, k, neg_im):
                pr, pi_ = bc32(pw[:, k, 0, :]), bc32(pw[:, k, 1, :])
                dv(lambda: nc.vector.tensor_tensor(Wr[:], xr, pr, ALU.mult))
                dv(lambda: nc.vector.tensor_tensor(Wt[:], xi, pi_, ALU.mult))
                dv(lambda: nc.vector.tensor_tensor(Wr[:], Wr[:], Wt[:], ALU.subtract))
                dv(lambda: nc.vector.tensor_tensor(Wi[:], xr, pi_, ALU.mult))
                dv(lambda: nc.vector.tensor_tensor(Wt[:], xi, pr, ALU.mult))
                dv(lambda: nc.vector.tensor_tensor(Wi[:], Wi[:], Wt[:], ALU.add))
                if neg_im:
                    dv(lambda: nc.vector.tensor_scalar(Wi[:], Wi[:], -1.0, None, ALU.mult))
            for s_ in range(8):
                cplx(bbd[:, 0, :, :], bbd[:, 1, :, :], 7 - s_, False)
                a = nst[0] % 2
                nst[0] += 1
                for ri, Wx in ((0, Wr), (1, Wi)):
                    for half in range(2):
                        S.group("pe", [(lambda q=q: nc.tensor.transpose(
                            pb[half][:, q * 128:(q + 1) * 128],
                            Wx[:, 4 * (half * 4 + q):4 * (half * 4 + q) + 4, :].rearrange("p j c -> p (j c)"), ident_f[:]))
                            for q in range(4)], reads=[bS, b_const], writes=[bp[half]])
                        S.op("act", lambda: nc.scalar.copy(stg[a][:, half * 4:(half + 1) * 4, ri * 128:(ri + 1) * 128],
                                                           pb[half][:, :].rearrange("p (q m) -> p q m", q=4)),
                             reads=[bp[half]], writes=[bstg[a], bS])
                S.dma("sp", s5v[:, :, s_ * 256:(s_ + 1) * 256], stg[a][:, :, :], reads=[bstg[a]], writes=[bs5w])
            for k in range(9):
                cplx(CCn[:, 0, :, :], CCn[:, 1, :, :], k, True)
                if k >= 1:
                    e = k - 1
                    a = nst[0] % 2
                    nst[0] += 1
                    stv = stg[a][:, :, 0:256].rearrange("p c (j r m) -> p c j r m", j=4, r=2)
                    for ri, Wx in ((0, Wr), (1, Wi)):
                        S.op("act", lambda: nc.scalar.copy(stv[:, :, :, ri, :], Wx[:, :, :].rearrange("p (c j) m -> p c j m", j=4)),
                             reads=[bS], writes=[bstg[a]])
                    for j in range(4):
                        S.dma("sp", s5v[:, :, 2048 + j * 512 + e * 64:2048 + j * 512 + (e + 1) * 64], stg[a][:, :, j * 64:(j + 1) * 64],
                              reads=[bstg[a]], writes=[bs5w])
                if k <= 7:
                    a = nst[0] % 2
                    nst[0] += 1
                    dv(lambda: nc.vector.tensor_copy(Wrb[:], Wr[:]))
                    dv(lambda: nc.vector.tensor_copy(Wib[:], Wi[:]))
                    for cp in range(2):
                        fns = []
                        for q in range(4):
                            ch = cp * 4 + q
                            for j in range(4):
                                for ri, Wx in ((0, Wrb), (1, Wib)):
                                    fns.append(lambda q=q, ch=ch, j=j, ri=ri, Wx=Wx: nc.tensor.matmul(
                                        pb[2 + cp][:, q * 128 + j * 32:q * 128 + (j + 1) * 32], bbz[:, 4 * ch + j, ri, :], Wx[:, 4 * ch + j, :],
                                        start=(ri == 0), stop=(ri == 1)))
                        S.group("pe", fns, reads=[bS], writes=[bp[2 + cp]])
                        S.op("act", lambda: nc.scalar.copy(stg[a][:, cp * 4:(cp + 1) * 4, 0:128], pb[2 + cp][:, :].rearrange("p (q m) -> p q m", q=4)),
                             reads=[bp[2 + cp]], writes=[bstg[a]])
                    S.dma("sp", s5v[:, :, 4096 + k * 128:4096 + (k + 1) * 128], stg[a][:, :, 0:128], reads=[bstg[a]], writes=[bs5w])
            dsk = sb(c, "dsk", [128, 8], F32)
            S.dma("sp", dsk[:], W['ssm_d'][0].rearrange("(k p) -> p k", p=128), writes=[bS], allow_slow_non_contiguous=True)
            a = nst[0] % 2
            for ch in range(8):
                S.op("dve", lambda ch=ch: nc.vector.tensor_scalar(stg[a][:, ch, 0:128], ident_f[:], dsk[:, ch:ch + 1], None, ALU.mult),
                     reads=[bS, b_const], writes=[bstg[a]])
            S.dma("sp", s5v[:, :, 5120:5248], stg[a][:, :, 0:128], reads=[bstg[a]], writes=[bs5w])

        def odd_mixer(kind, s, hf, tiles, c):
            norm_to_h(G_MIX + 1, tiles)
            nseq = 1 if kind == "p" else 4
            if kind == "p":
                if hf == 0:
                    S.op("pool", lambda: nc.gpsimd.memset(hstate[:], 0.0), writes=[bhst])
            else:
                for q in range(4):
                    for g2 in range(2):
                        rows = slice(g2 * 64, (g2 + 1) * 64)
                        S.dma("sp", hstate[rows, 0, :, q], ssre[q].rearrange("(j g) p -> g p j", g=2)[g2], writes=[bhst], allow_slow_non_contiguous=True)
                        S.dma("sp", hstate[rows, 1, :, q], ssim[q].rearrange("(j g) p -> g p j", g=2)[g2], writes=[bhst], allow_slow_non_contiguous=True)
            uT = sb(c, "uT", [128, 8, 1024], BF16); buT = Buf()
            with ExitStack() as c1:
                wio = sb(c1, "wio", [128, 8, 1024], BF16); bwio = Buf()
                S.dma("pool", wio[:], W['w_in_odd'][0].rearrange("(k p) n -> p k n", p=128), writes=[bwio])
                for t, T in tiles:
                    sl = slice(t * 512, t * 512 + T)
                    for m in range(8):
                        pu, bu_ = pb[m % 2], bp[m % 2]
                        S.group("pe", [(lambda k=k: nc.tensor.matmul(pu[:, 0:T], wio[:, k, m * 128:(m + 1) * 128], hT[:, k, sl], start=(k == 0), stop=(k == 7)))
                                       for k in range(8)], reads=[bwio, bh[t]], writes=[bu_])
                        if m % 2 == 0:
                            S.op("act", lambda: nc.scalar.copy(uT[:, m, sl], pu[:, 0:T]), reads=[bu_], writes=[buT])
                        else:
                            S.op("dve", lambda: nc.vector.tensor_copy(uT[:, m, sl], pu[:, 0:T]), reads=[bu_], writes=[buT])
                S.barrier()
            EA = sb(c, "EA", [128, 2, 32, 72], F32); EB = sb(c, "EB", [128, 2, 32, 72], F32)
            GA = sb(c, "GA", [128, 2, 32, 9], F32); GB = sb(c, "GB", [128, 2, 32, 9], F32)
            bE = {id(EA): [Buf(), Buf()], id(EB): [Buf(), Buf()], id(GA): [Buf(), Buf()], id(GB): [Buf(), Buf()]}
            tD = sb(c, "tD", [128, 32, 64], F32); btD = Buf()
            tP = sb(c, "tP", [128, 32, 64], F32); btP = Buf()
            tP2 = sb(c, "tP2", [128, 32, 64], F32); btP2 = Buf()
            Hpb = sb(c, "Hpb", [128, 2, 32, 64], BF16); bHp = Buf()
            wch = [sb(c, "wch%d" % i, [128, 5248], BF16) for i in range(2)]; bwch = [Buf(), Buf()]
            y2 = sb(c, "y2", [128, 512], F32); by2 = Buf()
            sgm = sb(c, "sgm", [128, 512], F32); bsg = Buf()
            wgl = [sb(c, "wgl%d" % i, [128, 8, 2, 128], BF16) for i in range(2)]; bwgl = [Buf(), Buf()]
            wglu_d = W['w_glu'][0].rearrange("(k p) n -> p k n", p=128)
            pending_glu = []
            def glu_all(tl):
                cnt_ = 0
                for oc in range(8):
                    wg, bwg = wgl[oc % 2], bwgl[oc % 2]
                    S.dma("pool", wg[:, :, 0, :], wglu_d[:, :, oc * 128:(oc + 1) * 128], writes=[bwg])
                    S.dma("pool", wg[:, :, 1, :], wglu_d[:, :, 1024 + oc * 128:1024 + (oc + 1) * 128], writes=[bwg], join=True)
                    for (t, T, sl) in tl:
                        pv_, bv_ = pb[2 * (cnt_ % 2)], bp[2 * (cnt_ % 2)]
                        pg_, bg_ = pb[2 * (cnt_ % 2) + 1], bp[2 * (cnt_ % 2) + 1]
                        cnt_ += 1
                        S.group("pe", [(lambda k=k: nc.tensor.matmul(pv_[:, 0:T], wg[:, k, 0, :], hT[:, k, sl], start=(k == 0), stop=(k == 7))) for k in range(8)],
                                reads=[bwg, bh[t]], writes=[bv_])
                        S.group("pe", [(lambda k=k: nc.tensor.matmul(pg_[:, 0:T], wg[:, k, 1, :], hT[:, k, sl], start=(k == 0), stop=(k == 7))) for k in range(8)],
                                reads=[bwg, bh[t]], writes=[bg_])
                        S.op("act", lambda: nc.scalar.activation(sgm[:, 0:T], pg_[:, 0:T], AF.Sigmoid), reads=[bg_], writes=[bsg])
                        S.op("dve", lambda: nc.vector.tensor_tensor(y2[:, 0:T], pv_[:, 0:T], sgm[:, 0:T], ALU.mult), reads=[bv_, bsg], writes=[by2])
                        S.op("dve", lambda: nc.vector.tensor_tensor(xT[:, oc, sl], xT[:, oc, sl], y2[:, 0:T], ALU.add), reads=[by2, bx[t]], writes=[bx[t]])
            wc = 0
            for t, T in tiles:
                sl = slice(t * 512, t * 512 + T)
                nb = (T // nseq) // 8
                NB = nseq * nb
                two_level = (nb == 64)
                QG = 8 if two_level else nseq
                W1 = QG * 9

                def vq(Et, ri, q_=None, w_=None):
                    q_ = QG if q_ is None else q_
                    return Et[:, ri, :, 0:q_ * 9].rearrange("p j (q b) -> p j q b", q=q_)
                v5 = vq
                nb = 8
                nseq_ = QG
                cur, oth = EA, EB
                for ch in range(8):
                    w, bw = wch[wc % 2], bwch[wc % 2]
                    wc += 1
                    S.dma("sp", w[:, 0:2048], s5w[ch][:, 0:2048], reads=[bs5w], writes=[bw])
                    WE = w[:, 0:2048].rearrange("p (s r m) -> p s r m", s=8, r=2)
                    fns = []
                    for ri in range(2):
                        for s_ in range(8):
                            for j in range(4):
                                u3 = uT[32 * j:32 * j + 32, ch, sl].rearrange("p (x e) -> p x e", e=8)
                                fns.append(lambda j=j, ri=ri, s_=s_, u3=u3: nc.tensor.matmul(
                                    pb[j][:, ri * NB:(ri + 1) * NB], WE[32 * j:32 * j + 32, s_, ri, :], u3[:, :, s_],
                                    start=(s_ == 0), stop=(s_ == 7), tile_position=(32 * j, 0)))
                    S.group("pe", fns, reads=[bw, buT], writes=[bp[0], bp[1], bp[2], bp[3]])
                    for j in range(4):
                        pE3 = pb[j][:, 0:2 * NB].rearrange("p (r q b) -> p r q b", r=2, q=QG)
                        dst = cur[:, :, 4 * ch + j, 0:W1].rearrange("p r (q b) -> p r q b", q=QG)[:, :, :, 1:9]
                        S.op("act", lambda: nc.scalar.copy(dst, pE3), reads=[bp[j]], writes=[bE[id(cur)][0], bE[id(cur)][1]])
                if os.environ.get("S5CUT") == "A":
                    return
                def first_fix(Et, q_, lev):
                    shp1 = [128, 32, q_]
                    cr_b = s5c[:, lev, 0, :].unsqueeze(2).broadcast_to(shp1)
                    ci_b = s5c[:, lev, 1, :].unsqueeze(2).broadcast_to(shp1)
                    tq = tD[:, :, 0:q_]
                    h0r, h0i = vq(Et, 0, q_)[:, :, :, 0], vq(Et, 1, q_)[:, :, :, 0]
                    e0r, e0i = vq(Et, 0, q_)[:, :, :, 1], vq(Et, 1, q_)[:, :, :, 1]
                    bcr, bci = bE[id(Et)]
                    for (dst, bd, src_, co, op) in ((e0r, bcr, h0r, cr_b, ALU.add), (e0r, bcr, h0i, ci_b, ALU.subtract),
                                                    (e0i, bci, h0i, cr_b, ALU.add), (e0i, bci, h0r, ci_b, ALU.add)):
                        S.op("dve", lambda: nc.vector.tensor_tensor(tq, src_, co, ALU.mult), reads=[bcr, bci, bs5c], writes=[btD])
                        S.op("dve", lambda: nc.vector.tensor_tensor(dst, dst, tq, op), reads=[btD], writes=[bd])

                def scan_levels(cur, oth, q_, lev0):
                    for l in range(3):
                        d = 1 << l
                        shp = [128, 32, q_, 8 - d]
                        bq2 = lambda ap2: ap2.unsqueeze(2).unsqueeze(3).broadcast_to(shp)
                        dr, di = bq2(s5c[:, lev0 + l, 0, :]), bq2(s5c[:, lev0 + l, 1, :])
                        lo = lambda Et, ri: vq(Et, ri, q_)[:, :, :, 1:9 - d]
                        hi = lambda Et, ri: vq(Et, ri, q_)[:, :, :, 1 + d:9]
                        tv = lambda Tt: Tt[:, :, 0:q_ * (8 - d)].rearrange("p j (q b) -> p j q b", q=q_)
                        tdv, tpv, tpv2 = tv(tD), tv(tP), tv(tP2)
                        bcr, bci = bE[id(cur)]
                        bor, boi = bE[id(oth)]
                        for ri in range(2):
                            S.op("act", lambda ri=ri: nc.scalar.copy(vq(oth, ri, q_)[:, :, :, 0:1 + d], vq(cur, ri, q_)[:, :, :, 0:1 + d]),
                                 reads=[bE[id(cur)][ri]], writes=[bE[id(oth)][ri]])
                        S.op("dve", lambda: nc.vector.tensor_tensor(tpv, lo(cur, 1), dr, ALU.mult), reads=[bci, bs5c], writes=[btP])
                        S.op("pool", lambda: nc.gpsimd.tensor_tensor(hi(oth, 1), hi(cur, 1), tpv, ALU.add), reads=[bci, btP], writes=[boi])
                        S.op("dve", lambda: nc.vector.tensor_tensor(tpv2, lo(cur, 0), di, ALU.mult), reads=[bcr, bs5c], writes=[btP2])
                        S.op("pool", lambda: nc.gpsimd.tensor_tensor(hi(oth, 1), hi(oth, 1), tpv2, ALU.add), reads=[btP2], writes=[boi])
                        S.op("dve", lambda: nc.vector.tensor_tensor(tdv, lo(cur, 0), dr, ALU.mult), reads=[bcr, bs5c], writes=[btD])
                        S.op("dve", lambda: nc.vector.tensor_tensor(hi(oth, 0), hi(cur, 0), tdv, ALU.add), reads=[bcr, btD], writes=[bor])
                        S.op("dve", lambda: nc.vector.tensor_tensor(tdv, lo(cur, 1), di, ALU.mult), reads=[bci, bs5c], writes=[btD])
                        S.op("dve", lambda: nc.vector.tensor_tensor(hi(oth, 0), hi(oth, 0), tdv, ALU.subtract), reads=[btD], writes=[bor])
                        cur, oth = oth, cur
                    return cur, oth
                if not two_level:
                    for ri in range(2):
                        S.op("act", lambda ri=ri: nc.scalar.copy(vq(cur, ri)[:, :, :, 0], hstate[:, ri, :, 0:nseq]), reads=[bhst], writes=[bE[id(cur)][ri]])
                    first_fix(cur, QG, 6)
                    cur, oth = scan_levels(cur, oth, QG, 0)
                    for ri in range(2):
                        S.op("act", lambda ri=ri: nc.scalar.copy(hstate[:, ri, :, 0:nseq], vq(cur, ri)[:, :, :, 8]), reads=[bE[id(cur)][ri]], writes=[bhst])
                else:
                    for ri in range(2):
                        S.op("pool", lambda ri=ri: nc.gpsimd.memset(vq(cur, ri)[:, :, :, 0], 0.0), reads=[bE[id(cur)][ri]], writes=[bE[id(cur)][ri]])
                    cur, oth = scan_levels(cur, oth, 8, 0)
                    gc, go = GA, GB
                    for ri in range(2):
                        S.op("act", lambda ri=ri: nc.scalar.copy(gc[:, ri, :, 0], hstate[:, ri, :, 0]), reads=[bhst], writes=[bE[id(gc)][ri]])
                        S.op("act", lambda ri=ri: nc.scalar.copy(gc[:, ri, :, 1:9], vq(cur, ri)[:, :, :, 8]), reads=[bE[id(cur)][ri]], writes=[bE[id(gc)][ri]])
                    first_fix(gc, 1, 3)
                    gc, go = scan_levels(gc, go, 1, 3)
                    for ri in range(2):
                        S.op("act", lambda ri=ri: nc.scalar.copy(vq(cur, ri)[:, :, :, 0], gc[:, ri, :, 0:8]), reads=[bE[id(gc)][ri]], writes=[bE[id(cur)][ri]])
                        S.op("act", lambda ri=ri: nc.scalar.copy(hstate[:, ri, :, 0], gc[:, ri, :, 8]), reads=[bE[id(gc)][ri]], writes=[bhst])
                    shp = [128, 32, 8, 8]
                    pwr = pw8[:, 0, :, :].unsqueeze(2).broadcast_to(shp)
                    pwi = pw8[:, 1, :, :].unsqueeze(2).broadcast_to(shp)
                    gpr = gc[:, 0, :, 0:8].unsqueeze(3).broadcast_to(shp)
                    gpi = gc[:, 1, :, 0:8].unsqueeze(3).broadcast_to(shp)
                    Xr, Xi = vq(cur, 0)[:, :, :, 1:9], vq(cur, 1)[:, :, :, 1:9]
                    tv = lambda Tt: Tt[:, :, 0:64].rearrange("p j (q b) -> p j q b", q=8)
                    tdv, tpv, tpv2 = tv(tD), tv(tP), tv(tP2)
                    bcr, bci = bE[id(cur)]
                    bgr, bgi = bE[id(gc)]
                    S.op("dve", lambda: nc.vector.tensor_tensor(tpv, gpi, pwr, ALU.mult), reads=[bgi, bs5c], writes=[btP])
                    S.op("pool", lambda: nc.gpsimd.tensor_tensor(Xi, Xi, tpv, ALU.add), reads=[btP], writes=[bci])
                    S.op("dve", lambda: nc.vector.tensor_tensor(tpv2, gpr, pwi, ALU.mult), reads=[bgr, bs5c], writes=[btP2])
                    S.op("pool", lambda: nc.gpsimd.tensor_tensor(Xi, Xi, tpv2, ALU.add), reads=[btP2], writes=[bci])
                    S.op("dve", lambda: nc.vector.tensor_tensor(tdv, gpr, pwr, ALU.mult), reads=[bgr, bs5c], writes=[btD])
                    S.op("dve", lambda: nc.vector.tensor_tensor(Xr, Xr, tdv, ALU.add), reads=[btD], writes=[bcr])
                    S.op("dve", lambda: nc.vector.tensor_tensor(tdv, gpi, pwi, ALU.mult), reads=[bgi, bs5c], writes=[btD])
                    S.op("dve", lambda: nc.vector.tensor_tensor(Xr, Xr, tdv, ALU.subtract), reads=[btD], writes=[bcr])
                if os.environ.get("S5CUT") == "S":
                    return
                for ri in range(2):
                    S.op("act", lambda ri=ri: nc.scalar.copy(Hpb[:, ri, :, 0:NB].rearrange("p j (q b) -> p j q b", q=QG), vq(cur, ri)[:, :, :, 0:8]),
                         reads=[bE[id(cur)][ri]], writes=[bHp])
                for ch in range(8):
                    w, bw = wch[wc % 2], bwch[wc % 2]
                    wc += 1
                    S.dma("sp", w[:, 2048:5248], s5w[ch][:, 2048:5248], reads=[bs5w], writes=[bw])
                    CA = w[:, 2048:4096].rearrange("p (j e r m) -> p j e r m", j=4, e=8, r=2)
                    KI = w[:, 4096:5120].rearrange("p (t m) -> p t m", t=8)
                    DG = w[:, 5120:5248]
                    py, bpy = pb[4 + ch % 2], bp[4 + ch % 2]
                    u3 = uT[:, ch, sl].rearrange("p (x e) -> p x e", e=8)
                    fns = [lambda: nc.tensor.matmul(py[:, 0:T], DG, uT[:, ch, sl].rearrange("p (x e) -> p e x", e=8), start=True, stop=False)]
                    for e in range(8):
                        for s_ in range(e + 1):
                            fns.append(lambda e=e, s_=s_: nc.tensor.matmul(py[:, e * NB:(e + 1) * NB], KI[:, e - s_, :], u3[:, :, s_], start=False, stop=False))
                    S.group("pe", fns, reads=[bw, buT], writes=[bpy])
                    fns = []
                    for e in range(8):
                        for j in range(4):
                            for ri in range(2):
                                last = (ri == 1)
                                fns.append(lambda e=e, j=j, ri=ri, last=last: nc.tensor.matmul(
                                    py[32 * j:32 * j + 32, e * NB:(e + 1) * NB], CA[:, j, e, ri, :], Hpb[:, ri, 4 * ch + j, 0:NB],
                                    start=False, stop=last, tile_position=(0, 32 * j)))
                    S.group("pe", fns, reads=[bw, bHp], writes=[bpy])
                    S.op("act", lambda: nc.scalar.activation(y2[:, 0:T], py[:, 0:T], AF.Square), reads=[bpy], writes=[by2])
                    S.op("dve", lambda: nc.vector.tensor_scalar(y2[:, 0:T], y2[:, 0:T], 0.044715, 1.0, ALU.mult, ALU.add), reads=[by2], writes=[by2])
                    S.op("dve", lambda: nc.vector.tensor_tensor(y2[:, 0:T], y2[:, 0:T], py[:, 0:T], ALU.mult), reads=[by2, bpy], writes=[by2])
                    S.op("act", lambda: nc.scalar.activation(sgm[:, 0:T], y2[:, 0:T], AF.Sigmoid, scale=1.5957691216057308), reads=[by2], writes=[bsg])
                    S.op("dve", lambda: nc.vector.tensor_tensor(hT[:, ch, sl].rearrange("p (x e) -> p e x", e=8),
                                                                sgm[:, 0:T].rearrange("p (e x) -> p e x", e=8),
                                                                py[:, 0:T].rearrange("p (e x) -> p e x", e=8), ALU.mult),
                         reads=[bsg, bpy, buT], writes=[bh[t]])
                if os.environ.get("S5CUT") == "B":
                    return
                pending_glu.append((t, T, sl))
            glu_all(pending_glu)
            if kind == "p" and hf == 1:
                outs = [(0, o_sre_p[s], o_sim_p[s])]
            elif kind == "s":
                outs = [(q, o_sre_s[q], o_sim_s[q]) for q in range(4)]
            else:
                outs = []
            for q, dre, dim in outs:
                for g2 in range(2):
                    rows = slice(g2 * 64, (g2 + 1) * 64)
                    S.dma("sp", dre.rearrange("(j g) p -> g p j", g=2)[g2], hstate[rows, 0, :, q], reads=[bhst], allow_slow_non_contiguous=True)
                    S.dma("sp", dim.rearrange("(j g) p -> g p j", g=2)[g2], hstate[rows, 1, :, q], reads=[bhst], allow_slow_non_contiguous=True)

        bomkv = {(l_, s_): Buf() for l_ in range(2) for s_ in range(2)}

        def cross_attn(l, kind, s, hf, tiles, c):
            norm_to_h(G_CROSS + l, tiles)
            nkv = 2 if kind == "s" else 1
            KTxs = [sb(c, "KTx%d" % i, [128, 8, 256], BF16) for i in range(nkv)]; bKTs = [Buf() for _ in range(nkv)]
            Vxs = [sb(c, "Vx%d" % i, [128, 2, 1024], BF16) for i in range(nkv)]; bVxs = [Buf() for _ in range(nkv)]
            KTx, bKT, Vx, bVx = KTxs[0], bKTs[0], Vxs[0], bVxs[0]
            QTx = sb(c, "QTx", [128, 8, 512], BF16); bQT = Buf()
            PTx = [sb(c, "PTx%d" % i, [128, 2, 512], BF16) for i in range(2)]; bPT = [Buf(), Buf()]
            OT = sb(c, "OT", [128, 8, 512], BF16); bOT = Buf()
            rec = sb(c, "rec", [128, 512], F32); brec = Buf()

            def walloc(cc, name):
                return sb(cc, name, [128, 8, 1024], BF16), Buf()

            def wissue(t, b, name):
                wd = W[name][l].rearrange("(k p) n -> p k n", p=128)
                for q4 in range(4):
                    S.dma("pool", t[:, :, q4 * 256:(q4 + 1) * 256], wd[:, :, q4 * 256:(q4 + 1) * 256], writes=[b], join=(q4 > 0))

            def wload(cc, name):
                t, b = walloc(cc, name)
                wissue(t, b, name)
                return t, b

            def attend(sl, T, kvi=0):
                KTx, bKT, Vx, bVx = KTxs[kvi], bKTs[kvi], Vxs[kvi], bVxs[kvi]
                for m in range(8):
                    pq, bq = pb[m % 2], bp[m % 2]
                    S.group("pe", [(lambda k=k: nc.tensor.matmul(pq[:, 0:T], wq[:, k, m * 128:(m + 1) * 128], hT[:, k, sl],
                                                                 start=(k == 0), stop=(k == 7))) for k in range(8)],
                            reads=[bwq, bh[0], bh[1]], writes=[bq])
                    if m % 2 == 0:
                        S.op("act", lambda: nc.scalar.copy(QTx[:, m, 0:T], pq[:, 0:T]), reads=[bq], writes=[bQT])
                    else:
                        S.op("dve", lambda: nc.vector.tensor_copy(QTx[:, m, 0:T], pq[:, 0:T]), reads=[bq], writes=[bQT])
                sbanks = [(2, 3), (5, 6)]

                def s_part(h):
                    for mt in range(2):
                        bi = sbanks[h % 2][mt]
                        S.group("pe", [(lambda dc=dc: nc.tensor.matmul(pb[bi][:, 0:T], KTx[:, 2 * h + dc, mt * 128:(mt + 1) * 128],
                                                                       QTx[:, 2 * h + dc, 0:T], start=(dc == 0), stop=(dc == 1)))
                                       for dc in range(2)], reads=[bKT, bQT], writes=[bp[bi]])
                        S.op("act", lambda: nc.scalar.activation(PTx[h % 2][:, mt, 0:T], pb[bi][:, 0:T], AF.Exp, scale=1.0 / 16.0),
                             reads=[bp[bi]], writes=[bPT[h % 2]])

                def o_part(h):
                    S.group("pe", [(lambda mt=mt: nc.tensor.matmul(pb[4][:, 0:T], ones_b[:], PTx[h % 2][:, mt, 0:T],
                                                                   start=(mt == 0), stop=(mt == 1))) for mt in range(2)],
                            reads=[bPT[h % 2], b_const], writes=[bp[4]])
                    S.op("act", lambda: nc.scalar.activation(rec[:, 0:T], pb[4][:, 0:T], AF.Ln), reads=[bp[4]], writes=[brec])
                    S.op("act", lambda: nc.scalar.activation(rec[:, 0:T], rec[:, 0:T], AF.Exp, scale=-1.0), reads=[brec], writes=[brec])
                    for dc in range(2):
                        pv, bv = pb[dc], bp[dc]
                        S.group("pe", [(lambda mt=mt: nc.tensor.matmul(pv[:, 0:T], Vx[:, mt, h * 256 + dc * 128:h * 256 + (dc + 1) * 128],
                                                                       PTx[h % 2][:, mt, 0:T], start=(mt == 0), stop=(mt == 1)))
                                       for mt in range(2)], reads=[bVx, bPT[h % 2]], writes=[bv])
                        S.op("dve", lambda: nc.vector.tensor_tensor(OT[:, 2 * h + dc, 0:T], pv[:, 0:T], rec[:, 0:T], ALU.mult),
                             reads=[bv, brec], writes=[bOT])
                s_part(0)
                for h in range(4):
                    if h + 1 < 4:
                        s_part(h + 1)
                    o_part(h)
                for o in range(8):
                    pa, ba = pb[4 + o % 2], bp[4 + o % 2]
                    S.group("pe", [(lambda k=k: nc.tensor.matmul(pa[:, 0:T], wo[:, k, o * 128:(o + 1) * 128], OT[:, k, 0:T],
                                                                 start=(k == 0), stop=(k == 7))) for k in range(8)],
                            reads=[bwo, bOT], writes=[ba])
                    tb = bx[sl.start // 512]
                    S.op("dve", lambda: nc.vector.scalar_tensor_tensor(xT[:, o, sl], pa[:, 0:T], 1.0, xT[:, o, sl], ALU.mult, ALU.add),
                         reads=[ba, tb], writes=[tb])

            def load_kv_from(srcK, srcV, rdeps, kvi=0):
                KTx, bKT, Vx, bVx = KTxs[kvi], bKTs[kvi], Vxs[kvi], bVxs[kvi]
                for mt in range(2):
                    st, bs = stage[mt], b_stage[mt]
                    S.dma("sp", st[:], srcK[mt * 128:(mt + 1) * 128, :], reads=rdeps, writes=[bs])
                    for half in range(2):
                        S.group("pe", [(lambda kk=kk: nc.tensor.transpose(pb[half][:, kk * 128:(kk + 1) * 128],
                                                                          st[:, (half * 4 + kk) * 128:(half * 4 + kk + 1) * 128], ident_f[:]))
                                       for kk in range(4)], reads=[bs, b_const], writes=[bp[half]])
                        S.op("act", lambda: nc.scalar.copy(KTx[:, half * 4:(half + 1) * 4, mt * 128:(mt + 1) * 128],
                                                           pb[half][:, :].rearrange("p (k t) -> p k t", k=4)),
                             reads=[bp[half]], writes=[bKT])
                    S.dma("pool", Vx[:, mt, :], srcV[mt * 128:(mt + 1) * 128, :], reads=rdeps, writes=[bVx])

            if kind == "p" and hf == 1:
                wq, bwq = wload(c, 'w_xq')
                wo, bwo = wload(c, 'w_xo')
                load_kv_from(o_mk[l, s], o_mv[l, s], [bomkv[(l, s)]])
                for t, T in tiles:
                    attend(slice(t * 512, t * 512 + T), T)
            elif kind == "p":
                wq, bwq = walloc(c, 'w_xq')
                wo, bwo = walloc(c, 'w_xo')
                with ExitStack() as c2:
                    wk, bwk = wload(c2, 'w_xk')
                    wv, bwv = wload(c2, 'w_xv')
                    wissue(wq, bwq, 'w_xq')
                    wissue(wo, bwo, 'w_xo')
                    mT = sb(c2, "mT", [128, 8, 256], F32); bmT = Buf()
                    mnT = sb(c2, "mnT", [128, 8, 256], BF16); bmn = Buf()
                    for mt in range(2):
                        st, bs = stage[mt], b_stage[mt]
                        S.dma("sp", st[:], memp[s, mt * 128:(mt + 1) * 128, :], writes=[bs])
                        for half in range(2):
                            S.group("pe", [(lambda kk=kk: nc.tensor.transpose(pb[half][:, kk * 128:(kk + 1) * 128],
                                                                              st[:, (half * 4 + kk) * 128:(half * 4 + kk + 1) * 128], ident_f[:]))
                                           for kk in range(4)], reads=[bs, b_const], writes=[bp[half]])
                            S.op("act", lambda: nc.scalar.copy(mT[:, half * 4:(half + 1) * 4, mt * 128:(mt + 1) * 128],
                                                               pb[half][:, :].rearrange("p (k t) -> p k t", k=4)),
                                 reads=[bp[half]], writes=[bmT])
                    S.op("act", lambda: nc.scalar.activation(sq[:, :, 0:256], mT[:, :, :], AF.Square), reads=[bmT], writes=[b_sq])
                    S.group("pe", [(lambda k=k: nc.tensor.matmul(pb[6][:, 0:256], ones_b[:], sq[:, k, 0:256], start=(k == 0), stop=(k == 7)))
                                   for k in range(8)], reads=[b_sq, b_const], writes=[bp[6]])
                    S.op("act", lambda: nc.scalar.activation(rs[:, 0:256], pb[6][:, 0:256], AF.Ln, bias=EPS, scale=1.0 / D), reads=[bp[6]], writes=[b_rs])
                    S.op("act", lambda: nc.scalar.activation(rs[:, 0:256], rs[:, 0:256], AF.Exp, scale=-0.5), reads=[b_rs], writes=[b_rs])
                    for k in range(8):
                        S.op("dve", lambda k=k: nc.vector.scalar_tensor_tensor(mnT[:, k, :], mT[:, k, :], gains[:, G_MEM + l, k:k + 1], rs[:, 0:256],
                                                                               ALU.mult, ALU.mult), reads=[bmT, b_rs, b_gains], writes=[bmn])
                    cnt = 0
                    for which, wt, bwt, dst in (("k", wk, bwk, o_mk), ("v", wv, bwv, o_mv)):
                        for mt in range(2):
                            st, bs = stage[cnt % 2], b_stage[cnt % 2]
                            cnt += 1
                            for ch in range(2):
                                pk, bk = pb[2 + ch], bp[2 + ch]
                                S.group("pe", [(lambda k=k: nc.tensor.matmul(pk[:, :], mnT[:, k, mt * 128:(mt + 1) * 128], wt[:, k, ch * 512:(ch + 1) * 512],
                                                                             start=(k == 0), stop=(k == 7))) for k in range(8)],
                                        reads=[bmn, bwt], writes=[bk])
                                S.op("act", lambda: nc.scalar.copy(st[:, ch * 512:(ch + 1) * 512], pk[:, :]), reads=[bk], writes=[bs])
                                if which == "v":
                                    S.op("dve", lambda: nc.vector.tensor_copy(Vx[:, mt, ch * 512:(ch + 1) * 512], st[:, ch * 512:(ch + 1) * 512]), reads=[bs], writes=[bVx])
                            S.dma("sp", dst[l, s, mt * 128:(mt + 1) * 128, :], st[:], reads=[bs], writes=[bomkv[(l, s)]])
                    for m in range(8):
                        pk, bk = pb[m % 2], bp[m % 2]
                        S.group("pe", [(lambda k=k: nc.tensor.matmul(pk[:, 0:256], wk[:, k, m * 128:(m + 1) * 128], mnT[:, k, :],
                                                                     start=(k == 0), stop=(k == 7))) for k in range(8)],
                                reads=[bmn, bwk], writes=[bk])
                        S.op("act", lambda: nc.scalar.copy(KTx[:, m, :], pk[:, 0:256]), reads=[bk], writes=[bKT])
                for t, T in tiles:
                    attend(slice(t * 512, t * 512 + T), T)
            else:
                wq, bwq = wload(c, 'w_xq')
                wo, bwo = wload(c, 'w_xo')
                load_kv_from(cmk[l, 0], cmv[l, 0], [], 0)
                for q in range(4):
                    if q + 1 < 4:
                        load_kv_from(cmk[l, q + 1], cmv[l, q + 1], [], (q + 1) % 2)
                    attend(slice(q * 64, q * 64 + 64), 64, q % 2)

        def final_out(dst_rows, tiles):
            for t, T in tiles:
                sl = slice(t * 512, t * 512 + T)
                norm_stats(t, T)
                for k in range(8):
                    S.op("dve", lambda k=k: nc.vector.scalar_tensor_tensor(
                        xT[:, k, sl], xT[:, k, sl], gains[:, G_FINAL, k:k + 1], rs[:, 0:T], ALU.mult, ALU.mult),
                        reads=[bx[t], b_rs, b_gains], writes=[bx[t]])
                for i in range(T // 128):
                    tok0 = t * 512 + i * 128
                    st, bs = stage[i % 2], b_stage[i % 2]
                    for half in range(2):
                        pbank, bbank = pb[half], bp[half]
                        S.group("pe", [
                            (lambda kk=kk: nc.tensor.transpose(pbank[:, kk * 128:(kk + 1) * 128],
                                                               xT[:, half * 4 + kk, tok0:tok0 + 128], ident_f[:]))
                            for kk in range(4)], reads=[bx[t], b_const], writes=[bbank])
                        if half == 0:
                            S.op("act", lambda: nc.scalar.copy(st[:, 0:512], pbank[:, :]), reads=[bbank], writes=[bs])
                        else:
                            S.op("dve", lambda: nc.vector.tensor_copy(st[:, 512:1024], pbank[:, :]), reads=[bbank],
                                 writes=[bs])
                    S.dma("sp", dst_rows[tok0:tok0 + 128, :], st[:], reads=[bs])

        if parts != "no_s5":
            with ExitStack() as c:
                s5_setup(c)
                S.barrier()

        sts = []
        for s in range(2):
            for hf in range(2):
                sts.append(("p", s, hf))
        sts.append(("s", 0, 0))
        if parts == "ffn_small":
            sts = [("p", 0, 0), ("s", 0, 0)]
        if parts == "only_s":
            sts = [("s", 0, 0)]
        if parts == "setup_only":
            sts = []
        if parts == "only_p":
            sts = [("p", 0, 0)]
        plan = []
        cur_scope = []

        def close_scope():
            nonlocal cur_scope
            if cur_scope:
                plan.append(cur_scope)
            cur_scope = []
        for kind, s, hf in sts:
            if kind == "p":
                src_ = xp[s, hf * 1024:(hf + 1) * 1024, :]
                dst_ = y_p[s, hf * 1024:(hf + 1) * 1024, :]
                ntok = 1024
                tiles = [(0, 512), (1, 512)]
            else:
                src_ = xs.rearrange("s t d -> (s t) d")
                dst_ = y_s.rearrange("s t d -> (s t) d")
                ntok = 256
                tiles = [(0, 256)]
            tag = "%s%d%d" % (kind, s, hf)
            A = dict(kind=kind, s=s, hf=hf, tiles=tiles)
            cur_scope.append(("load_x " + tag, lambda c, src_=src_, ntok=ntok: load_x(src_, ntok)))
            cur_scope.append(("ffn1 L0", lambda c, A=A: ffn(0, 1, A["tiles"], c)))
            close_scope()
            cur_scope.append(("even", lambda c, A=A: even_mixer(A["kind"], A["s"], A["hf"], A["tiles"], c)))
            close_scope()
            cur_scope.append(("cross L0", lambda c, A=A: cross_attn(0, A["kind"], A["s"], A["hf"], A["tiles"], c)))
            close_scope()
            cur_scope.append(("ffn2 L0", lambda c, A=A: ffn(0, 2, A["tiles"], c)))
            cur_scope.append(("ffn1 L1", lambda c, A=A: ffn(1, 1, A["tiles"], c)))
            close_scope()
            if parts != "no_s5":
                cur_scope.append(("odd", lambda c, A=A: odd_mixer(A["kind"], A["s"], A["hf"], A["tiles"], c)))
                close_scope()
            cur_scope.append(("cross L1", lambda c, A=A: cross_attn(1, A["kind"], A["s"], A["hf"], A["tiles"], c)))
            close_scope()
            cur_scope.append(("ffn2 L1", lambda c, A=A: ffn(1, 2, A["tiles"], c)))
            cur_scope.append(("final", lambda c, A=A, dst_=dst_: final_out(dst_, A["tiles"])))
        close_scope()
        for scope in plan:
            with ExitStack() as c:
                for name, fn in scope:
                    S.mark(name)
                    fn(c)
                S.barrier()
        S.finish()
    print("n_inst", S.n_inst)
    S.mark("end")
    build.marks = S.marks
    return nc


_NC_CACHE = {}


def _rope_table():
    inv = (np.float32(10000.0) ** (-np.arange(0, 32, 2, dtype=np.float32) / np.float32(32))).astype(np.float32)
    pos = np.concatenate([np.arange(2048), 1024 + np.arange(64)]).astype(np.float32)
    ang = (pos[:, None] * inv[None, :]).astype(np.float32)
    return np.concatenate([np.cos(ang), np.sin(ang)], axis=1).astype(np.float32)


def _rope_table_T():
    r = _rope_table()
    cosT = np.concatenate([r[:, 0:16].T, r[:, 0:16].T], axis=0)
    sinT = np.concatenate([r[:, 16:32].T, r[:, 16:32].T], axis=0)
    return np.ascontiguousarray(np.concatenate([cosT, sinT], axis=0)).astype(np.float32)


def kernel(**inputs):
    nc = _NC_CACHE.get("nc")
    if nc is None:
        nc = build()
        _NC_CACHE["nc"] = nc
    in_maps = []
    for c in range(NCORES):
        m = {"xp": np.ascontiguousarray(inputs["x_prompt"][2 * c:2 * c + 2]),
             "xs": np.ascontiguousarray(inputs["x_sample"][4 * c:4 * c + 4]),
             "memp": np.ascontiguousarray(inputs["mem_prompt"][2 * c:2 * c + 2]),
             "cmk": np.ascontiguousarray(inputs["cache_mem_k"][:, 4 * c:4 * c + 4]).reshape(2, 4, 256, 1024),
             "cmv": np.ascontiguousarray(inputs["cache_mem_v"][:, 4 * c:4 * c + 4]).reshape(2, 4, 256, 1024),
             "rope": _rope_table(), "ropeT": _rope_table_T(),
             "latc": np.ascontiguousarray(inputs["cache_mla_latent"][0, 4 * c:4 * c + 4]),
             "krc": np.ascontiguousarray(inputs["cache_mla_krope"][0, 4 * c:4 * c + 4]),
             "convc": np.ascontiguousarray(inputs["state_conv"][0, 4 * c:4 * c + 4]),
             "ssre": np.ascontiguousarray(inputs["state_ssm_re"][0, 4 * c:4 * c + 4]),
             "ssim": np.ascontiguousarray(inputs["state_ssm_im"][0, 4 * c:4 * c + 4])}
        for n in WEIGHT_NAMES:
            m[n] = np.ascontiguousarray(inputs[n])
        in_maps.append(m)
    res = run_bass_kernel_spmd(nc, in_maps, core_ids=list(range(NCORES)))
    R = res.results
    cat = lambda k, ax=0: np.concatenate([r[k] for r in R], axis=ax)
    y_prompt = cat("y_p")
    y_sample = cat("y_s")
    lat_p = cat("o_lat_p")[None]
    kr_p = cat("o_kr_p")[None]
    conv_p = cat("o_conv_p")[None]
    mk_p = cat("o_mk", 1).reshape(2, 16, 256, 4, 256)
    mv_p = cat("o_mv", 1).reshape(2, 16, 256, 4, 256)
    lat_s = cat("o_lat_s")[None]
    kr_s = cat("o_kr_s")[None]
    conv_s = cat("o_conv_s")[None]
    sre_p = cat("o_sre_p")[None]
    sim_p = cat("o_sim_p")[None]
    sre_s = cat("o_sre_s")[None]
    sim_s = cat("o_sim_s")[None]
    return (y_prompt, y_sample, lat_p, kr_p, conv_p, sre_p, sim_p, mk_p, mv_p, lat_s, kr_s, conv_s, sre_s, sim_s)
```

```python
from contextlib import ExitStack
import os
import numpy as np
import concourse.bass as bass
import concourse.mybir as mybir
from concourse.bass_utils import run_bass_kernel_spmd

F32 = mybir.dt.float32
BF16 = mybir.dt.bfloat16
AF = mybir.ActivationFunctionType
ALU = mybir.AluOpType

D = 1024
DFF = 2816
EPS = 1e-6
NCORES = 8


class Buf:
    __slots__ = ("name", "w", "r", "wj")

    def __init__(self, name=""):
        self.name = name
        self.w = None
        self.r = []
        self.wj = []


class Eng:
    def __init__(self, name, handle, sem):
        self.name = name
        self.h = handle
        self.sem = sem
        self.count = 0
        self.seen = {}


class Sched:
    def __init__(self, nc, ctx):
        self.nc = nc
        self.sems = {}
        self.E = {}
        for name, h in (("pe", nc.tensor), ("act", nc.scalar), ("dve", nc.vector),
                        ("pool", nc.gpsimd), ("sp", nc.sync)):
            s = ctx.enter_context(nc.semaphore("s_" + name))
            self.sems[name] = s
            self.E[name] = Eng(name, h, s)
        self.dpool = {"sp": [], "pool": [], "act": []}
        self.drr = {"sp": 0, "pool": 0, "act": 0}
        for q, n in (("sp", 16), ("pool", 12), ("act", 2)):
            for i in range(n):
                key = "d%s%d" % (q, i)
                self.sems[key] = ctx.enter_context(nc.semaphore(key))
                self.dpool[q].append([key, 0, None])
        self.n_inst = 0
        self.n_pe = 0
        self.marks = []

    def mark(self, name):
        self.marks.append((name, self.n_pe))

    def _wait(self, eng, tok):
        if tok is None:
            return
        key, val = tok
        if eng.seen.get(key, 0) >= val:
            return
        if key == eng.name:
            assert val <= eng.count, "wait on own future"
        eng.h.wait_ge(self.sems[key], val)
        eng.seen[key] = val

    def _deps(self, reads, writes, join=False):
        toks = {}

        def add(t):
            if t is not None and toks.get(t[0], 0) < t[1]:
                toks[t[0]] = t[1]
        for b in reads:
            add(b.w)
            for t in b.wj:
                add(t)
        for b in writes:
            add(b.w)
            if not join:
                for t in b.wj:
                    add(t)
            for t in b.r:
                add(t)
        return toks

    def _record(self, tok, reads, writes, join=False):
        for b in reads:
            b.r.append(tok)
            if len(b.r) > 64:
                mx = {}
                for k, v in b.r:
                    if mx.get(k, 0) < v:
                        mx[k] = v
                b.r = list(mx.items())
        for b in writes:
            if join:
                b.wj.append(tok)
            else:
                b.w = tok
                b.wj = []
            b.r = []

    def group(self, en, fns, reads=(), writes=()):
        eng = self.E[en]
        for k, v in self._deps(reads, writes).items():
            if en == "pe" and k == "pe":
                continue
            self._wait(eng, (k, v))
        inst = None
        for fn in fns:
            inst = fn()
            self.n_inst += 1
            if en == "pe":
                self.n_pe += 1
        inst.then_inc(eng.sem, 1)
        eng.count += 1
        tok = (en, eng.count)
        self._record(tok, reads, writes)
        return tok

    def op(self, en, fn, reads=(), writes=()):
        return self.group(en, [fn], reads, writes)

    def dma(self, en, out_ap, in_ap, reads=(), writes=(), join=False, **kw):
        eng = self.E[en]
        pool = self.dpool[en]
        slot = pool[self.drr[en]]
        self.drr[en] = (self.drr[en] + 1) % len(pool)
        self._wait(eng, slot[2])
        for k, v in self._deps(reads, writes, join).items():
            self._wait(eng, (k, v))
        inst = eng.h.dma_start(out=out_ap, in_=in_ap, **kw)
        self.n_inst += 1
        slot[1] += 16
        inst.then_inc(self.sems[slot[0]], 16)
        tok = (slot[0], slot[1])
        slot[2] = tok
        self._record(tok, reads, writes, join)
        return tok

    def barrier(self):
        toks = [(e.name, e.count) for e in self.E.values() if e.count > 0]
        for q in self.dpool.values():
            for s in q:
                if s[2] is not None:
                    toks.append(s[2])
        for e in self.E.values():
            for t in toks:
                if t[0] == e.name:
                    continue
                self._wait(e, t)

    def finish(self):
        eng = self.E["sp"]
        for q in self.dpool.values():
            for s in q:
                self._wait(eng, s[2])
        for e in self.E.values():
            if e.name != "sp" and e.count > 0:
                self._wait(eng, (e.name, e.count))


WEIGHT_NAMES = ['ln_ffn1', 'w_ffn1_gu', 'w_ffn1_down', 'ln_mix', 'w_in_even', 'q_norm', 'kv_norm', 'w_uq', 'w_ukv',
                'conv_w', 'w_out_even', 'w_in_odd', 'ssm_a_re', 'ssm_a_im', 'ssm_b_re', 'ssm_b_im', 'ssm_c_re',
                'ssm_c_im', 'ssm_log_dt', 'ssm_d', 'w_glu', 'ln_cross', 'ln_mem', 'w_xq', 'w_xk', 'w_xv', 'w_xo',
                'ln_ffn2', 'w_ffn2_gu', 'w_ffn2_down', 'ln_final']
WEIGHT_SHAPES = {
    'ln_ffn1': [2, 1024], 'w_ffn1_gu': [2, 1024, 5632], 'w_ffn1_down': [2, 2816, 1024], 'ln_mix': [2, 1024],
    'w_in_even': [1, 1024, 2080], 'q_norm': [1, 256], 'kv_norm': [1, 256], 'w_uq': [1, 256, 768],
    'w_ukv': [1, 256, 1024], 'conv_w': [1, 3, 512], 'w_out_even': [1, 1024, 1024], 'w_in_odd': [1, 1024, 1024],
    'ssm_a_re': [1, 64, 64], 'ssm_a_im': [1, 64, 64], 'ssm_b_re': [1, 64, 64, 16], 'ssm_b_im': [1, 64, 64, 16],
    'ssm_c_re': [1, 64, 16, 64], 'ssm_c_im': [1, 64, 16, 64], 'ssm_log_dt': [1, 64], 'ssm_d': [1, 1024],
    'w_glu': [1, 1024, 2048], 'ln_cross': [2, 1024], 'ln_mem': [2, 1024], 'w_xq': [2, 1024, 1024],
    'w_xk': [2, 1024, 1024], 'w_xv': [2, 1024, 1024], 'w_xo': [2, 1024, 1024], 'ln_ffn2': [2, 1024],
    'w_ffn2_gu': [2, 1024, 5632], 'w_ffn2_down': [2, 2816, 1024], 'ln_final': [1024]}


def build(parts="all"):
    nc = bass.Bass("TRN2", target_bir_lowering=False)

    def din(name, shape):
        return nc.dram_tensor(name, shape, F32, kind="ExternalInput").ap()

    def dout(name, shape):
        return nc.dram_tensor(name, shape, F32, kind="ExternalOutput").ap()

    xp = din("xp", [2, 2048, D])
    xs = din("xs", [4, 64, D])
    W = {n: din(n, WEIGHT_SHAPES[n]) for n in WEIGHT_NAMES}
    memp = din("memp", [2, 256, D])
    cmk = din("cmk", [2, 4, 256, D])
    cmv = din("cmv", [2, 4, 256, D])
    y_p = dout("y_p", [2, 2048, D])
    y_s = dout("y_s", [4, 64, D])
    rope = din("rope", [2112, 32])
    ropeT = din("ropeT", [64, 2112])
    latc = din("latc", [4, 1024, 256])
    krc = din("krc", [4, 1024, 32])
    convc = din("convc", [4, 2, 512])
    ssre = din("ssre", [4, 64, 64])
    ssim = din("ssim", [4, 64, 64])
    o_sre_p = dout("o_sre_p", [2, 64, 64])
    o_sim_p = dout("o_sim_p", [2, 64, 64])
    o_sre_s = dout("o_sre_s", [4, 64, 64])
    o_sim_s = dout("o_sim_s", [4, 64, 64])
    o_lat_p = dout("o_lat_p", [2, 2048, 256])
    o_kr_p = dout("o_kr_p", [2, 2048, 32])
    o_conv_p = dout("o_conv_p", [2, 2, 512])
    o_lat_s = dout("o_lat_s", [4, 64, 256])
    o_kr_s = dout("o_kr_s", [4, 64, 32])
    o_conv_s = dout("o_conv_s", [4, 2, 512])
    o_mk = dout("o_mk", [2, 2, 256, D])
    o_mv = dout("o_mv", [2, 2, 256, D])

    with ExitStack() as ctx:
        S = Sched(nc, ctx)

        uid = [0]

        def sb(c, name, shape, dt):
            uid[0] += 1
            return c.enter_context(nc.sbuf_tensor("%s_%d" % (name, uid[0]), shape, dt))

        xT = sb(ctx, "xT", [128, 8, 1024], F32)
        hT = sb(ctx, "hT", [128, 8, 1024], BF16)
        bx = [Buf("x0"), Buf("x1")]
        bh = [Buf("h0"), Buf("h1")]
        ident_f = sb(ctx, "ident_f", [128, 128], F32)
        ones_b = sb(ctx, "ones_b", [128, 128], BF16)
        b_const = Buf("const")
        gains = sb(ctx, "gains", [128, 11, 8], F32)
        b_gains = Buf("gains")
        stage = [sb(ctx, "stage%d" % i, [128, 1024], F32) for i in range(2)]
        b_stage = [Buf("stage0"), Buf("stage1")]
        sq = sb(ctx, "sq", [128, 8, 512], BF16)
        b_sq = Buf("sq")
        rs = sb(ctx, "rs", [128, 512], F32)
        b_rs = Buf("rs")
        pb = [ctx.enter_context(nc.psum_tensor("pb%d" % i, [128, 512], F32)) for i in range(7)]
        pT = ctx.enter_context(nc.psum_tensor("pT", [128, 1024], BF16))
        bp = [Buf("pb%d" % i) for i in range(7)]
        bpT = Buf("pT")
        ident_b = sb(ctx, "ident_b", [128, 128], BF16)
        latT = sb(ctx, "latT", [128, 2, 2048], BF16); blatT = Buf("latT")
        krT = sb(ctx, "krT", [128, 2048], BF16); bkrT = Buf("krT")
        chist = sb(ctx, "chist", [128, 4, 2], F32); bchist = Buf("chist")
        onesP = sb(ctx, "onesP", [128, 2, 128], BF16)
        zerosb = sb(ctx, "zerosb", [128, 128], BF16)
        hstate = sb(ctx, "hstate", [128, 2, 32, 4], F32); bhst = Buf("hstate")

        S.op("pool", lambda: nc.gpsimd.memset(ident_f[:], 0.0), writes=[b_const])
        S.op("pool", lambda: nc.gpsimd.affine_select(ident_f[:], ident_f[:], [[-1, 128]], ALU.not_equal, 1.0,
                                                     base=0, channel_multiplier=1), reads=[b_const], writes=[b_const])
        S.op("pool", lambda: nc.gpsimd.memset(ones_b[:], 1.0), writes=[b_const])
        S.op("pool", lambda: nc.gpsimd.tensor_copy(ident_b[:], ident_f[:]), reads=[b_const], writes=[b_const])
        S.op("pool", lambda: nc.gpsimd.memset(onesP[:], 0.0), writes=[b_const])
        S.op("pool", lambda: nc.gpsimd.memset(zerosb[:], 0.0), writes=[b_const])
        S.op("pool", lambda: nc.gpsimd.memset(onesP[:, 0, 0:64], 1.0), reads=[b_const], writes=[b_const])
        S.op("pool", lambda: nc.gpsimd.memset(onesP[:, 1, 64:128], 1.0), reads=[b_const], writes=[b_const])
        S.op("pool", lambda: nc.gpsimd.memset(krT[:], 0.0), writes=[bkrT])
        gsrc = [W['ln_ffn1'][0], W['ln_ffn1'][1], W['ln_mix'][0], W['ln_mix'][1], W['ln_cross'][0], W['ln_cross'][1],
                W['ln_ffn2'][0], W['ln_ffn2'][1], W['ln_final'], W['ln_mem'][0], W['ln_mem'][1]]
        for i, g in enumerate(gsrc):
            S.dma("sp", gains[:, i, :], g.rearrange("(k p) -> p k", p=128), writes=[b_gains],
                  allow_slow_non_contiguous=True)
        G_FFN1, G_MIX, G_CROSS, G_FFN2, G_FINAL, G_MEM = 0, 2, 4, 6, 8, 9

        def load_x(src_rows, ntok):
            for i in range(ntok // 128):
                st, bs = stage[i % 2], b_stage[i % 2]
                S.dma("sp", st[:], src_rows[i * 128:(i + 1) * 128, :], writes=[bs])
                for half in range(2):
                    pbank, bbank = pb[half], bp[half]
                    S.group("pe", [
                        (lambda kk=kk: nc.tensor.transpose(pbank[:, kk * 128:(kk + 1) * 128],
                                                           st[:, (half * 4 + kk) * 128:(half * 4 + kk + 1) * 128],
                                                           ident_f[:]))
                        for kk in range(4)], reads=[bs, b_const], writes=[bbank])
                    eng = "act" if half == 0 else "dve"
                    dst = xT[:, half * 4:(half + 1) * 4, i * 128:(i + 1) * 128]
                    srcp = pbank[:, :].rearrange("p (k t) -> p k t", k=4)
                    if eng == "act":
                        S.op("act", lambda: nc.scalar.copy(dst, srcp), reads=[bbank], writes=[bx[i // 4]])
                    else:
                        S.op("dve", lambda: nc.vector.tensor_copy(dst, srcp), reads=[bbank], writes=[bx[i // 4]])

        def norm_stats(t, T):
            sl = slice(t * 512, t * 512 + T)
            S.op("act", lambda: nc.scalar.activation(sq[:, :, 0:T], xT[:, :, sl], AF.Square),
                 reads=[bx[t]], writes=[b_sq])
            S.group("pe", [(lambda k=k: nc.tensor.matmul(pb[6][:, 0:T], ones_b[:], sq[:, k, 0:T],
                                                         start=(k == 0), stop=(k == 7))) for k in range(8)],
                    reads=[b_sq, b_const], writes=[bp[6]])
            S.op("act", lambda: nc.scalar.activation(rs[:, 0:T], pb[6][:, 0:T], AF.Ln, bias=EPS, scale=1.0 / D),
                 reads=[bp[6]], writes=[b_rs])
            S.op("act", lambda: nc.scalar.activation(rs[:, 0:T], rs[:, 0:T], AF.Exp, scale=-0.5),
                 reads=[b_rs], writes=[b_rs])

        def norm_to_h(gi, tiles):
            for t, T in tiles:
                sl = slice(t * 512, t * 512 + T)
                norm_stats(t, T)
                for k in range(8):
                    S.op("dve", lambda k=k: nc.vector.scalar_tensor_tensor(
                        hT[:, k, sl], xT[:, k, sl], gains[:, gi, k:k + 1], rs[:, 0:T], ALU.mult, ALU.mult),
                        reads=[bx[t], b_rs, b_gains], writes=[bh[t]])

        def ffn(l, which, tiles, c):
            wgu_d = W['w_ffn%d_gu' % which][l].rearrange("(k p) n -> p k n", p=128)
            wdn_d = W['w_ffn%d_down' % which][l].rearrange("(j p) n -> p j n", p=128)
            fb = getattr(c, "ffn_bufs", None)
            if fb is None:
                NS = 4
                fb = dict(NS=NS, gcount=[0],
                          wgu=[sb(c, "wgu%d" % i, [128, 8, 2, 512], BF16) for i in range(NS)],
                          wdn=[sb(c, "wdn%d" % i, [128, 4, 1024], BF16) for i in range(NS)],
                          bw=[Buf("w%d" % i) for i in range(NS)],
                          sg=[sb(c, "sg%d" % i, [128, 512], BF16) for i in range(2)], bsg=[Buf(), Buf()],
                          hid=[sb(c, "hid%d" % i, [128, 4, 512], BF16) for i in range(2)], bhid=[Buf(), Buf()])
                c.ffn_bufs = fb
            wgu, wdn, bw, sg, bsg, hid, bhid = fb["wgu"], fb["wdn"], fb["bw"], fb["sg"], fb["bsg"], fb["hid"], fb["bhid"]
            norm_to_h((G_FFN1 if which == 1 else G_FFN2) + l, tiles)
            groups = [(0, 4), (4, 4), (8, 4), (12, 4), (16, 4), (20, 2)]
            st = {"cnt": 0, "hc": 0, "oc": 0}

            def gu(s, ng, t, T, hb):
                sl = slice(t * 512, t * 512 + T)
                for j in range(ng):
                    cnt = st["cnt"]
                    st["cnt"] += 1
                    pg, pu = pb[2 * (cnt % 2)], pb[2 * (cnt % 2) + 1]
                    bg, bu = bp[2 * (cnt % 2)], bp[2 * (cnt % 2) + 1]
                    S.group("pe", [(lambda k=k: nc.tensor.matmul(pg[:, 0:T], wgu[s][:, k, 0, j * 128:(j + 1) * 128],
                                                                 hT[:, k, sl], start=(k == 0), stop=(k == 7)))
                                   for k in range(8)], reads=[bw[s], bh[t]], writes=[bg])
                    S.group("pe", [(lambda k=k: nc.tensor.matmul(pu[:, 0:T], wgu[s][:, k, 1, j * 128:(j + 1) * 128],
                                                                 hT[:, k, sl], start=(k == 0), stop=(k == 7)))
                                   for k in range(8)], reads=[bw[s], bh[t]], writes=[bu])
                    sgi = cnt % 2
                    S.op("act", lambda: nc.scalar.activation(sg[sgi][:, 0:T], pg[:, 0:T], AF.Silu),
                         reads=[bg], writes=[bsg[sgi]])
                    S.op("dve", lambda: nc.vector.tensor_tensor(hid[hb][:, j, 0:T], pu[:, 0:T], sg[sgi][:, 0:T], ALU.mult),
                         reads=[bu, bsg[sgi]], writes=[bhid[hb]])

            def down(s, ng, t, T, hb):
                sl = slice(t * 512, t * 512 + T)
                for o in range(8):
                    oc = st["oc"]
                    st["oc"] += 1
                    pa, ba = pb[4 + oc % 2], bp[4 + oc % 2]
                    S.group("pe", [(lambda j=j: nc.tensor.matmul(pa[:, 0:T], wdn[s][:, j, o * 128:(o + 1) * 128],
                                                                 hid[hb][:, j, 0:T], start=(j == 0), stop=(j == ng - 1)))
                                   for j in range(ng)], reads=[bw[s], bhid[hb]], writes=[ba])
                    S.op("dve", lambda: nc.vector.scalar_tensor_tensor(xT[:, o, sl], pa[:, 0:T], 0.5, xT[:, o, sl],
                                                                       ALU.mult, ALU.add),
                         reads=[ba, bx[t]], writes=[bx[t]])
            prev = None
            for gi, (j0, ng) in enumerate(groups):
                s = fb["gcount"][0] % fb["NS"]
                fb["gcount"][0] += 1
                S.dma("pool", wgu[s][:, :, 0, 0:ng * 128], wgu_d[:, :, j0 * 128:(j0 + ng) * 128], writes=[bw[s]])
                S.dma("pool", wgu[s][:, :, 1, 0:ng * 128], wgu_d[:, :, DFF + j0 * 128:DFF + (j0 + ng) * 128], writes=[bw[s]], join=True)
                S.dma("pool", wdn[s][:, 0:ng, :], wdn_d[:, j0:j0 + ng, :], writes=[bw[s]], join=True)
                for t, T in tiles:
                    hb = st["hc"] % 2
                    st["hc"] += 1
                    gu(s, ng, t, T, hb)
                    if prev is not None:
                        down(*prev)
                    prev = (s, ng, t, T, hb)
            down(*prev)

        def even_mixer(kind, s, hf, tiles, c):
            norm_to_h(G_MIX + 0, tiles)
            win = W['w_in_even'][0].rearrange("(k p) n -> p k n", p=128)
            SC = (64 + 32) ** -0.5
            ntok = sum(T for _, T in tiles)
            wkv = sb(c, "wkv", [128, 8, 288], BF16); bwkv = Buf()
            S.dma("pool", wkv[:], win[:, :, 256:544], writes=[bwkv])
            wqin = sb(c, "wqin", [128, 8, 256], BF16); bwqin = Buf()
            S.dma("pool", wqin[:], win[:, :, 0:256], writes=[bwqin])
            wuq = sb(c, "wuq", [128, 2, 768], BF16); bwuq = Buf()
            S.dma("pool", wuq[:], W['w_uq'][0].rearrange("(k p) n -> p k n", p=128), writes=[bwuq])
            wsw = sb(c, "wsw", [128, 2, 768], BF16); bwsw = Buf()
            wukv = sb(c, "wukv", [128, 2, 1024], BF16); bwukv = Buf()
            S.dma("pool", wukv[:], W['w_ukv'][0].rearrange("(k p) n -> p k n", p=128), writes=[bwukv])
            wout = sb(c, "wout", [128, 8, 1024], BF16); bwout = Buf()
            S.dma("pool", wout[:], W['w_out_even'][0].rearrange("(k p) n -> p k n", p=128), writes=[bwout])
            wuq4 = wuq[:, :, :].rearrange("p k (h d) -> p (k h) d", d=96)
            wsw4 = wsw[:, :, :].rearrange("p k (h d) -> p (k h) d", d=96)
            S.op("pool", lambda: nc.gpsimd.memset(wsw[:], 0.0), writes=[bwsw])
            S.op("dve", lambda: nc.vector.tensor_scalar(wsw4[:, :, 64:80], wuq4[:, :, 80:96], -1.0, None, ALU.mult),
                 reads=[bwuq, bwsw], writes=[bwsw])
            S.op("dve", lambda: nc.vector.tensor_copy(wsw4[:, :, 80:96], wuq4[:, :, 64:80]), reads=[bwuq, bwsw], writes=[bwsw])
            kvn = sb(c, "kvn", [128, 256], F32); bkvn = Buf()
            S.dma("sp", kvn[:], W['kv_norm'][0].partition_broadcast(128), writes=[bkvn])
            qn = sb(c, "qn", [128, 2], F32); bqn = Buf()
            S.dma("sp", qn[:], W['q_norm'][0].rearrange("(m p) -> p m", p=128), writes=[bqn], allow_slow_non_contiguous=True)
            cw = sb(c, "cw", [128, 4, 3], F32); bcw = Buf()
            for k3 in range(3):
                S.dma("sp", cw[:, :, k3], W['conv_w'][0][k3].rearrange("(c p) -> p c", p=128), writes=[bcw], allow_slow_non_contiguous=True)
            cosT = sb(c, "cosT", [128, 1024], F32); sinT = sb(c, "sinT", [128, 1024], F32); btab = Buf()
            p0 = hf * 1024 if kind == "p" else 2048
            npos = 1024 if kind == "p" else 64
            S.dma("sp", cosT[64:96, 0:npos], ropeT[0:32, p0:p0 + npos], writes=[btab])
            S.dma("sp", sinT[64:96, 0:npos], ropeT[32:64, p0:p0 + npos], writes=[btab])
            rt = [sb(c, "rt%d" % i, [128, 32], F32) for i in range(2)]; brt = [Buf(), Buf()]
            junk = sb(c, "junk", [128, 256], F32); bjunk = Buf()
            ss = [sb(c, "ss%d" % i, [128, 1], F32) for i in range(2)]; bss = [Buf(), Buf()]
            latf = [sb(c, "latf%d" % i, [128, 256], F32) for i in range(2)]; blat = [Buf(), Buf()]
            krf = [sb(c, "krf%d" % i, [128, 32], F32) for i in range(2)]; bkr = [Buf(), Buf()]
            tmp = sb(c, "tmpk", [128, 32], F32); btmp = Buf()
            latb = [sb(c, "latb%d" % i, [128, 256], BF16) for i in range(2)]; blatb = [Buf(), Buf()]
            krb = [sb(c, "krb%d" % i, [128, 128], BF16) for i in range(2)]; bkrb = [Buf(), Buf()]
            for i in range(2):
                S.op("pool", lambda i=i: nc.gpsimd.memset(krb[i][:], 0.0), writes=[bkrb[i]])
            block = [Buf(), Buf()]
            cqnT = sb(c, "cqnT", [128, 2, 1024], BF16); bcqn = Buf()
            KT = sb(c, "KT", [128, 2, 2048], BF16); bKT = Buf()
            Vp = sb(c, "Vp", [128, 2, 17, 128], BF16); bVp = Buf()
            S.op("pool", lambda: nc.gpsimd.memset(Vp[:], 0.0), writes=[bVp])
            QTs = [sb(c, "QT%d" % i, [128, 2, 512], BF16) for i in range(2)]; bQTs = [Buf(), Buf()]
            attT = sb(c, "attT", [128, 4, 1024], BF16); battT = [Buf() for _ in range(4)]
            PT = [sb(c, "PT%d" % i, [128, 512], BF16) for i in range(3)]; bPT = [Buf(), Buf(), Buf()]
            tq1 = sb(c, "tq1", [128, 512], F32); tq2 = sb(c, "tq2", [128, 512], F32); btq1 = Buf(); btq2 = Buf()
            rec = sb(c, "recm", [128, 512], F32); brec = Buf()
            bqblock = Buf()
            bvblock = Buf()
            cnt = {"u": 0, "pt": 0, "s": 0, "o": 0}

            def kv_unit(tok0, n, rrow, dlat, dkr, key0):
                a = cnt["u"] % 2
                cnt["u"] += 1
                ps, bps = pb[2 + a], bp[2 + a]
                tb = bh[tok0 // 512]
                S.dma("sp", rt[a][0:n, :], rope[rrow:rrow + n, :], writes=[brt[a]])
                S.group("pe", [(lambda k=k: nc.tensor.matmul(ps[0:n, 0:288], hT[:, k, tok0:tok0 + n], wkv[:, k, :],
                                                             start=(k == 0), stop=(k == 7))) for k in range(8)],
                        reads=[tb, bwkv], writes=[bps])
                S.op("act", lambda: nc.scalar.activation(junk[0:n, :], ps[0:n, 0:256], AF.Square, accum_out=ss[a][0:n, 0:1]),
                     reads=[bps], writes=[bjunk, bss[a], block[a]])
                S.op("act", lambda: nc.scalar.activation(ss[a][0:n, :], ss[a][0:n, :], AF.Ln, bias=EPS, scale=1.0 / 256.0),
                     reads=[bss[a]], writes=[bss[a]])
                S.op("act", lambda: nc.scalar.activation(ss[a][0:n, :], ss[a][0:n, :], AF.Exp, scale=-0.5),
                     reads=[bss[a]], writes=[bss[a]])
                S.op("dve", lambda: nc.vector.scalar_tensor_tensor(latf[a][0:n, :], ps[0:n, 0:256], ss[a][0:n, 0:1], kvn[0:n, :],
                                                                   ALU.mult, ALU.mult), reads=[bps, bss[a], bkvn], writes=[blat[a], block[a]])
                S.dma("sp", dlat, latf[a][0:n, :], reads=[blat[a]])
                x1, x2 = ps[0:n, 256:272], ps[0:n, 272:288]
                cs, sn = rt[a][0:n, 0:16], rt[a][0:n, 16:32]
                S.op("dve", lambda: nc.vector.tensor_tensor(krf[a][0:n, 0:16], x1, cs, ALU.mult), reads=[bps, brt[a]], writes=[bkr[a], block[a]])
                S.op("dve", lambda: nc.vector.tensor_tensor(tmp[0:n, 0:16], x2, sn, ALU.mult), reads=[bps, brt[a]], writes=[btmp, block[a]])
                S.op("dve", lambda: nc.vector.tensor_tensor(krf[a][0:n, 0:16], krf[a][0:n, 0:16], tmp[0:n, 0:16], ALU.subtract),
                     reads=[bkr[a], btmp], writes=[bkr[a]])
                S.op("dve", lambda: nc.vector.tensor_tensor(krf[a][0:n, 16:32], x1, sn, ALU.mult), reads=[bps, brt[a]], writes=[bkr[a], block[a]])
                S.op("dve", lambda: nc.vector.tensor_tensor(tmp[0:n, 16:32], x2, cs, ALU.mult), reads=[bps, brt[a], bkr[a]], writes=[btmp, block[a]])
                S.op("dve", lambda: nc.vector.tensor_tensor(krf[a][0:n, 16:32], krf[a][0:n, 16:32], tmp[0:n, 16:32], ALU.add),
                     reads=[bkr[a], btmp], writes=[bkr[a]])
                S.dma("sp", dkr, krf[a][0:n, :], reads=[bkr[a]])
                S.op("pool", lambda: nc.gpsimd.tensor_copy(latb[a][0:n, :], latf[a][0:n, :]), reads=[blat[a]], writes=[blatb[a]])
                S.op("pool", lambda: nc.gpsimd.tensor_copy(krb[a][0:n, 64:96], krf[a][0:n, :]), reads=[bkr[a]], writes=[bkrb[a]])
                return lambda: keys_from_tok(latb[a], blatb[a], krb[a], bkrb[a], n, key0)

            def keys_from_tok(lb, blb, kb, bkb, n, key0):
                S.group("pe", [(lambda: nc.tensor.transpose(pT[:, 0:n], lb[0:n, 0:128], ident_b[0:n, 0:n])),
                               (lambda: nc.tensor.transpose(pT[:, 128:128 + n], lb[0:n, 128:256], ident_b[0:n, 0:n])),
                               (lambda: nc.tensor.transpose(pT[:, 256:256 + n], kb[0:n, :], ident_b[0:n, 0:n]))],
                        reads=[blb, bkb, b_const], writes=[bpT])
                S.op("act", lambda: nc.scalar.copy(latT[:, :, key0:key0 + n], pT[:, 0:256].rearrange("p (k t) -> p k t", k=2)[:, :, 0:n]),
                     reads=[bpT], writes=[blatT])
                S.op("act", lambda: nc.scalar.copy(krT[64:96, key0:key0 + n], pT[64:96, 256:256 + n]), reads=[bpT], writes=[bkrT])

            def make_cqn():
                for t, T in tiles:
                    sl = slice(t * 512, t * 512 + T)
                    for m in range(2):
                        S.group("pe", [(lambda k=k: nc.tensor.matmul(pb[m][:, 0:T], wqin[:, k, m * 128:(m + 1) * 128], hT[:, k, sl],
                                                                     start=(k == 0), stop=(k == 7))) for k in range(8)],
                                reads=[bwqin, bh[t]], writes=[bp[m]])
                        S.op("act", lambda m=m: nc.scalar.activation(sq[:, m, 0:T], pb[m][:, 0:T], AF.Square), reads=[bp[m]], writes=[b_sq])
                    S.group("pe", [(lambda m=m: nc.tensor.matmul(pb[6][:, 0:T], ones_b[:], sq[:, m, 0:T], start=(m == 0), stop=(m == 1)))
                                   for m in range(2)], reads=[b_sq, b_const], writes=[bp[6]])
                    S.op("act", lambda: nc.scalar.activation(rs[:, 0:T], pb[6][:, 0:T], AF.Ln, bias=EPS, scale=1.0 / 256.0), reads=[bp[6]], writes=[b_rs])
                    S.op("act", lambda: nc.scalar.activation(rs[:, 0:T], rs[:, 0:T], AF.Exp, scale=-0.5), reads=[b_rs], writes=[b_rs])
                    for m in range(2):
                        S.op("dve", lambda m=m: nc.vector.scalar_tensor_tensor(cqnT[:, m, sl], pb[m][:, 0:T], qn[:, m:m + 1], rs[:, 0:T],
                                                                               ALU.mult, ALU.mult), reads=[bp[m], bqn, b_rs], writes=[bcqn])

            def add_x(pa, ba, o, sl, T):
                tb = bx[sl.start // 512]
                S.op("dve", lambda: nc.vector.scalar_tensor_tensor(xT[:, o, sl], pa[:, 0:T], 1.0, xT[:, o, sl], ALU.mult, ALU.add),
                     reads=[ba, tb], writes=[tb])

            def conv_path():
                with ExitStack() as cl:
                    conv_path_(cl)
                    S.barrier()

            def conv_path_(cl):
                upad = sb(cl, "upad", [128, 1056], F32); bup = Buf()
                Vs = [sb(cl, "Vs%d" % i, [128, 512], F32) for i in range(2)]; bVs = [Buf(), Buf()]
                Bs = [sb(cl, "Bs%d" % i, [128, 512], F32) for i in range(2)]; bBs = [Buf(), Buf()]
                cacc = [sb(cl, "cacc%d" % i, [128, 512], F32) for i in range(2)]; bca = [Buf(), Buf()]
                zb = [sb(cl, "zb%d" % i, [128, 512], BF16) for i in range(2)]; bzb = [Buf(), Buf()]
                wcv = [sb(cl, "wcv%d" % i, [128, 8, 3, 128], BF16) for i in range(2)]; bwcv = [Buf(), Buf()]
                up3 = upad[:, 0:264].rearrange("p (q t) -> p q t", q=4)
                ucnt = [0]

                def part1(cc, t, T, wv_, bwv_):
                    u = ucnt[0] % 2
                    ucnt[0] += 1
                    sl = slice(t * 512, t * 512 + T)
                    for j in range(3):
                        S.group("pe", [(lambda k=k: nc.tensor.matmul(pb[j][:, 0:T], wv_[:, k, j, :], hT[:, k, sl], start=(k == 0), stop=(k == 7)))
                                       for k in range(8)], reads=[bwv_, bh[t]], writes=[bp[j]])
                    S.op("act", lambda: nc.scalar.copy(Vs[u][:, 0:T], pb[2][:, 0:T]), reads=[bp[2]], writes=[bVs[u]])
                    S.op("act", lambda: nc.scalar.copy(Bs[u][:, 0:T], pb[0][:, 0:T]), reads=[bp[0]], writes=[bBs[u]])
                    if kind == "p":
                        S.op("dve", lambda: nc.vector.tensor_tensor(upad[:, 2 + t * 512:2 + t * 512 + T], pb[1][:, 0:T], Vs[u][:, 0:T], ALU.mult),
                             reads=[bp[1], bVs[u]], writes=[bup])
                        srcs = [upad[:, t * 512 + k:t * 512 + k + T] for k in range(3)]
                        dsts = cacc[u][:, 0:T]
                    else:
                        S.op("dve", lambda: nc.vector.tensor_tensor(up3[:, :, 2:66], pb[1][:, 0:256].rearrange("p (q t) -> p q t", q=4),
                                                                    Vs[u][:, 0:256].rearrange("p (q t) -> p q t", q=4), ALU.mult),
                             reads=[bp[1], bVs[u]], writes=[bup])
                        srcs = [up3[:, :, k:k + 64] for k in range(3)]
                        dsts = cacc[u][:, 0:256].rearrange("p (q t) -> p q t", q=4)
                    S.op("dve", lambda: nc.vector.tensor_scalar(dsts, srcs[0], cw[:, cc, 0:1], None, ALU.mult), reads=[bup, bcw], writes=[bca[u]])
                    S.op("dve", lambda: nc.vector.scalar_tensor_tensor(dsts, srcs[1], cw[:, cc, 1:2], dsts, ALU.mult, ALU.add),
                         reads=[bup, bcw, bca[u]], writes=[bca[u]])
                    S.op("dve", lambda: nc.vector.scalar_tensor_tensor(dsts, srcs[2], cw[:, cc, 2:3], dsts, ALU.mult, ALU.add),
                         reads=[bup, bcw, bca[u]], writes=[bca[u]])
                    S.op("dve", lambda: nc.vector.tensor_tensor(zb[u][:, 0:T], cacc[u][:, 0:T], Bs[u][:, 0:T], ALU.mult), reads=[bca[u], bBs[u]], writes=[bzb[u]])

                    def part2():
                        for o in range(8):
                            pa, ba = pb[4 + o % 2], bp[4 + o % 2]
                            S.group("pe", [lambda: nc.tensor.matmul(pa[:, 0:T], wout[:, 4 + cc, o * 128:(o + 1) * 128], zb[u][:, 0:T], start=True, stop=True)],
                                    reads=[bwout, bzb[u]], writes=[ba])
                            add_x(pa, ba, o, sl, T)
                    return part2
                pend = None
                for cc in range(4):
                    wv_, bwv_ = wcv[cc % 2], bwcv[cc % 2]
                    for j, c0 in enumerate((544, 1056, 1568)):
                        S.dma("pool", wv_[:, :, j, :], win[:, :, c0 + cc * 128:c0 + (cc + 1) * 128], writes=[bwv_], join=(j > 0))
                    if kind == "p":
                        if hf == 0:
                            S.op("pool", lambda: nc.gpsimd.memset(upad[:, 0:2], 0.0), reads=[bup], writes=[bup])
                        else:
                            S.op("pool", lambda: nc.gpsimd.tensor_copy(upad[:, 0:2], chist[:, cc, :]), reads=[bchist, bup], writes=[bup])
                    else:
                        for r2 in range(2):
                            S.dma("sp", up3[:, :, r2], convc[:, r2, cc * 128:(cc + 1) * 128].rearrange("q p -> p q"), reads=[bup], writes=[bup],
                                  allow_slow_non_contiguous=True)
                    for t, T in tiles:
                        nxt = part1(cc, t, T, wv_, bwv_)
                        if pend is not None:
                            pend()
                        pend = nxt
                    if kind == "p" and hf == 0:
                        S.op("pool", lambda: nc.gpsimd.tensor_copy(chist[:, cc, :], upad[:, 1024:1026]), reads=[bup], writes=[bchist])
                pend()

            def build_kv(hp, nk):
                nkt = (nk + 127) // 128
                for e in range(2):
                    h = 2 * hp + e
                    for k0 in range(0, nk, 512):
                        n = min(512, nk - k0)
                        pk, bk = pb[cnt["s"] % 2], bp[cnt["s"] % 2]
                        cnt["s"] += 1
                        S.group("pe", [(lambda kc=kc: nc.tensor.matmul(pk[0:64, 0:n], wukv[:, kc, h * 128:h * 128 + 64], latT[:, kc, k0:k0 + n],
                                                                       start=(kc == 0), stop=(kc == 1))) for kc in range(2)],
                                reads=[bwukv, blatT], writes=[bk])
                        if e == 0:
                            S.op("act", lambda: nc.scalar.copy(KT[0:64, e, k0:k0 + n], pk[0:64, 0:n]), reads=[bk], writes=[bKT])
                        else:
                            S.op("dve", lambda: nc.vector.tensor_copy(KT[0:64, e, k0:k0 + n], pk[0:64, 0:n]), reads=[bk], writes=[bKT])
                    S.op("pool", lambda: nc.gpsimd.tensor_copy(KT[64:96, e, 0:nk], krT[64:96, 0:nk]), reads=[bkrT], writes=[bKT])
                wv3 = wukv[:, :, :].rearrange("p k (h d) -> p k h d", d=128)
                for g0 in range(0, nkt, 4):
                    g1 = min(nkt, g0 + 4)
                    pk, bk = pb[cnt["s"] % 2], bp[cnt["s"] % 2]
                    cnt["s"] += 1
                    fns = []
                    for kt in range(g0, g1):
                        nkk = min(128, nk - kt * 128)
                        for kc in range(2):
                            fns.append(lambda kt=kt, kc=kc, nkk=nkk: nc.tensor.matmul(
                                pk[0:nkk, (kt - g0) * 128:(kt - g0 + 1) * 128], latT[:, kc, kt * 128:kt * 128 + nkk],
                                wv3[:, kc, 2 * hp:2 * hp + 2, 64:128], start=(kc == 0), stop=(kc == 1)))
                    S.group("pe", fns, reads=[bwukv, blatT], writes=[bk])
                    pk3 = pk[:, :].rearrange("p (t c) -> p t c", c=128)
                    ng = g1 - g0
                    S.op("act", lambda: nc.scalar.copy(Vp[:, 0, g0:g1, 0:64], pk3[:, 0:ng, 0:64]), reads=[bk], writes=[bVp, bvblock])
                    S.op("dve", lambda: nc.vector.tensor_copy(Vp[:, 1, g0:g1, 64:128], pk3[:, 0:ng, 64:128]), reads=[bk], writes=[bVp, bvblock])

            def build_qt(hp, qs, T, pos0, qb):
                QT, bQT = QTs[qb], bQTs[qb]
                for e in range(2):
                    h = 2 * hp + e
                    S.group("pe", [(lambda kc=kc: nc.tensor.matmul(pb[4][0:96, 0:T], wuq[:, kc, h * 96:(h + 1) * 96], cqnT[:, kc, qs],
                                                                   start=(kc == 0), stop=(kc == 1))) for kc in range(2)],
                            reads=[bwuq, bcqn], writes=[bp[4]])
                    S.group("pe", [(lambda kc=kc: nc.tensor.matmul(pb[5][0:96, 0:T], wsw[:, kc, h * 96:(h + 1) * 96], cqnT[:, kc, qs],
                                                                   start=(kc == 0), stop=(kc == 1))) for kc in range(2)],
                            reads=[bwsw, bcqn], writes=[bp[5]])
                    S.op("act", lambda: nc.scalar.copy(QT[0:64, e, 0:T], pb[4][0:64, 0:T]), reads=[bp[4]], writes=[bQT, bqblock])
                    S.op("dve", lambda: nc.vector.tensor_tensor(tq1[64:96, 0:T], pb[4][64:96, 0:T], cosT[64:96, pos0:pos0 + T], ALU.mult),
                         reads=[bp[4], btab], writes=[btq1, bqblock])
                    S.op("dve", lambda: nc.vector.tensor_tensor(tq2[64:96, 0:T], pb[5][64:96, 0:T], sinT[64:96, pos0:pos0 + T], ALU.mult),
                         reads=[bp[5], btab], writes=[btq2])
                    S.op("pool", lambda: nc.gpsimd.tensor_tensor(QT[64:96, e, 0:T], tq1[64:96, 0:T], tq2[64:96, 0:T], ALU.add),
                         reads=[btq1, btq2], writes=[bQT])

            def attend(hp, qs, T, ktiles, qb, prefetch=None):
                QT, bQT = QTs[qb], bQTs[qb]
                steps = [(e, kt, nkk, c0, diag) for e in range(2) for (kt, nkk, c0, diag) in ktiles]
                slots = []

                def s_mm(i):
                    e, kt, nkk, c0, diag = steps[i]
                    pS, bS = pb[cnt["s"] % 2], bp[cnt["s"] % 2]
                    cnt["s"] += 1
                    r = cnt["pt"] % 3
                    cnt["pt"] += 1
                    S.group("pe", [lambda: nc.tensor.matmul(pS[0:nkk, c0:T], KT[0:96, e, kt * 128:kt * 128 + nkk], QT[0:96, e, c0:T],
                                                            start=True, stop=True)], reads=[bKT, bQT], writes=[bS])
                    slots.append((pS, bS, r))
                s_mm(0)
                for i in range(len(steps)):
                    if i + 1 < len(steps):
                        s_mm(i + 1)
                    if i == 1 and prefetch is not None:
                        prefetch()
                        prefetch = None
                    e, kt, nkk, c0, diag = steps[i]
                    pS, bS, r = slots[i]
                    S.op("act", lambda: nc.scalar.activation(PT[r][0:nkk, c0:T], pS[0:nkk, c0:T], AF.Exp, scale=SC), reads=[bS], writes=[bPT[r]])
                    if diag:
                        S.op("pool", lambda: nc.gpsimd.memset(PT[r][64:128, c0:c0 + 64], 0.0), reads=[bPT[r]], writes=[bPT[r]])
                    first = (i == 0)
                    S.group("pe", [lambda: nc.tensor.matmul(pb[2][:, c0:T], Vp[0:nkk, e, kt, :], PT[r][0:nkk, c0:T], start=first, stop=False),
                                   lambda: nc.tensor.matmul(pb[3][:, c0:T], onesP[0:nkk, e, :], PT[r][0:nkk, c0:T], start=first, stop=False)],
                            reads=[bVp, bPT[r], b_const], writes=[bp[2], bp[3]])
                S.group("pe", [lambda: nc.tensor.matmul(pb[2][:, 0:T], zerosb[0:nkk, :], PT[r][0:nkk, 0:T], start=False, stop=True),
                               lambda: nc.tensor.matmul(pb[3][:, 0:T], zerosb[0:nkk, :], PT[r][0:nkk, 0:T], start=False, stop=True)],
                        reads=[bPT[r], b_const], writes=[bp[2], bp[3]])
                S.op("act", lambda: nc.scalar.activation(rec[:, 0:T], pb[3][:, 0:T], AF.Ln), reads=[bp[3]], writes=[brec])
                S.op("act", lambda: nc.scalar.activation(rec[:, 0:T], rec[:, 0:T], AF.Exp, scale=-1.0), reads=[brec], writes=[brec])
                if prefetch is not None:
                    prefetch()
                S.op("dve", lambda: nc.vector.tensor_tensor(attT[:, hp, qs], pb[2][:, 0:T], rec[:, 0:T], ALU.mult), reads=[bp[2], brec], writes=[battT[hp]])

            def wout_attn(qs, T):
                for o in range(8):
                    pa, ba = pb[4 + o % 2], bp[4 + o % 2]
                    S.group("pe", [(lambda hp=hp: nc.tensor.matmul(pa[:, 0:T], wout[:, hp, o * 128:(o + 1) * 128], attT[:, hp, qs],
                                                                   start=(hp == 0), stop=(hp == 3))) for hp in range(4)],
                            reads=[bwout] + battT, writes=[ba])
                    add_x(pa, ba, o, qs, T)

            make_cqn()
            if kind == "p":
                pend = None
                for i in range(8):
                    g0 = hf * 1024 + i * 128
                    nxt = kv_unit(i * 128, 128, g0, o_lat_p[s, g0:g0 + 128, :], o_kr_p[s, g0:g0 + 128, :], g0)
                    if pend is not None:
                        pend()
                    pend = nxt
                pend()
                nk = (hf + 1) * 1024
                blocks = [(hp, qi) for hp in range(4) for qi in range(2)]
                build_qt(0, slice(0, 512), 512, 0, 0)
                for bi, (hp, qi) in enumerate(blocks):
                    if qi == 0:
                        build_kv(hp, nk)
                    I = 2 * hf + qi
                    kts = [(kt, 128, 0, False) for kt in range(4 * I)] + [(4 * I + j, 128, j * 128, True) for j in range(4)]
                    pf = None
                    if bi + 1 < len(blocks):
                        nhp, nqi = blocks[bi + 1]
                        pf = (lambda nhp=nhp, nqi=nqi, nb_=(bi + 1) % 2: build_qt(nhp, slice(nqi * 512, (nqi + 1) * 512), 512, nqi * 512, nb_))
                    attend(hp, slice(qi * 512, (qi + 1) * 512), 512, kts, bi % 2, pf)
                for qi in range(2):
                    wout_attn(slice(qi * 512, (qi + 1) * 512), 512)
            else:
                clat = [sb(c, "clat%d" % i, [128, 256], BF16) for i in range(2)]; bclat = [Buf(), Buf()]
                for q in range(4):
                    for i in range(8):
                        a = i % 2
                        S.dma("pool", clat[a][:], latc[q, i * 128:(i + 1) * 128, :], writes=[bclat[a]])
                        S.dma("pool", krb[a][:, 64:96], krc[q, i * 128:(i + 1) * 128, :], writes=[bkrb[a]])
                        keys_from_tok(clat[a], bclat[a], krb[a], bkrb[a], 128, i * 128)
                    kv_unit(q * 64, 64, 2048, o_lat_s[q], o_kr_s[q], 1024)()
                    kts = [(kt, 128, 0, False) for kt in range(8)] + [(8, 64, 0, False)]
                    for hp in range(4):
                        build_kv(hp, 1088)
                        build_qt(hp, slice(q * 64, q * 64 + 64), 64, 0, hp % 2)
                        attend(hp, slice(q * 64, q * 64 + 64), 64, kts, hp % 2)
                    wout_attn(slice(q * 64, q * 64 + 64), 64)
            conv_path()
            if kind == "p" and hf == 0:
                return
            wC = sb(c, "wC", [128, 8, 512], BF16); bwC = Buf()
            wH = sb(c, "wH", [128, 8, 512], BF16); bwH = Buf()
            S.dma("pool", wC[:], win[:, :, 1056:1568], writes=[bwC])
            S.dma("pool", wH[:], win[:, :, 1568:2080], writes=[bwH])
            cu = sb(c, "cu", [2, 512], F32); bcu = Buf()
            cv = sb(c, "cv", [2, 512], F32); bcv = Buf()
            lasts = [(1022, o_conv_p[s])] if kind == "p" else [(q * 64 + 62, o_conv_s[q]) for q in range(4)]
            for tok0, dconv in lasts:
                tb = bh[tok0 // 512]
                S.group("pe", [(lambda k=k: nc.tensor.matmul(pb[0][0:2, :], hT[:, k, tok0:tok0 + 2], wC[:, k, :],
                                                             start=(k == 0), stop=(k == 7))) for k in range(8)],
                        reads=[tb, bwC], writes=[bp[0]])
                S.group("pe", [(lambda k=k: nc.tensor.matmul(pb[1][0:2, :], hT[:, k, tok0:tok0 + 2], wH[:, k, :],
                                                             start=(k == 0), stop=(k == 7))) for k in range(8)],
                        reads=[tb, bwH], writes=[bp[1]])
                S.op("act", lambda: nc.scalar.copy(cv[:, :], pb[1][0:2, :]), reads=[bp[1]], writes=[bcv])
                S.op("dve", lambda: nc.vector.tensor_tensor(cu[:, :], pb[0][0:2, :], cv[:, :], ALU.mult), reads=[bp[0], bcv], writes=[bcu])
                S.dma("sp", dconv, cu[:, :], reads=[bcu])

        s5w = nc.dram_tensor("s5w", [8, 128, 5248], BF16, kind="Internal").ap()
        bs5w = Buf("s5w")
        s5c = sb(ctx, "s5c", [128, 7, 2, 32], F32); bs5c = Buf("s5c")
        pw8 = sb(ctx, "pw8", [128, 2, 32, 8], F32)

        def s5_setup(c):
            PI = 3.141592653589793
            bS = Buf("s5pre")
            nat = lambda name: sb(c, name, [128, 32], F32)

            def dv(fn, extra_r=(), extra_w=()):
                S.op("dve", fn, reads=[bS] + list(extra_r), writes=[bS] + list(extra_w))

            def ac(fn):
                S.op("act", fn, reads=[bS], writes=[bS])
            are, aim, ldt = nat("are"), nat("aim"), nat("ldt")
            for g2 in range(2):
                rows = slice(g2 * 64, (g2 + 1) * 64)
                S.dma("sp", are[rows, :], W['ssm_a_re'][0].rearrange("(j g) p -> g p j", g=2)[g2], writes=[bS], allow_slow_non_contiguous=True)
                S.dma("sp", aim[rows, :], W['ssm_a_im'][0].rearrange("(j g) p -> g p j", g=2)[g2], writes=[bS], allow_slow_non_contiguous=True)
                S.dma("sp", ldt[rows, :], W['ssm_log_dt'][0].rearrange("(j g) -> g j", g=2)[g2].partition_broadcast(64), writes=[bS],
                      allow_slow_non_contiguous=True)
            t1, t2, t3, t4 = nat("t1"), nat("t2"), nat("t3"), nat("t4")
            zr, zi, mag = nat("zr"), nat("zi"), nat("mag")
            ac(lambda: nc.scalar.activation(ldt[:], ldt[:], AF.Exp))
            dv(lambda: nc.vector.tensor_tensor(zr[:], are[:], ldt[:], ALU.mult))
            dv(lambda: nc.vector.tensor_tensor(zi[:], aim[:], ldt[:], ALU.mult))
            pw = sb(c, "pw", [128, 9, 2, 32], F32)
            dp = s5c
            cr_, ci_ = nat("cr"), nat("ci")
            halfpi = sb(c, "halfpi", [128, 1], F32)
            S.op("pool", lambda: nc.gpsimd.memset(halfpi[:], PI / 2), writes=[bS])
            ac(lambda: nc.scalar.activation(mag[:], zr[:], AF.Exp, scale=1.0 / 16.0))
            ac(lambda: nc.scalar.activation(cr_[:], zi[:], AF.Sin, bias=halfpi[:, 0:1], scale=-1.0 / 16.0))
            ac(lambda: nc.scalar.activation(ci_[:], zi[:], AF.Sin, scale=1.0 / 16.0))
            dv(lambda: nc.vector.tensor_tensor(cr_[:], cr_[:], mag[:], ALU.mult))
            dv(lambda: nc.vector.tensor_tensor(ci_[:], ci_[:], mag[:], ALU.mult))

            def cmul(or_, oi_, ar, ai, br, bi):
                dv(lambda: nc.vector.tensor_tensor(t1[:], ar, br, ALU.mult))
                dv(lambda: nc.vector.tensor_tensor(t2[:], ai, bi, ALU.mult))
                dv(lambda: nc.vector.tensor_tensor(t3[:], ar, bi, ALU.mult))
                dv(lambda: nc.vector.tensor_tensor(t4[:], ai, br, ALU.mult))
                dv(lambda: nc.vector.tensor_tensor(or_, t1[:], t2[:], ALU.subtract))
                dv(lambda: nc.vector.tensor_tensor(oi_, t3[:], t4[:], ALU.add))
            for _ in range(3):
                cmul(cr_[:], ci_[:], cr_[:], ci_[:], cr_[:], ci_[:])
            cmul(pw[:, 1, 0, :], pw[:, 1, 1, :], cr_[:], ci_[:], cr_[:], ci_[:])
            S.op("pool", lambda: nc.gpsimd.memset(pw[:, 0, 0, :], 1.0), reads=[bS], writes=[bS])
            S.op("pool", lambda: nc.gpsimd.memset(pw[:, 0, 1, :], 0.0), reads=[bS], writes=[bS])
            for k in range(2, 9):
                cmul(pw[:, k, 0, :], pw[:, k, 1, :], pw[:, k - 1, 0, :], pw[:, k - 1, 1, :], pw[:, 1, 0, :], pw[:, 1, 1, :])
            dv(lambda: nc.vector.tensor_copy(dp[:, 0, :, :], pw[:, 8, :, :]), extra_w=[bs5c])
            dv(lambda: nc.vector.tensor_copy(dp[:, 6, :, :], pw[:, 8, :, :]), extra_w=[bs5c])
            for l in range(1, 6):
                cmul(dp[:, l, 0, :], dp[:, l, 1, :], dp[:, l - 1, 0, :], dp[:, l - 1, 1, :], dp[:, l - 1, 0, :], dp[:, l - 1, 1, :])
            dv(lambda: nc.vector.tensor_copy(pw8[:, :, :, 0], pw[:, 8, :, :]), extra_w=[bs5c])
            for b_ in range(1, 8):
                cmul(pw8[:, 0, :, b_], pw8[:, 1, :, b_], pw8[:, 0, :, b_ - 1], pw8[:, 1, :, b_ - 1], pw[:, 8, 0, :], pw[:, 8, 1, :])
            dv(lambda: nc.vector.tensor_copy(t1[:], t1[:]), extra_w=[bs5c])
            nr, m2 = nat("nr"), nat("m2")
            dv(lambda: nc.vector.tensor_scalar(nr[:], pw[:, 1, 0, :], -1.0, None, ALU.add))
            dv(lambda: nc.vector.tensor_tensor(t1[:], are[:], are[:], ALU.mult))
            dv(lambda: nc.vector.tensor_tensor(t2[:], aim[:], aim[:], ALU.mult))
            dv(lambda: nc.vector.tensor_tensor(m2[:], t1[:], t2[:], ALU.add))
            dv(lambda: nc.vector.reciprocal(m2[:], m2[:]))
            dv(lambda: nc.vector.tensor_tensor(t1[:], nr[:], are[:], ALU.mult))
            dv(lambda: nc.vector.tensor_tensor(t2[:], pw[:, 1, 1, :], aim[:], ALU.mult))
            dv(lambda: nc.vector.tensor_tensor(t1[:], t1[:], t2[:], ALU.add))
            dv(lambda: nc.vector.tensor_tensor(cr_[:], t1[:], m2[:], ALU.mult))
            dv(lambda: nc.vector.tensor_tensor(t1[:], pw[:, 1, 1, :], are[:], ALU.mult))
            dv(lambda: nc.vector.tensor_tensor(t2[:], nr[:], aim[:], ALU.mult))
            dv(lambda: nc.vector.tensor_tensor(t1[:], t1[:], t2[:], ALU.subtract))
            dv(lambda: nc.vector.tensor_tensor(ci_[:], t1[:], m2[:], ALU.mult))
            bbd = sb(c, "bbd", [128, 2, 32, 32], F32)
            CCn = sb(c, "CCn", [128, 2, 32, 32], F32)
            bbz = sb(c, "bbz", [128, 32, 2, 128], BF16)
            with ExitStack() as c2:
                Bn = sb(c2, "Bn", [128, 2, 32, 16], F32)
                for g2 in range(2):
                    rows = slice(g2 * 64, (g2 + 1) * 64)
                    S.dma("sp", Bn[rows, 0, :, :], W['ssm_b_re'][0].rearrange("(j g) p c -> g p j c", g=2)[g2], writes=[bS])
                    S.dma("sp", Bn[rows, 1, :, :], W['ssm_b_im'][0].rearrange("(j g) p c -> g p j c", g=2)[g2], writes=[bS])
                tb1 = sb(c2, "tb1", [128, 32, 16], F32); tb2 = sb(c2, "tb2", [128, 32, 16], F32)
                S.op("pool", lambda: nc.gpsimd.memset(bbd[:], 0.0), reads=[bS], writes=[bS])
                crb = cr_[:, :].unsqueeze(2).broadcast_to([128, 32, 16])
                cib = ci_[:, :].unsqueeze(2).broadcast_to([128, 32, 16])
                for ri in range(2):
                    dv(lambda: nc.vector.tensor_tensor(tb1[:], Bn[:, ri, :, :], crb, ALU.mult))
                    dv(lambda: nc.vector.tensor_tensor(tb2[:], Bn[:, 1 - ri, :, :], cib, ALU.mult))
                    dv(lambda: nc.vector.tensor_tensor(tb1[:], tb1[:], tb2[:], ALU.subtract if ri == 0 else ALU.add))
                    dv(lambda: nc.vector.tensor_copy(bbd[0:64, ri, :, 0:16], tb1[0:64, :, :]))
                    dv(lambda: nc.vector.tensor_copy(bbd[64:128, ri, :, 16:32], tb1[64:128, :, :]))
                Sc = sb(c2, "Sc", [32, 2, 32, 128], F32)
                S.op("pool", lambda: nc.gpsimd.memset(Sc[:], 0.0), reads=[bS], writes=[bS])
                for g2 in range(2):
                    S.dma("sp", Sc[g2 * 16:(g2 + 1) * 16, 0, :, g2 * 64:(g2 + 1) * 64], W['ssm_c_re'][0].rearrange("(j g) c p -> g c j p", g=2)[g2], writes=[bS])
                    S.dma("sp", Sc[g2 * 16:(g2 + 1) * 16, 1, :, g2 * 64:(g2 + 1) * 64], W['ssm_c_im'][0].rearrange("(j g) c p -> g c j p", g=2)[g2], writes=[bS])
                for ri in range(2):
                    for jb in range(2):
                        S.group("pe", [(lambda jj=jj: nc.tensor.transpose(pb[jb][:, jj * 32:(jj + 1) * 32], Sc[:, ri, jb * 16 + jj, :], ident_f[0:32, 0:32]))
                                       for jj in range(16)], reads=[bS, b_const], writes=[bp[jb]])
                        S.op("act", lambda: nc.scalar.copy(CCn[:, ri, jb * 16:(jb + 1) * 16, :], pb[jb][:, :].rearrange("p (j m) -> p j m", m=32)),
                             reads=[bp[jb]], writes=[bS])
                S.barrier()
            S.op("pool", lambda: nc.gpsimd.memset(bbz[:], 0.0), reads=[bS], writes=[bS])
            bbz5 = bbz[:, :, :, :].rearrange("p j r (q m) -> p j r q m", q=4)
            for ri in range(2):
                for slot in range(4):
                    dv(lambda: nc.vector.tensor_copy(bbz5[:, slot:32:4, ri, slot, :], bbd[:, ri, slot:32:4, :]))
            Wr = sb(c, "Wr", [128, 32, 32], F32); Wi = sb(c, "Wi", [128, 32, 32], F32); Wt = sb(c, "Wt", [128, 32, 32], F32)
            Wrb = sb(c, "Wrb", [128, 32, 32], BF16); Wib = sb(c, "Wib", [128, 32, 32], BF16)
            stg = [sb(c, "stg%d" % i, [128, 8, 256], BF16) for i in range(2)]; bstg = [Buf(), Buf()]
            s5v = s5w.rearrange("c p x -> p c x")
            nst = [0]

            def bc32(ap2):
                return ap2.unsqueeze(2).broadcast_to([128, 32, 32])

            def cplx(xr, xi, k, neg_im):
                pr, pi_ = bc32(pw[:, k, 0, :]), bc32(pw[:, k, 1, :])
                dv(lambda: nc.vector.tensor_tensor(Wr[:], xr, pr, ALU.mult))
                dv(lambda: nc.vector.tensor_tensor(Wt[:], xi, pi_, ALU.mult))
                dv(lambda: nc.vector.tensor_tensor(Wr[:], Wr[:], Wt[:], ALU.subtract))
                dv(lambda: nc.vector.tensor_tensor(Wi[:], xr, pi_, ALU.mult))
                dv(lambda: nc.vector.tensor_tensor(Wt[:], xi, pr, ALU.mult))
                dv(lambda: nc.vector.tensor_tensor(Wi[:], Wi[:], Wt[:], ALU.add))
                if neg_im:
                    dv(lambda: nc.vector.tensor_scalar(Wi[:], Wi[:], -1.0, None, ALU.mult))
            for s_ in range(8):
                cplx(bbd[:, 0, :, :], bbd[:, 1, :, :], 7 - s_, False)
                a = nst[0] % 2
                nst[0] += 1
                for ri, Wx in ((0, Wr), (1, Wi)):
                    for half in range(2):
                        S.group("pe", [(lambda q=q: nc.tensor.transpose(
                            pb[half][:, q * 128:(q + 1) * 128],
                            Wx[:, 4 * (half * 4 + q):4 * (half * 4 + q) + 4, :].rearrange("p j c -> p (j c)"), ident_f[:]))
                            for q in range(4)], reads=[bS, b_const], writes=[bp[half]])
                        S.op("act", lambda: nc.scalar.copy(stg[a][:, half * 4:(half + 1) * 4, ri * 128:(ri + 1) * 128],
                                                           pb[half][:, :].rearrange("p (q m) -> p q m", q=4)),
                             reads=[bp[half]], writes=[bstg[a], bS])
                S.dma("sp", s5v[:, :, s_ * 256:(s_ + 1) * 256], stg[a][:, :, :], reads=[bstg[a]], writes=[bs5w])
            for k in range(9):
                cplx(CCn[:, 0, :, :], CCn[:, 1, :, :], k, True)
                if k >= 1:
                    e = k - 1
                    a = nst[0] % 2
                    nst[0] += 1
                    stv = stg[a][:, :, 0:256].rearrange("p c (j r m) -> p c j r m", j=4, r=2)
                    for ri, Wx in ((0, Wr), (1, Wi)):
                        S.op("act", lambda: nc.scalar.copy(stv[:, :, :, ri, :], Wx[:, :, :].rearrange("p (c j) m -> p c j m", j=4)),
                             reads=[bS], writes=[bstg[a]])
                    for j in range(4):
                        S.dma("sp", s5v[:, :, 2048 + j * 512 + e * 64:2048 + j * 512 + (e + 1) * 64], stg[a][:, :, j * 64:(j + 1) * 64],
                              reads=[bstg[a]], writes=[bs5w])
                if k <= 7:
                    a = nst[0] % 2
                    nst[0] += 1
                    dv(lambda: nc.vector.tensor_copy(Wrb[:], Wr[:]))
                    dv(lambda: nc.vector.tensor_copy(Wib[:], Wi[:]))
                    for cp in range(2):
                        fns = []
                        for q in range(4):
                            ch = cp * 4 + q
                            for j in range(4):
                                for ri, Wx in ((0, Wrb), (1, Wib)):
                                    fns.append(lambda q=q, ch=ch, j=j, ri=ri, Wx=Wx: nc.tensor.matmul(
                                        pb[2 + cp][:, q * 128 + j * 32:q * 128 + (j + 1) * 32], bbz[:, 4 * ch + j, ri, :], Wx[:, 4 * ch + j, :],
                                        start=(ri == 0), stop=(ri == 1)))
                        S.group("pe", fns, reads=[bS], writes=[bp[2 + cp]])
                        S.op("act", lambda: nc.scalar.copy(stg[a][:, cp * 4:(cp + 1) * 4, 0:128], pb[2 + cp][:, :].rearrange("p (q m) -> p q m", q=4)),
                             reads=[bp[2 + cp]], writes=[bstg[a]])
                    S.dma("sp", s5v[:, :, 4096 + k * 128:4096 + (k + 1) * 128], stg[a][:, :, 0:128], reads=[bstg[a]], writes=[bs5w])
            dsk = sb(c, "dsk", [128, 8], F32)
            S.dma("sp", dsk[:], W['ssm_d'][0].rearrange("(k p) -> p k", p=128), writes=[bS], allow_slow_non_contiguous=True)
            a = nst[0] % 2
            for ch in range(8):
                S.op("dve", lambda ch=ch: nc.vector.tensor_scalar(stg[a][:, ch, 0:128], ident_f[:], dsk[:, ch:ch + 1], None, ALU.mult),
                     reads=[bS, b_const], writes=[bstg[a]])
            S.dma("sp", s5v[:, :, 5120:5248], stg[a][:, :, 0:128], reads=[bstg[a]], writes=[bs5w])

        def odd_mixer(kind, s, hf, tiles, c):
            norm_to_h(G_MIX + 1, tiles)
            nseq = 1 if kind == "p" else 4
            if kind == "p":
                if hf == 0:
                    S.op("pool", lambda: nc.gpsimd.memset(hstate[:], 0.0), writes=[bhst])
            else:
                for q in range(4):
                    for g2 in range(2):
                        rows = slice(g2 * 64, (g2 + 1) * 64)
                        S.dma("sp", hstate[rows, 0, :, q], ssre[q].rearrange("(j g) p -> g p j", g=2)[g2], writes=[bhst], allow_slow_non_contiguous=True)
                        S.dma("sp", hstate[rows, 1, :, q], ssim[q].rearrange("(j g) p -> g p j", g=2)[g2], writes=[bhst], allow_slow_non_contiguous=True)
            uT = sb(c, "uT", [128, 8, 1024], BF16); buT = Buf()
            with ExitStack() as c1:
                wio = sb(c1, "wio", [128, 8, 1024], BF16); bwio = Buf()
                S.dma("pool", wio[:], W['w_in_odd'][0].rearrange("(k p) n -> p k n", p=128), writes=[bwio])
                for t, T in tiles:
                    sl = slice(t * 512, t * 512 + T)
                    for m in range(8):
                        pu, bu_ = pb[m % 2], bp[m % 2]
                        S.group("pe", [(lambda k=k: nc.tensor.matmul(pu[:, 0:T], wio[:, k, m * 128:(m + 1) * 128], hT[:, k, sl], start=(k == 0), stop=(k == 7)))
                                       for k in range(8)], reads=[bwio, bh[t]], writes=[bu_])
                        if m % 2 == 0:
                            S.op("act", lambda: nc.scalar.copy(uT[:, m, sl], pu[:, 0:T]), reads=[bu_], writes=[buT])
                        else:
                            S.op("dve", lambda: nc.vector.tensor_copy(uT[:, m, sl], pu[:, 0:T]), reads=[bu_], writes=[buT])
                S.barrier()
            EA = sb(c, "EA", [128, 2, 32, 72], F32); EB = sb(c, "EB", [128, 2, 32, 72], F32)
            GA = sb(c, "GA", [128, 2, 32, 9], F32); GB = sb(c, "GB", [128, 2, 32, 9], F32)
            bE = {id(EA): [Buf(), Buf()], id(EB): [Buf(), Buf()], id(GA): [Buf(), Buf()], id(GB): [Buf(), Buf()]}
            tD = sb(c, "tD", [128, 32, 64], F32); btD = Buf()
            tP = sb(c, "tP", [128, 32, 64], F32); btP = Buf()
            tP2 = sb(c, "tP2", [128, 32, 64], F32); btP2 = Buf()
            Hpb = sb(c, "Hpb", [128, 2, 32, 64], BF16); bHp = Buf()
            wch = [sb(c, "wch%d" % i, [128, 5248], BF16) for i in range(2)]; bwch = [Buf(), Buf()]
            y2 = sb(c, "y2", [128, 512], F32); by2 = Buf()
            sgm = sb(c, "sgm", [128, 512], F32); bsg = Buf()
            wgl = [sb(c, "wgl%d" % i, [128, 8, 2, 128], BF16) for i in range(2)]; bwgl = [Buf(), Buf()]
            wglu_d = W['w_glu'][0].rearrange("(k p) n -> p k n", p=128)
            pending_glu = []
            def glu_all(tl):
                cnt_ = 0
                for oc in range(8):
                    wg, bwg = wgl[oc % 2], bwgl[oc % 2]
                    S.dma("pool", wg[:, :, 0, :], wglu_d[:, :, oc * 128:(oc + 1) * 128], writes=[bwg])
                    S.dma("pool", wg[:, :, 1, :], wglu_d[:, :, 1024 + oc * 128:1024 + (oc + 1) * 128], writes=[bwg], join=True)
                    for (t, T, sl) in tl:
                        pv_, bv_ = pb[2 * (cnt_ % 2)], bp[2 * (cnt_ % 2)]
                        pg_, bg_ = pb[2 * (cnt_ % 2) + 1], bp[2 * (cnt_ % 2) + 1]
                        cnt_ += 1
                        S.group("pe", [(lambda k=k: nc.tensor.matmul(pv_[:, 0:T], wg[:, k, 0, :], hT[:, k, sl], start=(k == 0), stop=(k == 7))) for k in range(8)],
                                reads=[bwg, bh[t]], writes=[bv_])
                        S.group("pe", [(lambda k=k: nc.tensor.matmul(pg_[:, 0:T], wg[:, k, 1, :], hT[:, k, sl], start=(k == 0), stop=(k == 7))) for k in range(8)],
                                reads=[bwg, bh[t]], writes=[bg_])
                        S.op("act", lambda: nc.scalar.activation(sgm[:, 0:T], pg_[:, 0:T], AF.Sigmoid), reads=[bg_], writes=[bsg])
                        S.op("dve", lambda: nc.vector.tensor_tensor(y2[:, 0:T], pv_[:, 0:T], sgm[:, 0:T], ALU.mult), reads=[bv_, bsg], writes=[by2])
                        S.op("dve", lambda: nc.vector.tensor_tensor(xT[:, oc, sl], xT[:, oc, sl], y2[:, 0:T], ALU.add), reads=[by2, bx[t]], writes=[bx[t]])
            wc = 0
            for t, T in tiles:
                sl = slice(t * 512, t * 512 + T)
                nb = (T // nseq) // 8
                NB = nseq * nb
                two_level = (nb == 64)
                QG = 8 if two_level else nseq
                W1 = QG * 9

                def vq(Et, ri, q_=None, w_=None):
                    q_ = QG if q_ is None else q_
                    return Et[:, ri, :, 0:q_ * 9].rearrange("p j (q b) -> p j q b", q=q_)
                v5 = vq
                nb = 8
                nseq_ = QG
                cur, oth = EA, EB
                for ch in range(8):
                    w, bw = wch[wc % 2], bwch[wc % 2]
                    wc += 1
                    S.dma("sp", w[:, 0:2048], s5w[ch][:, 0:2048], reads=[bs5w], writes=[bw])
                    WE = w[:, 0:2048].rearrange("p (s r m) -> p s r m", s=8, r=2)
                    fns = []
                    for ri in range(2):
                        for s_ in range(8):
                            for j in range(4):
                                u3 = uT[32 * j:32 * j + 32, ch, sl].rearrange("p (x e) -> p x e", e=8)
                                fns.append(lambda j=j, ri=ri, s_=s_, u3=u3: nc.tensor.matmul(
                                    pb[j][:, ri * NB:(ri + 1) * NB], WE[32 * j:32 * j + 32, s_, ri, :], u3[:, :, s_],
                                    start=(s_ == 0), stop=(s_ == 7), tile_position=(32 * j, 0)))
                    S.group("pe", fns, reads=[bw, buT], writes=[bp[0], bp[1], bp[2], bp[3]])
                    for j in range(4):
                        pE3 = pb[j][:, 0:2 * NB].rearrange("p (r q b) -> p r q b", r=2, q=QG)
                        dst = cur[:, :, 4 * ch + j, 0:W1].rearrange("p r (q b) -> p r q b", q=QG)[:, :, :, 1:9]
                        S.op("act", lambda: nc.scalar.copy(dst, pE3), reads=[bp[j]], writes=[bE[id(cur)][0], bE[id(cur)][1]])
                if os.environ.get("S5CUT") == "A":
                    return
                def first_fix(Et, q_, lev):
                    shp1 = [128, 32, q_]
                    cr_b = s5c[:, lev, 0, :].unsqueeze(2).broadcast_to(shp1)
                    ci_b = s5c[:, lev, 1, :].unsqueeze(2).broadcast_to(shp1)
                    tq = tD[:, :, 0:q_]
                    h0r, h0i = vq(Et, 0, q_)[:, :, :, 0], vq(Et, 1, q_)[:, :, :, 0]
                    e0r, e0i = vq(Et, 0, q_)[:, :, :, 1], vq(Et, 1, q_)[:, :, :, 1]
                    bcr, bci = bE[id(Et)]
                    for (dst, bd, src_, co, op) in ((e0r, bcr, h0r, cr_b, ALU.add), (e0r, bcr, h0i, ci_b, ALU.subtract),
                                                    (e0i, bci, h0i, cr_b, ALU.add), (e0i, bci, h0r, ci_b, ALU.add)):
                        S.op("dve", lambda: nc.vector.tensor_tensor(tq, src_, co, ALU.mult), reads=[bcr, bci, bs5c], writes=[btD])
                        S.op("dve", lambda: nc.vector.tensor_tensor(dst, dst, tq, op), reads=[btD], writes=[bd])

                def scan_levels(cur, oth, q_, lev0):
                    for l in range(3):
                        d = 1 << l
                        shp = [128, 32, q_, 8 - d]
                        bq2 = lambda ap2: ap2.unsqueeze(2).unsqueeze(3).broadcast_to(shp)
                        dr, di = bq2(s5c[:, lev0 + l, 0, :]), bq2(s5c[:, lev0 + l, 1, :])
                        lo = lambda Et, ri: vq(Et, ri, q_)[:, :, :, 1:9 - d]
                        hi = lambda Et, ri: vq(Et, ri, q_)[:, :, :, 1 + d:9]
                        tv = lambda Tt: Tt[:, :, 0:q_ * (8 - d)].rearrange("p j (q b) -> p j q b", q=q_)
                        tdv, tpv, tpv2 = tv(tD), tv(tP), tv(tP2)
                        bcr, bci = bE[id(cur)]
                        bor, boi = bE[id(oth)]
                        for ri in range(2):
                            S.op("act", lambda ri=ri: nc.scalar.copy(vq(oth, ri, q_)[:, :, :, 0:1 + d], vq(cur, ri, q_)[:, :, :, 0:1 + d]),
                                 reads=[bE[id(cur)][ri]], writes=[bE[id(oth)][ri]])
                        S.op("dve", lambda: nc.vector.tensor_tensor(tpv, lo(cur, 1), dr, ALU.mult), reads=[bci, bs5c], writes=[btP])
                        S.op("pool", lambda: nc.gpsimd.tensor_tensor(hi(oth, 1), hi(cur, 1), tpv, ALU.add), reads=[bci, btP], writes=[boi])
                        S.op("dve", lambda: nc.vector.tensor_tensor(tpv2, lo(cur, 0), di, ALU.mult), reads=[bcr, bs5c], writes=[btP2])
                        S.op("pool", lambda: nc.gpsimd.tensor_tensor(hi(oth, 1), hi(oth, 1), tpv2, ALU.add), reads=[btP2], writes=[boi])
                        S.op("dve", lambda: nc.vector.tensor_tensor(tdv, lo(cur, 0), dr, ALU.mult), reads=[bcr, bs5c], writes=[btD])
                        S.op("dve", lambda: nc.vector.tensor_tensor(hi(oth, 0), hi(cur, 0), tdv, ALU.add), reads=[bcr, btD], writes=[bor])
                        S.op("dve", lambda: nc.vector.tensor_tensor(tdv, lo(cur, 1), di, ALU.mult), reads=[bci, bs5c], writes=[btD])
                        S.op("dve", lambda: nc.vector.tensor_tensor(hi(oth, 0), hi(oth, 0), tdv, ALU.subtract), reads=[btD], writes=[bor])
                        cur, oth = oth, cur
                    return cur, oth
                if not two_level:
                    for ri in range(2):
                        S.op("act", lambda ri=ri: nc.scalar.copy(vq(cur, ri)[:, :, :, 0], hstate[:, ri, :, 0:nseq]), reads=[bhst], writes=[bE[id(cur)][ri]])
                    first_fix(cur, QG, 6)
                    cur, oth = scan_levels(cur, oth, QG, 0)
                    for ri in range(2):
                        S.op("act", lambda ri=ri: nc.scalar.copy(hstate[:, ri, :, 0:nseq], vq(cur, ri)[:, :, :, 8]), reads=[bE[id(cur)][ri]], writes=[bhst])
                else:
                    for ri in range(2):
                        S.op("pool", lambda ri=ri: nc.gpsimd.memset(vq(cur, ri)[:, :, :, 0], 0.0), reads=[bE[id(cur)][ri]], writes=[bE[id(cur)][ri]])
                    cur, oth = scan_levels(cur, oth, 8, 0)
                    gc, go = GA, GB
                    for ri in range(2):
                        S.op("act", lambda ri=ri: nc.scalar.copy(gc[:, ri, :, 0], hstate[:, ri, :, 0]), reads=[bhst], writes=[bE[id(gc)][ri]])
                        S.op("act", lambda ri=ri: nc.scalar.copy(gc[:, ri, :, 1:9], vq(cur, ri)[:, :, :, 8]), reads=[bE[id(cur)][ri]], writes=[bE[id(gc)][ri]])
                    first_fix(gc, 1, 3)
                    gc, go = scan_levels(gc, go, 1, 3)
                    for ri in range(2):
                        S.op("act", lambda ri=ri: nc.scalar.copy(vq(cur, ri)[:, :, :, 0], gc[:, ri, :, 0:8]), reads=[bE[id(gc)][ri]], writes=[bE[id(cur)][ri]])
                        S.op("act", lambda ri=ri: nc.scalar.copy(hstate[:, ri, :, 0], gc[:, ri, :, 8]), reads=[bE[id(gc)][ri]], writes=[bhst])
                    shp = [128, 32, 8, 8]
                    pwr = pw8[:, 0, :, :].unsqueeze(2).broadcast_to(shp)
                    pwi = pw8[:, 1, :, :].unsqueeze(2).broadcast_to(shp)
                    gpr = gc[:, 0, :, 0:8].unsqueeze(3).broadcast_to(shp)
                    gpi = gc[:, 1, :, 0:8].unsqueeze(3).broadcast_to(shp)
                    Xr, Xi = vq(cur, 0)[:, :, :, 1:9], vq(cur, 1)[:, :, :, 1:9]
                    tv = lambda Tt: Tt[:, :, 0:64].rearrange("p j (q b) -> p j q b", q=8)
                    tdv, tpv, tpv2 = tv(tD), tv(tP), tv(tP2)
                    bcr, bci = bE[id(cur)]
                    bgr, bgi = bE[id(gc)]
                    S.op("dve", lambda: nc.vector.tensor_tensor(tpv, gpi, pwr, ALU.mult), reads=[bgi, bs5c], writes=[btP])
                    S.op("pool", lambda: nc.gpsimd.tensor_tensor(Xi, Xi, tpv, ALU.add), reads=[btP], writes=[bci])
                    S.op("dve", lambda: nc.vector.tensor_tensor(tpv2, gpr, pwi, ALU.mult), reads=[bgr, bs5c], writes=[btP2])
                    S.op("pool", lambda: nc.gpsimd.tensor_tensor(Xi, Xi, tpv2, ALU.add), reads=[btP2], writes=[bci])
                    S.op("dve", lambda: nc.vector.tensor_tensor(tdv, gpr, pwr, ALU.mult), reads=[bgr, bs5c], writes=[btD])
                    S.op("dve", lambda: nc.vector.tensor_tensor(Xr, Xr, tdv, ALU.add), reads=[btD], writes=[bcr])
                    S.op("dve", lambda: nc.vector.tensor_tensor(tdv, gpi, pwi, ALU.mult), reads=[bgi, bs5c], writes=[btD])
                    S.op("dve", lambda: nc.vector.tensor_tensor(Xr, Xr, tdv, ALU.subtract), reads=[btD], writes=[bcr])
                if os.environ.get("S5CUT") == "S":
                    return
                for ri in range(2):
                    S.op("act", lambda ri=ri: nc.scalar.copy(Hpb[:, ri, :, 0:NB].rearrange("p j (q b) -> p j q b", q=QG), vq(cur, ri)[:, :, :, 0:8]),
                         reads=[bE[id(cur)][ri]], writes=[bHp])
                for ch in range(8):
                    w, bw = wch[wc % 2], bwch[wc % 2]
                    wc += 1
                    S.dma("sp", w[:, 2048:5248], s5w[ch][:, 2048:5248], reads=[bs5w], writes=[bw])
                    CA = w[:, 2048:4096].rearrange("p (j e r m) -> p j e r m", j=4, e=8, r=2)
                    KI = w[:, 4096:5120].rearrange("p (t m) -> p t m", t=8)
                    DG = w[:, 5120:5248]
                    py, bpy = pb[4 + ch % 2], bp[4 + ch % 2]
                    u3 = uT[:, ch, sl].rearrange("p (x e) -> p x e", e=8)
                    fns = [lambda: nc.tensor.matmul(py[:, 0:T], DG, uT[:, ch, sl].rearrange("p (x e) -> p e x", e=8), start=True, stop=False)]
                    for e in range(8):
                        for s_ in range(e + 1):
                            fns.append(lambda e=e, s_=s_: nc.tensor.matmul(py[:, e * NB:(e + 1) * NB], KI[:, e - s_, :], u3[:, :, s_], start=False, stop=False))
                    S.group("pe", fns, reads=[bw, buT], writes=[bpy])
                    fns = []
                    for e in range(8):
                        for j in range(4):
                            for ri in range(2):
                                last = (ri == 1)
                                fns.append(lambda e=e, j=j, ri=ri, last=last: nc.tensor.matmul(
                                    py[32 * j:32 * j + 32, e * NB:(e + 1) * NB], CA[:, j, e, ri, :], Hpb[:, ri, 4 * ch + j, 0:NB],
                                    start=False, stop=last, tile_position=(0, 32 * j)))
                    S.group("pe", fns, reads=[bw, bHp], writes=[bpy])
                    S.op("act", lambda: nc.scalar.activation(y2[:, 0:T], py[:, 0:T], AF.Square), reads=[bpy], writes=[by2])
                    S.op("dve", lambda: nc.vector.tensor_scalar(y2[:, 0:T], y2[:, 0:T], 0.044715, 1.0, ALU.mult, ALU.add), reads=[by2], writes=[by2])
                    S.op("dve", lambda: nc.vector.tensor_tensor(y2[:, 0:T], y2[:, 0:T], py[:, 0:T], ALU.mult), reads=[by2, bpy], writes=[by2])
                    S.op("act", lambda: nc.scalar.activation(sgm[:, 0:T], y2[:, 0:T], AF.Sigmoid, scale=1.5957691216057308), reads=[by2], writes=[bsg])
                    S.op("dve", lambda: nc.vector.tensor_tensor(hT[:, ch, sl].rearrange("p (x e) -> p e x", e=8),
                                                                sgm[:, 0:T].rearrange("p (e x) -> p e x", e=8),
                                                                py[:, 0:T].rearrange("p (e x) -> p e x", e=8), ALU.mult),
                         reads=[bsg, bpy, buT], writes=[bh[t]])
                if os.environ.get("S5CUT") == "B":
                    return
                pending_glu.append((t, T, sl))
            glu_all(pending_glu)
            if kind == "p" and hf == 1:
                outs = [(0, o_sre_p[s], o_sim_p[s])]
            elif kind == "s":
                outs = [(q, o_sre_s[q], o_sim_s[q]) for q in range(4)]
            else:
                outs = []
            for q, dre, dim in outs:
                for g2 in range(2):
                    rows = slice(g2 * 64, (g2 + 1) * 64)
                    S.dma("sp", dre.rearrange("(j g) p -> g p j", g=2)[g2], hstate[rows, 0, :, q], reads=[bhst], allow_slow_non_contiguous=True)
                    S.dma("sp", dim.rearrange("(j g) p -> g p j", g=2)[g2], hstate[rows, 1, :, q], reads=[bhst], allow_slow_non_contiguous=True)

        bomkv = {(l_, s_): Buf() for l_ in range(2) for s_ in range(2)}

        def cross_attn(l, kind, s, hf, tiles, c):
            norm_to_h(G_CROSS + l, tiles)
            nkv = 2 if kind == "s" else 1
            KTxs = [sb(c, "KTx%d" % i, [128, 8, 256], BF16) for i in range(nkv)]; bKTs = [Buf() for _ in range(nkv)]
            Vxs = [sb(c, "Vx%d" % i, [128, 2, 1024], BF16) for i in range(nkv)]; bVxs = [Buf() for _ in range(nkv)]
            KTx, bKT, Vx, bVx = KTxs[0], bKTs[0], Vxs[0], bVxs[0]
            QTx = sb(c, "QTx", [128, 8, 512], BF16); bQT = Buf()
            PTx = [sb(c, "PTx%d" % i, [128, 2, 512], BF16) for i in range(2)]; bPT = [Buf(), Buf()]
            OT = sb(c, "OT", [128, 8, 512], BF16); bOT = Buf()
            rec = sb(c, "rec", [128, 512], F32); brec = Buf()

            def walloc(cc, name):
                return sb(cc, name, [128, 8, 1024], BF16), Buf()

            def wissue(t, b, name):
                wd = W[name][l].rearrange("(k p) n -> p k n", p=128)
                for q4 in range(4):
                    S.dma("pool", t[:, :, q4 * 256:(q4 + 1) * 256], wd[:, :, q4 * 256:(q4 + 1) * 256], writes=[b], join=(q4 > 0))

            def wload(cc, name):
                t, b = walloc(cc, name)
                wissue(t, b, name)
                return t, b

            def attend(sl, T, kvi=0):
                KTx, bKT, Vx, bVx = KTxs[kvi], bKTs[kvi], Vxs[kvi], bVxs[kvi]
                for m in range(8):
                    pq, bq = pb[m % 2], bp[m % 2]
                    S.group("pe", [(lambda k=k: nc.tensor.matmul(pq[:, 0:T], wq[:, k, m * 128:(m + 1) * 128], hT[:, k, sl],
                                                                 start=(k == 0), stop=(k == 7))) for k in range(8)],
                            reads=[bwq, bh[0], bh[1]], writes=[bq])
                    if m % 2 == 0:
                        S.op("act", lambda: nc.scalar.copy(QTx[:, m, 0:T], pq[:, 0:T]), reads=[bq], writes=[bQT])
                    else:
                        S.op("dve", lambda: nc.vector.tensor_copy(QTx[:, m, 0:T], pq[:, 0:T]), reads=[bq], writes=[bQT])
                sbanks = [(2, 3), (5, 6)]

                def s_part(h):
                    for mt in range(2):
                        bi = sbanks[h % 2][mt]
                        S.group("pe", [(lambda dc=dc: nc.tensor.matmul(pb[bi][:, 0:T], KTx[:, 2 * h + dc, mt * 128:(mt + 1) * 128],
                                                                       QTx[:, 2 * h + dc, 0:T], start=(dc == 0), stop=(dc == 1)))
                                       for dc in range(2)], reads=[bKT, bQT], writes=[bp[bi]])
                        S.op("act", lambda: nc.scalar.activation(PTx[h % 2][:, mt, 0:T], pb[bi][:, 0:T], AF.Exp, scale=1.0 / 16.0),
                             reads=[bp[bi]], writes=[bPT[h % 2]])

                def o_part(h):
                    S.group("pe", [(lambda mt=mt: nc.tensor.matmul(pb[4][:, 0:T], ones_b[:], PTx[h % 2][:, mt, 0:T],
                                                                   start=(mt == 0), stop=(mt == 1))) for mt in range(2)],
                            reads=[bPT[h % 2], b_const], writes=[bp[4]])
                    S.op("act", lambda: nc.scalar.activation(rec[:, 0:T], pb[4][:, 0:T], AF.Ln), reads=[bp[4]], writes=[brec])
                    S.op("act", lambda: nc.scalar.activation(rec[:, 0:T], rec[:, 0:T], AF.Exp, scale=-1.0), reads=[brec], writes=[brec])
                    for dc in range(2):
                        pv, bv = pb[dc], bp[dc]
                        S.group("pe", [(lambda mt=mt: nc.tensor.matmul(pv[:, 0:T], Vx[:, mt, h * 256 + dc * 128:h * 256 + (dc + 1) * 128],
                                                                       PTx[h % 2][:, mt, 0:T], start=(mt == 0), stop=(mt == 1)))
                                       for mt in range(2)], reads=[bVx, bPT[h % 2]], writes=[bv])
                        S.op("dve", lambda: nc.vector.tensor_tensor(OT[:, 2 * h + dc, 0:T], pv[:, 0:T], rec[:, 0:T], ALU.mult),
                             reads=[bv, brec], writes=[bOT])
                s_part(0)
                for h in range(4):
                    if h + 1 < 4:
                        s_part(h + 1)
                    o_part(h)
                for o in range(8):
                    pa, ba = pb[4 + o % 2], bp[4 + o % 2]
                    S.group("pe", [(lambda k=k: nc.tensor.matmul(pa[:, 0:T], wo[:, k, o * 128:(o + 1) * 128], OT[:, k, 0:T],
                                                                 start=(k == 0), stop=(k == 7))) for k in range(8)],
                            reads=[bwo, bOT], writes=[ba])
                    tb = bx[sl.start // 512]
                    S.op("dve", lambda: nc.vector.scalar_tensor_tensor(xT[:, o, sl], pa[:, 0:T], 1.0, xT[:, o, sl], ALU.mult, ALU.add),
                         reads=[ba, tb], writes=[tb])

            def load_kv_from(srcK, srcV, rdeps, kvi=0):
                KTx, bKT, Vx, bVx = KTxs[kvi], bKTs[kvi], Vxs[kvi], bVxs[kvi]
                for mt in range(2):
                    st, bs = stage[mt], b_stage[mt]
                    S.dma("sp", st[:], srcK[mt * 128:(mt + 1) * 128, :], reads=rdeps, writes=[bs])
                    for half in range(2):
                        S.group("pe", [(lambda kk=kk: nc.tensor.transpose(pb[half][:, kk * 128:(kk + 1) * 128],
                                                                          st[:, (half * 4 + kk) * 128:(half * 4 + kk + 1) * 128], ident_f[:]))
                                       for kk in range(4)], reads=[bs, b_const], writes=[bp[half]])
                        S.op("act", lambda: nc.scalar.copy(KTx[:, half * 4:(half + 1) * 4, mt * 128:(mt + 1) * 128],
                                                           pb[half][:, :].rearrange("p (k t) -> p k t", k=4)),
                             reads=[bp[half]], writes=[bKT])
                    S.dma("pool", Vx[:, mt, :], srcV[mt * 128:(mt + 1) * 128, :], reads=rdeps, writes=[bVx])

            if kind == "p" and hf == 1:
                wq, bwq = wload(c, 'w_xq')
                wo, bwo = wload(c, 'w_xo')
                load_kv_from(o_mk[l, s], o_mv[l, s], [bomkv[(l, s)]])
                for t, T in tiles:
                    attend(slice(t * 512, t * 512 + T), T)
            elif kind == "p":
                wq, bwq = walloc(c, 'w_xq')
                wo, bwo = walloc(c, 'w_xo')
                with ExitStack() as c2:
                    wk, bwk = wload(c2, 'w_xk')
                    wv, bwv = wload(c2, 'w_xv')
                    wissue(wq, bwq, 'w_xq')
                    wissue(wo, bwo, 'w_xo')
                    mT = sb(c2, "mT", [128, 8, 256], F32); bmT = Buf()
                    mnT = sb(c2, "mnT", [128, 8, 256], BF16); bmn = Buf()
                    for mt in range(2):
                        st, bs = stage[mt], b_stage[mt]
                        S.dma("sp", st[:], memp[s, mt * 128:(mt + 1) * 128, :], writes=[bs])
                        for half in range(2):
                            S.group("pe", [(lambda kk=kk: nc.tensor.transpose(pb[half][:, kk * 128:(kk + 1) * 128],
                                                                              st[:, (half * 4 + kk) * 128:(half * 4 + kk + 1) * 128], ident_f[:]))
                                           for kk in range(4)], reads=[bs, b_const], writes=[bp[half]])
                            S.op("act", lambda: nc.scalar.copy(mT[:, half * 4:(half + 1) * 4, mt * 128:(mt + 1) * 128],
                                                               pb[half][:, :].rearrange("p (k t) -> p k t", k=4)),
                                 reads=[bp[half]], writes=[bmT])
                    S.op("act", lambda: nc.scalar.activation(sq[:, :, 0:256], mT[:, :, :], AF.Square), reads=[bmT], writes=[b_sq])
                    S.group("pe", [(lambda k=k: nc.tensor.matmul(pb[6][:, 0:256], ones_b[:], sq[:, k, 0:256], start=(k == 0), stop=(k == 7)))
                                   for k in range(8)], reads=[b_sq, b_const], writes=[bp[6]])
                    S.op("act", lambda: nc.scalar.activation(rs[:, 0:256], pb[6][:, 0:256], AF.Ln, bias=EPS, scale=1.0 / D), reads=[bp[6]], writes=[b_rs])
                    S.op("act", lambda: nc.scalar.activation(rs[:, 0:256], rs[:, 0:256], AF.Exp, scale=-0.5), reads=[b_rs], writes=[b_rs])
                    for k in range(8):
                        S.op("dve", lambda k=k: nc.vector.scalar_tensor_tensor(mnT[:, k, :], mT[:, k, :], gains[:, G_MEM + l, k:k + 1], rs[:, 0:256],
                                                                               ALU.mult, ALU.mult), reads=[bmT, b_rs, b_gains], writes=[bmn])
                    cnt = 0
                    for which, wt, bwt, dst in (("k", wk, bwk, o_mk), ("v", wv, bwv, o_mv)):
                        for mt in range(2):
                            st, bs = stage[cnt % 2], b_stage[cnt % 2]
                            cnt += 1
                            for ch in range(2):
                                pk, bk = pb[2 + ch], bp[2 + ch]
                                S.group("pe", [(lambda k=k: nc.tensor.matmul(pk[:, :], mnT[:, k, mt * 128:(mt + 1) * 128], wt[:, k, ch * 512:(ch + 1) * 512],
                                                                             start=(k == 0), stop=(k == 7))) for k in range(8)],
                                        reads=[bmn, bwt], writes=[bk])
                                S.op("act", lambda: nc.scalar.copy(st[:, ch * 512:(ch + 1) * 512], pk[:, :]), reads=[bk], writes=[bs])
                                if which == "v":
                                    S.op("dve", lambda: nc.vector.tensor_copy(Vx[:, mt, ch * 512:(ch + 1) * 512], st[:, ch * 512:(ch + 1) * 512]), reads=[bs], writes=[bVx])
                            S.dma("sp", dst[l, s, mt * 128:(mt + 1) * 128, :], st[:], reads=[bs], writes=[bomkv[(l, s)]])
                    for m in range(8):
                        pk, bk = pb[m % 2], bp[m % 2]
                        S.group("pe", [(lambda k=k: nc.tensor.matmul(pk[:, 0:256], wk[:, k, m * 128:(m + 1) * 128], mnT[:, k, :],
                                                                     start=(k == 0), stop=(k == 7))) for k in range(8)],
                                reads=[bmn, bwk], writes=[bk])
                        S.op("act", lambda: nc.scalar.copy(KTx[:, m, :], pk[:, 0:256]), reads=[bk], writes=[bKT])
                for t, T in tiles:
                    attend(slice(t * 512, t * 512 + T), T)
            else:
                wq, bwq = wload(c, 'w_xq')
                wo, bwo = wload(c, 'w_xo')
                load_kv_from(cmk[l, 0], cmv[l, 0], [], 0)
                for q in range(4):
                    if q + 1 < 4:
                        load_kv_from(cmk[l, q + 1], cmv[l, q + 1], [], (q + 1) % 2)
                    attend(slice(q * 64, q * 64 + 64), 64, q % 2)

        def final_out(dst_rows, tiles):
            for t, T in tiles:
                sl = slice(t * 512, t * 512 + T)
                norm_stats(t, T)
                for k in range(8):
                    S.op("dve", lambda k=k: nc.vector.scalar_tensor_tensor(
                        xT[:, k, sl], xT[:, k, sl], gains[:, G_FINAL, k:k + 1], rs[:, 0:T], ALU.mult, ALU.mult),
                        reads=[bx[t], b_rs, b_gains], writes=[bx[t]])
                for i in range(T // 128):
                    tok0 = t * 512 + i * 128
                    st, bs = stage[i % 2], b_stage[i % 2]
                    for half in range(2):
                        pbank, bbank = pb[half], bp[half]
                        S.group("pe", [
                            (lambda kk=kk: nc.tensor.transpose(pbank[:, kk * 128:(kk + 1) * 128],
                                                               xT[:, half * 4 + kk, tok0:tok0 + 128], ident_f[:]))
                            for kk in range(4)], reads=[bx[t], b_const], writes=[bbank])
                        if half == 0:
                            S.op("act", lambda: nc.scalar.copy(st[:, 0:512], pbank[:, :]), reads=[bbank], writes=[bs])
                        else:
                            S.op("dve", lambda: nc.vector.tensor_copy(st[:, 512:1024], pbank[:, :]), reads=[bbank],
                                 writes=[bs])
                    S.dma("sp", dst_rows[tok0:tok0 + 128, :], st[:], reads=[bs])

        if parts != "no_s5":
            with ExitStack() as c:
                s5_setup(c)
                S.barrier()

        sts = []
        for s in range(2):
            for hf in range(2):
                sts.append(("p", s, hf))
        sts.append(("s", 0, 0))
        if parts == "ffn_small":
            sts = [("p", 0, 0), ("s", 0, 0)]
        if parts == "only_s":
            sts = [("s", 0, 0)]
        if parts == "setup_only":
            sts = []
        if parts == "only_p":
            sts = [("p", 0, 0)]
        plan = []
        cur_scope = []

        def close_scope():
            nonlocal cur_scope
            if cur_scope:
                plan.append(cur_scope)
            cur_scope = []
        for kind, s, hf in sts:
            if kind == "p":
                src_ = xp[s, hf * 1024:(hf + 1) * 1024, :]
                dst_ = y_p[s, hf * 1024:(hf + 1) * 1024, :]
                ntok = 1024
                tiles = [(0, 512), (1, 512)]
            else:
                src_ = xs.rearrange("s t d -> (s t) d")
                dst_ = y_s.rearrange("s t d -> (s t) d")
                ntok = 256
                tiles = [(0, 256)]
            tag = "%s%d%d" % (kind, s, hf)
            A = dict(kind=kind, s=s, hf=hf, tiles=tiles)
            cur_scope.append(("load_x " + tag, lambda c, src_=src_, ntok=ntok: load_x(src_, ntok)))
            cur_scope.append(("ffn1 L0", lambda c, A=A: ffn(0, 1, A["tiles"], c)))
            close_scope()
            cur_scope.append(("even", lambda c, A=A: even_mixer(A["kind"], A["s"], A["hf"], A["tiles"], c)))
            close_scope()
            cur_scope.append(("cross L0", lambda c, A=A: cross_attn(0, A["kind"], A["s"], A["hf"], A["tiles"], c)))
            close_scope()
            cur_scope.append(("ffn2 L0", lambda c, A=A: ffn(0, 2, A["tiles"], c)))
            cur_scope.append(("ffn1 L1", lambda c, A=A: ffn(1, 1, A["tiles"], c)))
            close_scope()
            if parts != "no_s5":
                cur_scope.append(("odd", lambda c, A=A: odd_mixer(A["kind"], A["s"], A["hf"], A["tiles"], c)))
                close_scope()
            cur_scope.append(("cross L1", lambda c, A=A: cross_attn(1, A["kind"], A["s"], A["hf"], A["tiles"], c)))
            close_scope()
            cur_scope.append(("ffn2 L1", lambda c, A=A: ffn(1, 2, A["tiles"], c)))
            cur_scope.append(("final", lambda c, A=A, dst_=dst_: final_out(dst_, A["tiles"])))
        close_scope()
        for scope in plan:
            with ExitStack() as c:
                for name, fn in scope:
                    S.mark(name)
                    fn(c)
                S.barrier()
        S.finish()
    print("n_inst", S.n_inst)
    S.mark("end")
    build.marks = S.marks
    return nc


_NC_CACHE = {}


def _rope_table():
    inv = (np.float32(10000.0) ** (-np.arange(0, 32, 2, dtype=np.float32) / np.float32(32))).astype(np.float32)
    pos = np.concatenate([np.arange(2048), 1024 + np.arange(64)]).astype(np.float32)
    ang = (pos[:, None] * inv[None, :]).astype(np.float32)
    return np.concatenate([np.cos(ang), np.sin(ang)], axis=1).astype(np.float32)


def _rope_table_T():
    r = _rope_table()
    cosT = np.concatenate([r[:, 0:16].T, r[:, 0:16].T], axis=0)
    sinT = np.concatenate([r[:, 16:32].T, r[:, 16:32].T], axis=0)
    return np.ascontiguousarray(np.concatenate([cosT, sinT], axis=0)).astype(np.float32)


def kernel(**inputs):
    nc = _NC_CACHE.get("nc")
    if nc is None:
        nc = build()
        _NC_CACHE["nc"] = nc
    in_maps = []
    for c in range(NCORES):
        m = {"xp": np.ascontiguousarray(inputs["x_prompt"][2 * c:2 * c + 2]),
             "xs": np.ascontiguousarray(inputs["x_sample"][4 * c:4 * c + 4]),
             "memp": np.ascontiguousarray(inputs["mem_prompt"][2 * c:2 * c + 2]),
             "cmk": np.ascontiguousarray(inputs["cache_mem_k"][:, 4 * c:4 * c + 4]).reshape(2, 4, 256, 1024),
             "cmv": np.ascontiguousarray(inputs["cache_mem_v"][:, 4 * c:4 * c + 4]).reshape(2, 4, 256, 1024),
             "rope": _rope_table(), "ropeT": _rope_table_T(),
             "latc": np.ascontiguousarray(inputs["cache_mla_latent"][0, 4 * c:4 * c + 4]),
             "krc": np.ascontiguousarray(inputs["cache_mla_krope"][0, 4 * c:4 * c + 4]),
             "convc": np.ascontiguousarray(inputs["state_conv"][0, 4 * c:4 * c + 4]),
             "ssre": np.ascontiguousarray(inputs["state_ssm_re"][0, 4 * c:4 * c + 4]),
             "ssim": np.ascontiguousarray(inputs["state_ssm_im"][0, 4 * c:4 * c + 4])}
        for n in WEIGHT_NAMES:
            m[n] = np.ascontiguousarray(inputs[n])
        in_maps.append(m)
    res = run_bass_kernel_spmd(nc, in_maps, core_ids=list(range(NCORES)))
    R = res.results
    cat = lambda k, ax=0: np.concatenate([r[k] for r in R], axis=ax)
    y_prompt = cat("y_p")
    y_sample = cat("y_s")
    lat_p = cat("o_lat_p")[None]
    kr_p = cat("o_kr_p")[None]
    conv_p = cat("o_conv_p")[None]
    mk_p = cat("o_mk", 1).reshape(2, 16, 256, 4, 256)
    mv_p = cat("o_mv", 1).reshape(2, 16, 256, 4, 256)
    lat_s = cat("o_lat_s")[None]
    kr_s = cat("o_kr_s")[None]
    conv_s = cat("o_conv_s")[None]
    sre_p = cat("o_sre_p")[None]
    sim_p = cat("o_sim_p")[None]
    sre_s = cat("o_sre_s")[None]
    sim_s = cat("o_sim_s")[None]
    return (y_prompt, y_sample, lat_p, kr_p, conv_p, sre_p, sim_p, mk_p, mv_p, lat_s, kr_s, conv_s, sre_s, sim_s)
```
